# Optimizing a Trainium2 kernel written in Bass

```python
import math
import jax, jax.numpy as jnp
from jax import lax
import numpy as np


D_MODEL = 1024
BATCH = 4
SEQ = 8192
DEPTH = 4

N_MIXERS = 3
GRID_W = 64
EPS = 1e-6
N_HEADS = 16
HEAD_DIM = D_MODEL // N_HEADS
N_KV_HEADS = 4
Q_BLOCK = 128
ROPE_THETA = 10000.0
HY_ORDER = 2
HY_EMB_DIM = 33
HY_FILTER_WIDTH = 64
HY_FAST_DECAY = 0.3
HY_SLOW_DECAY = 1.5
HY_TARGET = 1e-2
S5_GROUP = 16
S5_GROUPS = D_MODEL // S5_GROUP
S5_STATE = 64
S5_DT_MIN = 1e-3
S5_DT_MAX = 1e-1
D_FF = 2816

kernel_name = 'hybrid_attn_hyena_s5_encoder'


def rmsnorm(x, g):
    xf = x.astype(jnp.float32)
    y = xf * lax.rsqrt(jnp.mean(xf * xf, axis=-1, keepdims=True) + EPS)
    return (y * g.astype(jnp.float32)).astype(x.dtype)


def dwconv3(x, w, b):
    L = x.shape[1]
    xp = jnp.pad(x, ((0, 0), (1, 1), (0, 0)))
    return xp[:, :L] * w[0] + xp[:, 1:L + 1] * w[1] + xp[:, 2:] * w[2] + b


def axial_rope_tables(L):
    rows = L // GRID_W
    n_freq = HEAD_DIM // 4
    inv = 1.0 / (ROPE_THETA ** (jnp.arange(n_freq, dtype=jnp.float32) / n_freq))
    r = jnp.arange(rows, dtype=jnp.float32)
    col = jnp.arange(GRID_W, dtype=jnp.float32)
    ang_r = jnp.broadcast_to(r[:, None, None] * inv, (rows, GRID_W, n_freq))
    ang_c = jnp.broadcast_to(col[None, :, None] * inv, (rows, GRID_W, n_freq))
    ang = jnp.concatenate([ang_r, ang_c], axis=-1).reshape(L, 2 * n_freq)
    return jnp.cos(ang), jnp.sin(ang)


def apply_rope(x, cos, sin):
    xf = x.astype(jnp.float32).reshape(x.shape[:-1] + (HEAD_DIM // 2, 2))
    x0, x1 = xf[..., 0], xf[..., 1]
    c = cos[None, :, None, :]
    s = sin[None, :, None, :]
    out = jnp.stack([x0 * c - x1 * s, x0 * s + x1 * c], axis=-1).reshape(x.shape)
    return out.astype(x.dtype)


def attention_mixer(h, w_qkv, w_o, q_gain, k_gain, cos, sin):
    Bsz, L, _ = h.shape
    G = N_HEADS // N_KV_HEADS
    qkv = h @ w_qkv
    q, k, v = jnp.split(qkv, [N_HEADS * HEAD_DIM, (N_HEADS + N_KV_HEADS) * HEAD_DIM], axis=-1)
    q = q.reshape(Bsz, L, N_HEADS, HEAD_DIM)
    k = k.reshape(Bsz, L, N_KV_HEADS, HEAD_DIM)
    v = v.reshape(Bsz, L, N_KV_HEADS, HEAD_DIM)
    q = apply_rope(rmsnorm(q, q_gain), cos, sin)
    k = apply_rope(rmsnorm(k, k_gain), cos, sin)
    nb = L // Q_BLOCK
    qb = q.reshape(Bsz, nb, Q_BLOCK, N_KV_HEADS, G, HEAD_DIM).transpose(1, 0, 2, 3, 4, 5)
    scale = HEAD_DIM ** -0.5

    def block(q_blk):
        s = jnp.einsum('bqkgd,bskd->bkgqs', q_blk, k, preferred_element_type=jnp.float32) * scale
        p = jax.nn.softmax(s, axis=-1).astype(v.dtype)
        return jnp.einsum('bkgqs,bskd->bqkgd', p, v)

    o = lax.map(block, qb)
    o = o.transpose(1, 0, 2, 3, 4, 5).reshape(Bsz, L, N_HEADS * HEAD_DIM)
    return o @ w_o


def hyena_pos_features(L):
    t = jnp.linspace(0.0, 1.0, L, dtype=jnp.float32)[:, None]
    bands = (HY_EMB_DIM - 1) // 2
    w = 2.0 * math.pi * jnp.arange(L, dtype=jnp.float32) / L
    f = jnp.linspace(1e-4, bands - 1, bands, dtype=jnp.float32)
    ang = w[:, None] * f[None, :]
    z = jnp.concatenate([t, jnp.cos(ang), -jnp.sin(ang)], axis=-1)
    deltas = jnp.abs(jnp.linspace(math.log(HY_TARGET) / HY_SLOW_DECAY,
                                  math.log(HY_TARGET) / HY_FAST_DECAY, D_MODEL, dtype=jnp.float32))
    decay = jnp.exp(-t * deltas[None, :])
    return z, decay


def hyena_mixer(h, w_in, conv_w, conv_b, f_w1, f_b1, f_w2, f_b2, f_w3, f_freq, skip, w_out, z, decay):
    Bsz, L, D = h.shape
    u = dwconv3(h @ w_in, conv_w, conv_b)
    v, x1, x2 = jnp.split(u, 3, axis=-1)
    a = jnp.sin(f_freq * (z @ f_w1 + f_b1))
    a = jnp.sin(f_freq * (a @ f_w2 + f_b2))
    filt = (a @ f_w3).astype(jnp.float32).reshape(L, HY_ORDER, 2, D) * decay[:, None, None, :]
    fwd = filt[:, :, 0]
    bwd = filt[:, :, 1]
    k2 = jnp.concatenate([fwd, jnp.zeros_like(fwd[:1]), bwd[1:][::-1]], axis=0)
    k2 = k2 / jnp.sum(jnp.abs(k2), axis=0, keepdims=True)
    kf = jnp.fft.rfft(k2, axis=0)
    gates = (x1, x2)
    zc = v
    for o in range(HY_ORDER):
        zf = jnp.fft.rfft(zc.astype(jnp.float32), n=2 * L, axis=1)
        y = jnp.fft.irfft(zf * kf[None, :, o], n=2 * L, axis=1)[:, :L]
        zc = gates[o] * (y.astype(h.dtype) + skip[o] * zc)
    return zc @ w_out


def s5_direction(u, A_re, A_im, log_dt, B_re, B_im, C_re, C_im, reverse):
    L = u.shape[1]
    lam = lax.complex(jnp.minimum(A_re.astype(jnp.float32), -1e-4), A_im.astype(jnp.float32))
    dt = jnp.exp(log_dt.astype(jnp.float32))[:, None]
    lam_bar = jnp.exp(lam * dt)
    b_c = lax.complex(B_re.astype(jnp.float32), B_im.astype(jnp.float32))
    b_bar = ((lam_bar - 1.0) / lam)[..., None] * b_c
    bu = jnp.einsum('blgh,gph->blgp', u.astype(jnp.complex64), b_bar)
    a = jnp.broadcast_to(lam_bar[None, None], (1, L) + lam_bar.shape)

    def combine(e1, e2):
        a1, b1 = e1
        a2, b2 = e2
        return a2 * a1, a2 * b1 + b2

    _, xs = lax.associative_scan(combine, (a, bu), axis=1, reverse=reverse)
    c_c = lax.complex(C_re.astype(jnp.float32), C_im.astype(jnp.float32))
    return jnp.einsum('blgp,ghp->blgh', xs, c_c).real


def s5_mixer(h, A_re, A_im, log_dt, B_re, B_im, C_re, C_im, d_skip, w_glu):
    Bsz, L, D = h.shape
    hf = h.astype(jnp.float32)
    u = hf.reshape(Bsz, L, S5_GROUPS, S5_GROUP)
    y_f = s5_direction(u, A_re[0], A_im[0], log_dt[0], B_re[0], B_im[0], C_re[0], C_im[0], False)
    y_b = s5_direction(u, A_re[1], A_im[1], log_dt[1], B_re[1], B_im[1], C_re[1], C_im[1], True)
    y = (y_f + y_b).reshape(Bsz, L, D) + d_skip.astype(jnp.float32) * hf
    y = jax.nn.gelu(y.astype(h.dtype))
    g_a, g_b = jnp.split(y @ w_glu, 2, axis=-1)
    return g_a * jax.nn.sigmoid(g_b)


def conv_ffn(h, w_up, conv_w, conv_b, w_down):
    gate, val = jnp.split(h @ w_up, 2, axis=-1)
    gate = dwconv3(gate, conv_w, conv_b)
    return (jax.nn.silu(gate) * val) @ w_down


def setup_inputs(seed: int = 0) -> dict:
    key = jax.random.key(seed)
    ks = iter(jax.random.split(key, 48))
    f32 = jnp.float32

    def nrm(shape, scale):
        return jax.random.normal(next(ks), shape, f32) * scale

    n_attn = len(range(0, DEPTH, N_MIXERS))
    n_hy = len(range(1, DEPTH, N_MIXERS))
    n_s5 = len(range(2, DEPTH, N_MIXERS))
    D = D_MODEL
    G, P, Hg = S5_GROUPS, S5_STATE, S5_GROUP
    qkv_w = (N_HEADS + 2 * N_KV_HEADS) * HEAD_DIM
    return {
        'x': nrm((BATCH, SEQ, D), 1.0),
        'c': nrm((BATCH, D), 1.0),
        'ada_w': nrm((DEPTH, D, 6 * D), 0.5 * D ** -0.5),
        'ada_b': nrm((DEPTH, 6 * D), 0.02),
        'norm1_g': 1.0 + nrm((DEPTH, D), 0.02),
        'norm2_g': 1.0 + nrm((DEPTH, D), 0.02),
        'final_g': 1.0 + nrm((D,), 0.02),
        'attn_w_qkv': nrm((n_attn, D, qkv_w), D ** -0.5),
        'attn_w_o': nrm((n_attn, N_HEADS * HEAD_DIM, D), D ** -0.5),
        'attn_q_gain': 1.0 + nrm((n_attn, HEAD_DIM), 0.02),
        'attn_k_gain': 1.0 + nrm((n_attn, HEAD_DIM), 0.02),
        'hy_w_in': nrm((n_hy, D, 3 * D), D ** -0.5),
        'hy_conv_w': nrm((n_hy, 3, 3 * D), 3 ** -0.5),
        'hy_conv_b': nrm((n_hy, 3 * D), 0.02),
        'hy_f_w1': nrm((n_hy, HY_EMB_DIM, HY_FILTER_WIDTH), HY_EMB_DIM ** -0.5),
        'hy_f_b1': nrm((n_hy, HY_FILTER_WIDTH), 0.1),
        'hy_f_w2': nrm((n_hy, HY_FILTER_WIDTH, HY_FILTER_WIDTH), HY_FILTER_WIDTH ** -0.5),
        'hy_f_b2': nrm((n_hy, HY_FILTER_WIDTH), 0.1),
        'hy_f_w3': nrm((n_hy, HY_FILTER_WIDTH, HY_ORDER * 2 * D), HY_FILTER_WIDTH ** -0.5),
        'hy_f_freq': 1.0 + nrm((n_hy, HY_FILTER_WIDTH), 0.02),
        'hy_skip': nrm((n_hy, HY_ORDER, D), 1.0),
        'hy_w_out': nrm((n_hy, D, D), D ** -0.5),
        's5_A_re': -0.5 + nrm((n_s5, 2, G, P), 0.01),
        's5_A_im': math.pi * jnp.arange(P, dtype=f32) + nrm((n_s5, 2, G, P), 0.01),
        's5_log_dt': jax.random.uniform(next(ks), (n_s5, 2, G), f32, math.log(S5_DT_MIN), math.log(S5_DT_MAX)),
        's5_B_re': nrm((n_s5, 2, G, P, Hg), (2 * Hg) ** -0.5),
        's5_B_im': nrm((n_s5, 2, G, P, Hg), (2 * Hg) ** -0.5),
        's5_C_re': nrm((n_s5, 2, G, Hg, P), (2 * P) ** -0.5),
        's5_C_im': nrm((n_s5, 2, G, Hg, P), (2 * P) ** -0.5),
        's5_D': nrm((n_s5, D), 1.0),
        's5_w_glu': nrm((n_s5, D, 2 * D), D ** -0.5),
        'ffn_w_up': nrm((DEPTH, D, 2 * D_FF), D ** -0.5),
        'ffn_conv_w': nrm((DEPTH, 3, D_FF), 3 ** -0.5),
        'ffn_conv_b': nrm((DEPTH, D_FF), 0.02),
        'ffn_w_down': nrm((DEPTH, D_FF, D), D_FF ** -0.5),
    }


def reference(x, c, ada_w, ada_b, norm1_g, norm2_g, final_g,
              attn_w_qkv, attn_w_o, attn_q_gain, attn_k_gain,
              hy_w_in, hy_conv_w, hy_conv_b, hy_f_w1, hy_f_b1, hy_f_w2, hy_f_b2, hy_f_w3, hy_f_freq,
              hy_skip, hy_w_out,
              s5_A_re, s5_A_im, s5_log_dt, s5_B_re, s5_B_im, s5_C_re, s5_C_im, s5_D, s5_w_glu,
              ffn_w_up, ffn_conv_w, ffn_conv_b, ffn_w_down):
    L = x.shape[1]
    cos, sin = axial_rope_tables(L)
    z, decay = hyena_pos_features(L)
    c_act = jax.nn.silu(c)
    for i in range(DEPTH):
        m, j = i % N_MIXERS, i // N_MIXERS
        mod = (c_act @ ada_w[i] + ada_b[i])[:, None, :]
        sh1, sc1, g1, sh2, sc2, g2 = jnp.split(mod, 6, axis=-1)
        h = rmsnorm(x, norm1_g[i]) * (1.0 + sc1) + sh1
        if m == 0:
            y = attention_mixer(h, attn_w_qkv[j], attn_w_o[j], attn_q_gain[j], attn_k_gain[j], cos, sin)
        elif m == 1:
            y = hyena_mixer(h, hy_w_in[j], hy_conv_w[j], hy_conv_b[j], hy_f_w1[j], hy_f_b1[j],
                            hy_f_w2[j], hy_f_b2[j], hy_f_w3[j], hy_f_freq[j], hy_skip[j], hy_w_out[j],
                            z, decay)
        else:
            y = s5_mixer(h, s5_A_re[j], s5_A_im[j], s5_log_dt[j], s5_B_re[j], s5_B_im[j],
                         s5_C_re[j], s5_C_im[j], s5_D[j], s5_w_glu[j])
        x = x + g1 * y
        h = rmsnorm(x, norm2_g[i]) * (1.0 + sc2) + sh2
        x = x + g2 * conv_ffn(h, ffn_w_up[i], ffn_conv_w[i], ffn_conv_b[i], ffn_w_down[i])
    return rmsnorm(x, final_g)
```

```python
import numpy as np
import ml_dtypes
import concourse.bass as bass
import concourse.mybir as mybir
from concourse.bass_utils import run_bass_kernel_spmd

F32 = mybir.dt.float32
BF16 = mybir.dt.bfloat16
I32 = mybir.dt.int32
AF = mybir.ActivationFunctionType
ALU = mybir.AluOpType
AX = mybir.AxisListType

D = 1024
DFF = 2816
EPS = 1e-6


class Res:
    __slots__ = ("w", "r")

    def __init__(self):
        self.w = None
        self.r = []


class KB:
    ENG = ("pe", "act", "dve", "pool", "sp")

    def __init__(self, n_dma_sems=24):
        nc = bass.Bass("TRN2", target_bir_lowering=False)
        self.nc = nc
        self.e = {"pe": nc.tensor, "act": nc.scalar, "dve": nc.vector, "pool": nc.gpsimd, "sp": nc.sync}
        self.sem = {}
        self.cnt = {}
        for k in self.ENG:
            self.sem[k] = nc.semaphore("s_" + k).__enter__()
            self.cnt[k] = 0
        self.dsem = []
        for i in range(n_dma_sems):
            key = "d%d" % i
            self.sem[key] = nc.semaphore("s_" + key).__enter__()
            self.cnt[key] = 0
            self.dsem.append(key)
        self.dnext = 0
        self.waited = {k: {} for k in self.ENG}
        self.out_events = []
        self.n_inst = 0
        self._names = 0

    def sb(self, shape, dt, name=None):
        self._names += 1
        return self.nc.sbuf_tensor(name or ("t%d" % self._names), list(shape), dt).__enter__()

    def ps(self, name=None, shape=(128, 512), dt=F32):
        self._names += 1
        return self.nc.psum_tensor(name or ("p%d" % self._names), list(shape), dt).__enter__()

    def dram(self, name, shape, dt, kind="Internal"):
        return self.nc.dram_tensor(name, list(shape), dt, kind=kind).ap()

    def inp(self, name, shape, dt=F32):
        return self.nc.dram_tensor(name, list(shape), dt, kind="ExternalInput").ap()

    def outp(self, name, shape, dt=F32):
        return self.nc.dram_tensor(name, list(shape), dt, kind="ExternalOutput").ap()

    def _wait(self, eng, ev):
        if ev is None:
            return
        key, val = ev
        if key == eng and eng == "pe":
            return
        if self.waited[eng].get(key, 0) >= val:
            return
        self.e[eng].wait_ge(self.sem[key], val)
        self.waited[eng][key] = val

    def _deps(self, eng, reads, writes):
        for r in reads:
            self._wait(eng, r.w)
        for w in writes:
            self._wait(eng, w.w)
            for ev in w.r:
                if ev[0] == eng:
                    continue
                self._wait(eng, ev)

    def _commit(self, ev, reads, writes):
        for r in reads:
            r.r.append(ev)
            if len(r.r) > 64:
                best = {}
                for k, v in r.r:
                    if best.get(k, 0) < v:
                        best[k] = v
                r.r = list(best.items())
        for w in writes:
            w.w = ev
            w.r = []

    def op(self, eng, fn, reads=(), writes=()):
        self._deps(eng, reads, writes)
        inst = fn(self.e[eng])
        self.cnt[eng] += 1
        inst.then_inc(self.sem[eng], 1)
        ev = (eng, self.cnt[eng])
        self._commit(ev, reads, writes)
        self.n_inst += 1
        return ev

    def dma(self, out, in_, reads=(), writes=(), q="sp", is_output=False, **kw):
        key = self.dsem[self.dnext]
        self.dnext = (self.dnext + 1) % len(self.dsem)
        if self.cnt[key] > 0:
            self._wait(q, (key, self.cnt[key]))
        self._deps(q, reads, writes)
        inst = self.e[q].dma_start(out=out, in_=in_, **kw)
        self.cnt[key] += 16
        inst.then_inc(self.sem[key], 16)
        ev = (key, self.cnt[key])
        self._commit(ev, reads, writes)
        if is_output:
            self.out_events.append(ev)
        self.n_inst += 1
        return ev

    def finish(self):
        for ev in self.out_events:
            self._wait("sp", ev)
        return self.nc


def bf(x):
    return np.asarray(x, dtype=np.float32).astype(ml_dtypes.bfloat16).astype(np.float32)


def load_cast_weight(kb, w_ap, kchunks, ncols, name, res, colblk=1024):
    wt = kb.sb([128, kchunks, ncols], BF16, name)
    src = w_ap.rearrange("(k p) n -> p k n", p=128)
    for k in range(kchunks):
        for c0 in range(0, ncols, colblk):
            c1 = min(ncols, c0 + colblk)
            kb.dma(wt[:, k, c0:c1], src[:, k, c0:c1], writes=[res], q="pool")
    return wt


def emit_mod(kb, c_ap, adaw_ap, adab_ap, nch, name, ps=None, r_ps_in=None):
    r_c, r_w, r_ps, r_mod, r_b = Res(), Res(), Res(), Res(), Res()
    ct = kb.sb([128, 8, 2], F32, name + "_c")
    sg = kb.sb([128, 8, 2], F32, name + "_sg")
    ca = kb.sb([128, 8, 2], F32, name + "_ca")
    bt = kb.sb([128, nch], F32, name + "_b")
    mod = kb.sb([128, nch], F32, name)
    kb.dma(ct[:], c_ap, writes=[r_c])
    kb.dma(bt[:], adab_ap, writes=[r_b])
    kb.op("act", lambda e: e.activation(out=sg[:], in_=ct[:], func=AF.Sigmoid), reads=[r_c], writes=[r_mod])
    kb.op("dve", lambda e: e.tensor_tensor(out=ca[:], in0=ct[:], in1=sg[:], op=ALU.mult), reads=[r_c, r_mod], writes=[r_ps])
    r_ca = r_ps
    src = adaw_ap.rearrange("(k p) n -> p k n", p=128)
    wts = [kb.sb([128, 8, 128], F32, name + "_w%d" % i) for i in range(2)]
    r_wts = [Res(), Res()]
    ps1 = ps if ps is not None else kb.ps(name + "_ps")
    r_ps1 = r_ps_in if r_ps_in is not None else Res()
    for j in range(nch):
        s = j % 2
        kb.dma(wts[s][:], src[:, :, j * 128:(j + 1) * 128], writes=[r_wts[s]])
        for k in range(8):
            kb.op("pe", lambda e, k=k, s=s, j=j: e.matmul(ps1[:, 2 * j:2 * j + 2], lhsT=wts[s][:, k, :], rhs=ca[:, k, :],
                                                      start=(k == 0), stop=(k == 7)),
                  reads=[r_wts[s], r_ca], writes=[r_ps1])
    kb.op("dve", lambda e: e.tensor_tensor(out=mod[:], in0=ps1[:, 0:2 * nch:2], in1=bt[:], op=ALU.add),
          reads=[r_ps1, r_b], writes=[r_mod])
    return mod, r_mod


def small_load(kb, ap, shape, name, dt=F32):
    t = kb.sb(shape, dt, name)
    r = Res()
    kb.dma(t[:], ap, writes=[r])
    return t, r


class Norm:
    def __init__(self, kb, W, name):
        self.kb = kb
        self.ones = kb.sb([128, 128], BF16, name + "_ones")
        self.r_ones = Res()
        kb.op("dve", lambda e: e.memset(self.ones[:], 1.0), writes=[self.r_ones])
        self.sq = kb.sb([128, 8, W], BF16, name + "_sq")
        self.r_sq = Res()
        self.ps = kb.ps(name + "_ps")
        self.r_ps = Res()
        self.sd = kb.sb([128, W], F32, name + "_sd")
        self.rstd = kb.sb([128, W], F32, name + "_rstd")
        self.r_sd = Res()
        self.r_rstd = Res()

    def emit(self, x3, r_x, w):
        kb = self
        kb = self.kb
        kb.op("act", lambda e: e.activation(out=self.sq[:, :, :w], in_=x3, func=AF.Square), reads=[r_x], writes=[self.r_sq])
        for k in range(8):
            kb.op("pe", lambda e, k=k: e.matmul(self.ps[:, :w], lhsT=self.ones[:], rhs=self.sq[:, k, :w], start=(k == 0), stop=(k == 7)),
                  reads=[self.r_sq, self.r_ones], writes=[self.r_ps])
        kb.op("act", lambda e: e.activation(out=self.sd[:, :w], in_=self.ps[:, :w], func=AF.Sqrt, scale=1.0 / D, bias=self.epsb[:]),
              reads=[self.r_ps, self.r_eps], writes=[self.r_sd])
        kb.op("dve", lambda e: e.reciprocal(out=self.rstd[:, :w], in_=self.sd[:, :w]), reads=[self.r_sd], writes=[self.r_rstd])
        return self.rstd[:, :w], self.r_rstd

    def init_eps(self):
        kb = self.kb
        self.epsb = kb.sb([128, 1], F32, "epsb%d" % id(self))
        self.r_eps = Res()
        kb.op("dve", lambda e: e.memset(self.epsb[:], EPS), writes=[self.r_eps])
        return self


def emit_gs(kb, mod, r_mod, sc_off, g_t, r_g, name):
    gs = kb.sb([128, 8], F32, name)
    r = Res()
    kb.op("dve", lambda e: e.scalar_tensor_tensor(out=gs[:], in0=mod[:, sc_off:sc_off + 8], scalar=1.0, in1=g_t[:],
                                                  op0=ALU.add, op1=ALU.mult), reads=[r_mod, r_g], writes=[r])
    return gs, r


def build_ffn(NT, final=False, TW=256):
    kb = KB()
    xT = kb.inp("xT", [128, 8, NT + 2])
    c_in = kb.inp("c2", [128, 8, 2])
    adaw = kb.inp("adaw", [1024, 3 * D])
    adab = kb.inp("adab", [128, 24])
    ng = kb.inp("ng", [128, 8])
    wup = kb.inp("wup", [D, 2 * DFF])
    wdn = kb.inp("wdn", [DFF, D])
    cw = kb.inp("cw", [128, 22 * 3])
    cb = kb.inp("cb", [128, 22])
    msk = kb.inp("msk", [128, 2])
    if final:
        fg = kb.inp("fg", [128, 8])
    out = kb.outp("out", [128, 8, NT])

    W = TW + 2
    r_wup, r_wdn = Res(), Res()
    wup_t = load_cast_weight(kb, wup, 8, 2 * DFF, "wup_t", r_wup, colblk=1408)
    wdn_t = load_cast_weight(kb, wdn, 22, D, "wdn_t", r_wdn)
    mod, r_mod = emit_mod(kb, c_in, adaw, adab, 24, "mod")
    ng_t, r_ng = small_load(kb, ng, [128, 8], "ng_t")
    cw_t, r_cw = small_load(kb, cw, [128, 66], "cw_t")
    cb_t, r_cb = small_load(kb, cb, [128, 22], "cb_t")
    mk_t, r_mk = small_load(kb, msk, [128, 2], "mk_t")
    if final:
        fg_t, r_fg = small_load(kb, fg, [128, 8], "fg_t")
    gs, r_gs = emit_gs(kb, mod, r_mod, 8, ng_t, r_ng, "gs")
    nrm = Norm(kb, W, "nrm").init_eps()

    xt = [kb.sb([128, 8, W], F32, "xt%d" % i) for i in range(2)]
    r_xt = [Res(), Res()]
    tmp = kb.sb([128, W], F32, "tmp")
    r_tmp = Res()
    h = kb.sb([128, 8, W], BF16, "h")
    r_h = Res()
    a = kb.sb([128, 22, W], BF16, "a")
    r_a = Res()
    cbuf = [kb.sb([128, W], F32, "cbuf%d" % i) for i in range(2)]
    r_cbuf = [Res(), Res()]
    sbuf_ = [kb.sb([128, W], F32, "sbuf%d" % i) for i in range(2)]
    r_sbuf = [Res(), Res()]
    xo = kb.sb([128, 8, W], F32, "xo")
    r_xo = Res()
    pg = [kb.ps("pg%d" % i) for i in range(2)]
    pv = [kb.ps("pv%d" % i) for i in range(2)]
    r_pg = [Res(), Res()]
    r_pv = [Res(), Res()]
    po = [kb.ps("po%d" % i) for i in range(2)]
    r_po = [Res(), Res()]

    ntiles = (NT + TW - 1) // TW
    for ti in range(ntiles):
        o0 = ti * TW
        o1 = min(NT, o0 + TW)
        w = o1 - o0 + 2
        s = ti % 2
        x3 = xt[s][:, :, :w]
        kb.dma(x3, xT[:, :, o0:o0 + w], writes=[r_xt[s]])
        rstd, r_rstd = nrm.emit(x3, r_xt[s], w)
        for k in range(8):
            kb.op("dve", lambda e, k=k: e.scalar_tensor_tensor(out=tmp[:, :w], in0=xt[s][:, k, :w], scalar=gs[:, k:k + 1],
                                                                in1=rstd, op0=ALU.mult, op1=ALU.mult),
                  reads=[r_xt[s], r_gs, r_rstd], writes=[r_tmp])
            kb.op("act", lambda e, k=k: e.activation(out=h[:, k, :w], in_=tmp[:, :w], func=AF.Identity,
                                                     bias=mod[:, k:k + 1], scale=1.0),
                  reads=[r_tmp, r_mod], writes=[r_h])
        if ti == 0:
            kb.op("dve", lambda e: e.tensor_scalar(out=h[:, :, 0:1], in0=h[:, :, 0:1], scalar1=mk_t[:, 0:1], scalar2=None, op0=ALU.mult),
                  reads=[r_mk, r_h], writes=[r_h])
        if ti == ntiles - 1:
            kb.op("dve", lambda e: e.tensor_scalar(out=h[:, :, w - 1:w], in0=h[:, :, w - 1:w], scalar1=mk_t[:, 1:2], scalar2=None, op0=ALU.mult),
                  reads=[r_mk, r_h], writes=[r_h])
        for j in range(22):
            q = j % 2
            for k in range(8):
                kb.op("pe", lambda e, k=k, j=j, q=q: e.matmul(pg[q][:, :w], lhsT=wup_t[:, k, j * 128:(j + 1) * 128], rhs=h[:, k, :w],
                                                               start=(k == 0), stop=(k == 7)),
                      reads=[r_wup, r_h], writes=[r_pg[q]])
            for k in range(8):
                kb.op("pe", lambda e, k=k, j=j, q=q: e.matmul(pv[q][:, :w], lhsT=wup_t[:, k, DFF + j * 128:DFF + (j + 1) * 128], rhs=h[:, k, :w],
                                                               start=(k == 0), stop=(k == 7)),
                      reads=[r_wup, r_h], writes=[r_pv[q]])
            cbq = cbuf[q]
            kb.op("act", lambda e, j=j, q=q, cbq=cbq: e.activation(out=cbq[:, 1:w - 1], in_=pg[q][:, 1:w - 1], func=AF.Identity,
                                                                 scale=cw_t[:, 3 * j + 1:3 * j + 2], bias=cb_t[:, j:j + 1]),
                  reads=[r_pg[q], r_cw, r_cb], writes=[r_cbuf[q]])
            kb.op("dve", lambda e, j=j, q=q, cbq=cbq: e.scalar_tensor_tensor(out=cbq[:, 1:w - 1], in0=pg[q][:, 0:w - 2], scalar=cw_t[:, 3 * j:3 * j + 1],
                                                                           in1=cbq[:, 1:w - 1], op0=ALU.mult, op1=ALU.add),
                  reads=[r_pg[q], r_cw, r_cbuf[q]], writes=[r_cbuf[q]])
            kb.op("dve", lambda e, j=j, q=q, cbq=cbq: e.scalar_tensor_tensor(out=cbq[:, 1:w - 1], in0=pg[q][:, 2:w], scalar=cw_t[:, 3 * j + 2:3 * j + 3],
                                                                           in1=cbq[:, 1:w - 1], op0=ALU.mult, op1=ALU.add),
                  reads=[r_pg[q], r_cw, r_cbuf[q]], writes=[r_cbuf[q]])
            sbq = sbuf_[q]
            kb.op("act", lambda e, q=q, cbq=cbq, sbq=sbq: e.activation(out=sbq[:, 1:w - 1], in_=cbq[:, 1:w - 1], func=AF.Silu),
                  reads=[r_cbuf[q]], writes=[r_sbuf[q]])
            kb.op("dve", lambda e, j=j, q=q, sbq=sbq: e.tensor_tensor(out=a[:, j, 1:w - 1], in0=sbq[:, 1:w - 1], in1=pv[q][:, 1:w - 1], op=ALU.mult),
                  reads=[r_sbuf[q], r_pv[q]], writes=[r_a])
        for m in range(8):
            q = m % 2
            for j in range(22):
                kb.op("pe", lambda e, m=m, j=j, q=q: e.matmul(po[q][:, :w - 2], lhsT=wdn_t[:, j, m * 128:(m + 1) * 128], rhs=a[:, j, 1:w - 1],
                                                               start=(j == 0), stop=(j == 21)),
                      reads=[r_wdn, r_a], writes=[r_po[q]])
            kb.op("dve", lambda e, m=m, q=q: e.scalar_tensor_tensor(out=xo[:, m, :w - 2], in0=po[q][:, :w - 2], scalar=mod[:, 16 + m:17 + m],
                                                                    in1=xt[s][:, m, 1:w - 1], op0=ALU.mult, op1=ALU.add),
                  reads=[r_po[q], r_mod, r_xt[s]], writes=[r_xo])
        if final:
            rstd2, r_rstd2 = nrm.emit(xo[:, :, :w - 2], r_xo, w - 2)
            for m in range(8):
                kb.op("dve", lambda e, m=m: e.scalar_tensor_tensor(out=xo[:, m, :w - 2], in0=xo[:, m, :w - 2], scalar=fg_t[:, m:m + 1],
                                                                   in1=rstd2, op0=ALU.mult, op1=ALU.mult),
                      reads=[r_xo, r_fg, r_rstd2], writes=[r_xo])
        kb.dma(out[:, :, o0:o1], xo[:, :, :w - 2], reads=[r_xo], q="sp", is_output=True)
    return kb.finish()


HD = 64
QPERM = [0, 4, 1, 5, 2, 6, 3, 7, 8, 12, 9, 13, 10, 14, 11, 15]


def build_attn(L=8192, NQ=4096, TW=512):
    kb = KB()
    nc = kb.nc
    xall = kb.inp("xall", [128, 8, L])
    xq = kb.inp("xq", [128, 8, NQ])
    c_in = kb.inp("c2", [128, 8, 2])
    adaw = kb.inp("adaw", [1024, 3 * D])
    adab = kb.inp("adab", [128, 24])
    ng = kb.inp("ng", [128, 8])
    wkv = kb.inp("wkv", [D, 768])
    wq = kb.inp("wq", [D, 2048])
    wo = kb.inp("wo", [D, D])
    gains = kb.inp("gains", [128, 4])
    ropek = kb.inp("ropek", [2, 128, L])
    ropeq = kb.inp("ropeq", [2, 128, NQ])
    out = kb.outp("out", [128, 8, NQ])

    kT = kb.sb([128, 2, L], BF16, "kT")
    r_kT = Res()
    NKT = L // 128
    vaug = kb.sb([128, NKT, 4, 65], BF16, "vaug")
    r_v = Res()
    kb.op("pool", lambda e: e.memset(vaug[:, :, :, 64:65], 1.0), writes=[r_v])
    pss = kb.ps("pss")
    r_pss = Res()
    mod, r_mod = emit_mod(kb, c_in, adaw, adab, 24, "mod", ps=pss, r_ps_in=r_pss)
    ng_t, r_ng = small_load(kb, ng, [128, 8], "ng_t")
    gn_t, r_gn = small_load(kb, gains, [128, 4], "gn_t")
    gs, r_gs = emit_gs(kb, mod, r_mod, 8, ng_t, r_ng, "gs")
    nrm = Norm(kb, TW, "nrm").init_eps()
    bones = kb.sb([128, 128], BF16, "bones")
    r_bones = Res()
    kb.op("dve", lambda e: e.memset(bones[:], 0.0), writes=[r_bones])
    kb.op("dve", lambda e: e.memset(bones[0:64, 0:64], 1.0), writes=[r_bones])
    kb.op("dve", lambda e: e.memset(bones[64:128, 64:128], 1.0), writes=[r_bones])
    sel = kb.sb([65, 64], F32, "sel")
    r_sel = Res()
    kb.op("dve", lambda e: e.memset(sel[:], 0.0), writes=[r_sel])
    kb.op("dve", lambda e: e.memset(sel[64:65, :], 1.0), writes=[r_sel])

    xt = kb.sb([128, 8, TW], F32, "xt")
    r_xt = Res()
    h = kb.sb([128, 8, TW], BF16, "h")
    r_h = Res()
    tmp = kb.sb([128, TW], F32, "tmp")
    r_tmp = Res()
    ctab = kb.sb([128, 2, TW], F32, "ctab")
    r_ctab = Res()
    sqh = kb.sb([128, TW], BF16, "sqh")
    r_sqh = Res()
    t1 = kb.sb([128, TW], F32, "t1")
    t2 = kb.sb([128, TW], F32, "t2")
    r_t1, r_t2 = Res(), Res()
    rs = kb.sb([128, TW], F32, "rs")
    r_rs = Res()
    pa = [kb.ps("pa%d" % i) for i in range(2)]
    r_pa = [Res(), Res()]

    def modnorm(src_ap, w):
        kb.dma(xt[:, :, :w], src_ap, writes=[r_xt])
        rstd, r_rstd = nrm.emit(xt[:, :, :w], r_xt, w)
        for k in range(8):
            kb.op("dve", lambda e, k=k: e.scalar_tensor_tensor(out=tmp[:, :w], in0=xt[:, k, :w], scalar=gs[:, k:k + 1],
                                                                in1=rstd, op0=ALU.mult, op1=ALU.mult),
                  reads=[r_xt, r_gs, r_rstd], writes=[r_tmp])
            kb.op("act", lambda e, k=k: e.activation(out=h[:, k, :w], in_=tmp[:, :w], func=AF.Identity,
                                                     bias=mod[:, k:k + 1], scale=1.0),
                  reads=[r_tmp, r_mod], writes=[r_h])

    def proj_rope(wt, r_wt, col, col_sw, gcol, dst_ap, r_dst, w, scale):
        for k in range(8):
            kb.op("pe", lambda e, k=k: e.matmul(pa[0][:, :w], lhsT=wt[:, k, col:col + 128], rhs=h[:, k, :w], start=(k == 0), stop=(k == 7)),
                  reads=[r_wt, r_h], writes=[r_pa[0]])
        for k in range(8):
            kb.op("pe", lambda e, k=k: e.matmul(pa[1][:, :w], lhsT=wt[:, k, col_sw:col_sw + 128], rhs=h[:, k, :w], start=(k == 0), stop=(k == 7)),
                  reads=[r_wt, r_h], writes=[r_pa[1]])
        kb.op("act", lambda e: e.activation(out=sqh[:, :w], in_=pa[0][:, :w], func=AF.Square), reads=[r_pa[0]], writes=[r_sqh])
        kb.op("pe", lambda e: e.matmul(pss[:, :w], lhsT=bones[:], rhs=sqh[:, :w], start=True, stop=True),
              reads=[r_sqh, r_bones], writes=[r_pss])
        kb.op("act", lambda e: e.activation(out=rs[:, :w], in_=pss[:, :w], func=AF.Sqrt, scale=1.0 / HD, bias=nrm.epsb[:]),
              reads=[r_pss, nrm.r_eps], writes=[r_rs])
        kb.op("dve", lambda e: e.reciprocal(out=rs[:, :w], in_=rs[:, :w]), reads=[r_rs], writes=[r_rs])
        kb.op("dve", lambda e: e.scalar_tensor_tensor(out=t1[:, :w], in0=pa[0][:, :w], scalar=gn_t[:, gcol:gcol + 1], in1=ctab[:, 0, :w],
                                                      op0=ALU.mult, op1=ALU.mult), reads=[r_pa[0], r_gn, r_ctab], writes=[r_t1])
        kb.op("dve", lambda e: e.scalar_tensor_tensor(out=t2[:, :w], in0=pa[1][:, :w], scalar=gn_t[:, gcol + 1:gcol + 2], in1=ctab[:, 1, :w],
                                                      op0=ALU.mult, op1=ALU.mult), reads=[r_pa[1], r_gn, r_ctab], writes=[r_t2])
        kb.op("pool", lambda e: e.tensor_tensor(out=t1[:, :w], in0=t1[:, :w], in1=t2[:, :w], op=ALU.add), reads=[r_t1, r_t2], writes=[r_t1])
        kb.op("dve", lambda e: e.scalar_tensor_tensor(out=dst_ap, in0=t1[:, :w], scalar=float(scale), in1=rs[:, :w],
                                                      op0=ALU.mult, op1=ALU.mult), reads=[r_t1, r_rs], writes=[r_dst])

    r_wkv = Res()
    g_wkv = nc.sbuf_tensor("wkv_t", [128, 8, 768], BF16)
    wkv_t = g_wkv.__enter__()
    srckv = wkv.rearrange("(k p) n -> p k n", p=128)
    for k in range(8):
        kb.dma(wkv_t[:, k, :], srckv[:, k, :], writes=[r_wkv], q="pool")
    pvp = kb.ps("pvp")
    r_pvp = Res()
    for ti in range(L // TW):
        t0 = ti * TW
        w = TW
        modnorm(xall[:, :, t0:t0 + w], w)
        kb.dma(ctab[:, :, :w], ropek[:, :, t0:t0 + w].rearrange("c p t -> p c t"), writes=[r_ctab])
        for kc in range(2):
            proj_rope(wkv_t, r_wkv, kc * 128, 256 + kc * 128, 2, kT[:, kc, t0:t0 + w], r_kT, w, 1.0)
        for ts in range(w // 128):
            kt = (t0 // 128) + ts
            for k in range(8):
                kb.op("pe", lambda e, k=k, ts=ts: e.matmul(pvp[:, 0:256], lhsT=h[:, k, ts * 128:(ts + 1) * 128], rhs=wkv_t[:, k, 512:768],
                                                           start=(k == 0), stop=(k == 7)), reads=[r_h, r_wkv], writes=[r_pvp])
            kb.op("act", lambda e, kt=kt: e.activation(out=vaug[:, kt, :, 0:64], in_=pvp[:, 0:256].rearrange("p (a b) -> p a b", a=4),
                                                       func=AF.Copy), reads=[r_pvp], writes=[r_v])
    r_free = r_wkv
    g_wkv.__exit__(None, None, None)

    r_wq, r_wo = Res(), Res()
    r_wq.r = list(r_free.r)
    r_wq.w = r_free.w
    r_wo.r = list(r_free.r)
    r_wo.w = r_free.w
    wq_t = kb.sb([128, 8, 2048], BF16, "wq_t")
    srcq = wq.rearrange("(k p) n -> p k n", p=128)
    for k in range(8):
        for c0 in range(0, 2048, 1024):
            kb.dma(wq_t[:, k, c0:c0 + 1024], srcq[:, k, c0:c0 + 1024], writes=[r_wq], q="pool")
    wo_t = kb.sb([128, 8, D], BF16, "wo_t")
    srco = wo.rearrange("(k p) n -> p k n", p=128)
    for k in range(8):
        kb.dma(wo_t[:, k, :], srco[:, k, :], writes=[r_wo], q="pool")
    qT = kb.sb([128, 8, TW], BF16, "qT")
    r_qT = Res()
    oT = kb.sb([128, 8, TW], BF16, "oT")
    r_oT = Res()
    otmp = [kb.sb([64, TW], BF16, "otmp%d" % i) for i in range(2)]
    r_otmp = [Res(), Res()]
    pT = [kb.sb([128, TW], BF16, "pT%d" % i) for i in range(3)]
    r_pT = [Res() for _ in range(3)]
    oacc = kb.sb([65, TW], F32, "oacc")
    r_oacc = Res()
    rec = kb.sb([64, TW], F32, "rec")
    r_rec = Res()
    psc = [kb.ps("psc%d" % i) for i in range(2)]
    r_psc = [Res(), Res()]
    pso = kb.ps("pso")
    r_pso = Res()
    pmisc = pvp
    r_pmisc = r_pvp

    for qi in range(NQ // TW):
        t0 = qi * TW
        w = TW
        modnorm(xq[:, :, t0:t0 + w], w)
        kb.dma(ctab[:, :, :w], ropeq[:, :, t0:t0 + w].rearrange("c p t -> p c t"), writes=[r_ctab])
        for j in range(8):
            proj_rope(wq_t, r_wq, j * 128, 1024 + j * 128, 0, qT[:, j, :w], r_qT, w, 0.125)
        for j in range(8):
            for hb in range(2):
                hq = QPERM[2 * j + hb]
                kvh = hq // 4
                assert kvh % 2 == hb
                kc = kvh // 2
                b0 = 64 * hb
                for kt in range(NKT):
                    sl = kt % 2
                    pl = kt % 3
                    kb.op("pe", lambda e, kt=kt, sl=sl: e.matmul(psc[sl][:, :w], lhsT=kT[b0:b0 + 64, kc, kt * 128:(kt + 1) * 128],
                                                                 rhs=qT[b0:b0 + 64, j, :w], start=True, stop=True),
                          reads=[r_kT, r_qT], writes=[r_psc[sl]])
                    kb.op("act", lambda e, sl=sl, pl=pl: e.activation(out=pT[pl][:, :w], in_=psc[sl][:, :w], func=AF.Exp),
                          reads=[r_psc[sl]], writes=[r_pT[pl]])
                    kb.op("pe", lambda e, kt=kt, pl=pl: e.matmul(pso[0:65, :w], lhsT=vaug[:, kt, kvh, :], rhs=pT[pl][:, :w],
                                                                 start=(kt == 0), stop=(kt == NKT - 1)),
                          reads=[r_v, r_pT[pl]], writes=[r_pso])
                kb.op("dve", lambda e: e.tensor_copy(out=oacc[:, :w], in_=pso[0:65, :w]), reads=[r_pso], writes=[r_oacc])
                kb.op("pe", lambda e: e.matmul(pmisc[0:64, :w], lhsT=sel[:], rhs=oacc[:, :w], start=True, stop=True),
                      reads=[r_sel, r_oacc], writes=[r_pmisc])
                kb.op("dve", lambda e: e.reciprocal(out=rec[:, :w], in_=pmisc[0:64, :w]), reads=[r_pmisc], writes=[r_rec])
                if hq % 2 == 0:
                    kb.op("dve", lambda e, hq=hq: e.tensor_tensor(out=oT[0:64, hq // 2, :w], in0=oacc[0:64, :w], in1=rec[:, :w], op=ALU.mult),
                          reads=[r_oacc, r_rec], writes=[r_oT])
                else:
                    osl = (hq // 2) % 2
                    kb.op("dve", lambda e, osl=osl: e.tensor_tensor(out=otmp[osl][:, :w], in0=oacc[0:64, :w], in1=rec[:, :w], op=ALU.mult),
                          reads=[r_oacc, r_rec], writes=[r_otmp[osl]])
                    kb.dma(oT[64:128, hq // 2, :w], otmp[osl][:, :w], reads=[r_otmp[osl]], writes=[r_oT], q="sp")
        for m in range(8):
            for k in range(8):
                kb.op("pe", lambda e, m=m, k=k: e.matmul(pmisc[:, :w], lhsT=wo_t[:, k, m * 128:(m + 1) * 128], rhs=oT[:, k, :w],
                                                         start=(k == 0), stop=(k == 7)), reads=[r_wo, r_oT], writes=[r_pmisc])
            kb.op("dve", lambda e, m=m: e.scalar_tensor_tensor(out=xt[:, m, :w], in0=pmisc[:, :w], scalar=mod[:, 16 + m:17 + m],
                                                               in1=xt[:, m, :w], op0=ALU.mult, op1=ALU.add),
                  reads=[r_pmisc, r_mod, r_xt], writes=[r_xt])
        kb.dma(out[:, :, t0:t0 + w], xt[:, :, :w], reads=[r_xt], q="sp", is_output=True)
    return kb.finish()


TWO_PI = 6.283185307179586


def emit_sincos(kb, ang, r_ang, sin_out, cos_out, r_out, shape, name):
    PI = 3.141592653589793
    MAGIC = 12582912.0
    kf = kb.sb(shape, F32, name + "_kf")
    mk = kb.sb(shape, F32, name + "_mk")
    r2, r3 = Res(), Res()
    kb.op("dve", lambda e: e.tensor_scalar(out=kf[:], in0=ang, scalar1=1.0 / TWO_PI, scalar2=MAGIC, op0=ALU.mult, op1=ALU.add), reads=[r_ang], writes=[r2])
    kb.op("dve", lambda e: e.tensor_scalar(out=kf[:], in0=kf[:], scalar1=-MAGIC, scalar2=None, op0=ALU.add), reads=[r2], writes=[r2])
    kb.op("dve", lambda e: e.scalar_tensor_tensor(out=ang, in0=kf[:], scalar=-TWO_PI, in1=ang, op0=ALU.mult, op1=ALU.add),
          reads=[r2, r_ang], writes=[r_ang])
    kb.op("dve", lambda e: e.tensor_scalar(out=kf[:], in0=ang, scalar1=-PI, scalar2=PI, op0=ALU.max, op1=ALU.min), reads=[r_ang], writes=[r2])
    kb.op("act", lambda e: e.activation(out=sin_out, in_=kf[:], func=AF.Sin), reads=[r2], writes=[r_out])
    kb.op("dve", lambda e: e.tensor_scalar(out=ang, in0=ang, scalar1=PI / 2, scalar2=None, op0=ALU.add), reads=[r_ang, r2], writes=[r_ang])
    kb.op("dve", lambda e: e.tensor_scalar(out=mk[:], in0=ang, scalar1=PI, scalar2=None, op0=ALU.is_gt), reads=[r_ang], writes=[r3])
    kb.op("dve", lambda e: e.scalar_tensor_tensor(out=ang, in0=mk[:], scalar=-TWO_PI, in1=ang, op0=ALU.mult, op1=ALU.add),
          reads=[r3, r_ang], writes=[r_ang])
    kb.op("dve", lambda e: e.tensor_scalar(out=kf[:], in0=ang, scalar1=-PI, scalar2=PI, op0=ALU.max, op1=ALU.min), reads=[r_ang, r_out], writes=[r2])
    kb.op("act", lambda e: e.activation(out=cos_out, in_=kf[:], func=AF.Sin), reads=[r2], writes=[r_out])


def build_s5(L=8192, NB=4, TW=512):
    kb = KB()
    hT = kb.inp("hT", [NB, 128, L])
    are = kb.inp("are", [128, 8])
    aim = kb.inp("aim", [128, 8])
    ldt = kb.inp("ldt", [128, 8])
    bre = kb.inp("bre", [128, 4, 128])
    bim = kb.inp("bim", [128, 4, 128])
    cre = kb.inp("cre", [128, 4, 128])
    cim = kb.inp("cim", [128, 4, 128])
    dsk = kb.inp("dsk", [128, 1])
    tau = kb.inp("tau", [128, TW + 1])
    y = kb.outp("y", [NB, 128, L])
    NG = 4
    are_t, r_are = small_load(kb, are, [128, 8], "are_t")
    aim_t, r_aim = small_load(kb, aim, [128, 8], "aim_t")
    ldt_t, r_ldt = small_load(kb, ldt, [128, 8], "ldt_t")
    bre_t, r_bre = small_load(kb, bre, [128, 4, 128], "bre_t")
    bim_t, r_bim = small_load(kb, bim, [128, 4, 128], "bim_t")
    cre_t, r_cre = small_load(kb, cre, [128, 4, 128], "cre_t")
    cim_t, r_cim = small_load(kb, cim, [128, 4, 128], "cim_t")
    dsk_t, r_dsk = small_load(kb, dsk, [128, 1], "dsk_t")
    tau_t, r_tau = small_load(kb, tau, [128, TW + 1], "tau_t")
    P = {}
    rP = Res()

    def sm(name):
        P[name] = kb.sb([128, 8], F32, "p_" + name)
        return P[name]
    for n in ("lre", "dt", "a", "th", "r", "s1", "c1", "lbr1", "lbi", "den", "fr", "fi", "t"):
        sm(n)
    V = lambda n: P[n][:]
    rr_all = [r_are, r_aim, r_ldt, rP]
    kb.op("dve", lambda e: e.tensor_scalar(out=V("lre"), in0=are_t[:], scalar1=-1e-4, scalar2=None, op0=ALU.min), reads=rr_all, writes=[rP])
    kb.op("act", lambda e: e.activation(out=V("dt"), in_=ldt_t[:], func=AF.Exp), reads=rr_all, writes=[rP])
    kb.op("dve", lambda e: e.tensor_tensor(out=V("a"), in0=V("lre"), in1=V("dt"), op=ALU.mult), reads=rr_all, writes=[rP])
    kb.op("dve", lambda e: e.tensor_tensor(out=V("th"), in0=aim_t[:], in1=V("dt"), op=ALU.mult), reads=rr_all, writes=[rP])
    kb.op("act", lambda e: e.activation(out=V("r"), in_=V("a"), func=AF.Exp), reads=rr_all, writes=[rP])
    kb.op("dve", lambda e: e.tensor_copy(out=V("t"), in_=V("th")), reads=rr_all, writes=[rP])
    emit_sincos(kb, V("t"), rP, V("s1"), V("c1"), rP, [128, 8], "sc0")
    kb.op("dve", lambda e: e.tensor_tensor(out=V("lbr1"), in0=V("r"), in1=V("c1"), op=ALU.mult), reads=[rP], writes=[rP])
    kb.op("dve", lambda e: e.tensor_scalar(out=V("lbr1"), in0=V("lbr1"), scalar1=-1.0, scalar2=None, op0=ALU.add), reads=[rP], writes=[rP])
    kb.op("dve", lambda e: e.tensor_tensor(out=V("lbi"), in0=V("r"), in1=V("s1"), op=ALU.mult), reads=[rP], writes=[rP])
    kb.op("dve", lambda e: e.tensor_tensor(out=V("den"), in0=V("lre"), in1=V("lre"), op=ALU.mult), reads=[rP], writes=[rP])
    kb.op("dve", lambda e: e.tensor_tensor(out=V("t"), in0=aim_t[:], in1=aim_t[:], op=ALU.mult), reads=[rP, r_aim], writes=[rP])
    kb.op("dve", lambda e: e.tensor_tensor(out=V("den"), in0=V("den"), in1=V("t"), op=ALU.add), reads=[rP], writes=[rP])
    kb.op("dve", lambda e: e.reciprocal(out=V("den"), in_=V("den")), reads=[rP], writes=[rP])
    kb.op("dve", lambda e: e.tensor_tensor(out=V("fr"), in0=V("lbr1"), in1=V("lre"), op=ALU.mult), reads=[rP], writes=[rP])
    kb.op("dve", lambda e: e.tensor_tensor(out=V("t"), in0=V("lbi"), in1=aim_t[:], op=ALU.mult), reads=[rP], writes=[rP])
    kb.op("dve", lambda e: e.tensor_tensor(out=V("fr"), in0=V("fr"), in1=V("t"), op=ALU.add), reads=[rP], writes=[rP])
    kb.op("dve", lambda e: e.tensor_tensor(out=V("fr"), in0=V("fr"), in1=V("den"), op=ALU.mult), reads=[rP], writes=[rP])
    kb.op("dve", lambda e: e.tensor_tensor(out=V("fi"), in0=V("lbi"), in1=V("lre"), op=ALU.mult), reads=[rP], writes=[rP])
    kb.op("dve", lambda e: e.tensor_tensor(out=V("t"), in0=V("lbr1"), in1=aim_t[:], op=ALU.mult), reads=[rP], writes=[rP])
    kb.op("dve", lambda e: e.tensor_tensor(out=V("fi"), in0=V("fi"), in1=V("t"), op=ALU.subtract), reads=[rP], writes=[rP])
    kb.op("dve", lambda e: e.tensor_tensor(out=V("fi"), in0=V("fi"), in1=V("den"), op=ALU.mult), reads=[rP], writes=[rP])
    crp = kb.sb([128, 4, 128], F32, "crp")
    ncip = kb.sb([128, 4, 128], F32, "ncip")
    ctmp = kb.sb([128, 128], F32, "ctmp")
    r_cp = Res()
    for j in range(NG):
        kb.op("dve", lambda e, j=j: e.tensor_scalar(out=ctmp[:], in0=cim_t[:, j, :], scalar1=P["fi"][:, j:j + 1], scalar2=None, op0=ALU.mult),
              reads=[rP, r_cim, r_cp], writes=[r_cp])
        kb.op("dve", lambda e, j=j: e.scalar_tensor_tensor(out=crp[:, j, :], in0=cre_t[:, j, :], scalar=P["fr"][:, j:j + 1], in1=ctmp[:],
                                                           op0=ALU.mult, op1=ALU.subtract), reads=[rP, r_cre, r_cp], writes=[r_cp])
        kb.op("dve", lambda e, j=j: e.tensor_scalar(out=ctmp[:], in0=cim_t[:, j, :], scalar1=P["fr"][:, j:j + 1], scalar2=None, op0=ALU.mult),
              reads=[rP, r_cim, r_cp], writes=[r_cp])
        kb.op("dve", lambda e, j=j: e.scalar_tensor_tensor(out=ncip[:, j, :], in0=cre_t[:, j, :], scalar=P["fi"][:, j:j + 1], in1=ctmp[:],
                                                           op0=ALU.mult, op1=ALU.add), reads=[rP, r_cre, r_cp], writes=[r_cp])
        kb.op("dve", lambda e, j=j: e.tensor_scalar(out=ncip[:, j, :], in0=ncip[:, j, :], scalar1=-1.0, scalar2=None, op0=ALU.mult),
              reads=[r_cp], writes=[r_cp])
    ctab = kb.sb([128, NG, TW + 1], F32, "s5ctab")
    stab = kb.sb([128, NG, TW + 1], F32, "s5stab")
    rtab = kb.sb([128, NG, TW], F32, "s5rtab")
    angt = kb.sb([128, TW + 1], F32, "angt")
    r_tab, r_angt = Res(), Res()
    for j in range(NG):
        kb.op("dve", lambda e, j=j: e.tensor_scalar(out=angt[:], in0=tau_t[:], scalar1=P["th"][:, j:j + 1], scalar2=None, op0=ALU.mult),
              reads=[rP, r_tau, r_angt], writes=[r_angt])
        emit_sincos(kb, angt[:], r_angt, stab[:, j, :], ctab[:, j, :], r_tab, [128, TW + 1], "sc%d" % (j + 1))
        kb.op("dve", lambda e, j=j: e.tensor_scalar(out=rtab[:, j, :], in0=tau_t[:, 0:TW], scalar1=0.0, scalar2=P["r"][:, j:j + 1],
                                                    op0=ALU.mult, op1=ALU.add), reads=[rP, r_tau], writes=[r_tab])
    ut = [kb.sb([128, TW], F32, "ut%d" % i) for i in range(2)]
    r_ut = [Res(), Res()]
    pbr = [kb.ps("pbr%d" % i) for i in range(2)]
    pbi = [kb.ps("pbi%d" % i) for i in range(2)]
    r_pbr = [Res(), Res()]
    r_pbi = [Res(), Res()]
    py = [kb.ps("py%d" % i) for i in range(2)]
    r_py = [Res(), Res()]
    names = ["bur", "bui", "m1", "m2", "m3", "m4", "wr", "wi", "zr", "zi", "xr", "xi"]
    T = {n: [kb.sb([128, TW], F32, "s5_%s%d" % (n, i)) for i in range(2)] for n in names}
    R = {n: [Res(), Res()] for n in names}
    carry = kb.sb([128, NG, 4], F32, "carry")
    r_carry = [Res() for _ in range(NG)]
    yo = [kb.sb([128, TW], F32, "yo%d" % i) for i in range(2)]
    r_yo = [Res(), Res()]
    it = 0
    for b in range(NB):
        for ck in range(L // TW):
            us = ck % 2
            kb.dma(ut[us][:], hT[b, :, ck * TW:(ck + 1) * TW], writes=[r_ut[us]])
            for gp in range(NG):
                s = it % 2
                it += 1
                g = lambda n: T[n][s][:]
                rg = lambda n: R[n][s]
                kb.op("pe", lambda e: e.matmul(pbr[s][:], lhsT=bre_t[:, gp, :], rhs=ut[us][:], start=True, stop=True),
                      reads=[r_bre, r_ut[us]], writes=[r_pbr[s]])
                kb.op("pe", lambda e: e.matmul(pbi[s][:], lhsT=bim_t[:, gp, :], rhs=ut[us][:], start=True, stop=True),
                      reads=[r_bim, r_ut[us]], writes=[r_pbi[s]])
                kb.op("act", lambda e: e.activation(out=g("bur"), in_=pbr[s][:], func=AF.Copy), reads=[r_pbr[s]], writes=[rg("bur")])
                kb.op("act", lambda e: e.activation(out=g("bui"), in_=pbi[s][:], func=AF.Copy), reads=[r_pbi[s]], writes=[rg("bui")])
                cc = ctab[:, gp, 0:TW]
                ss = stab[:, gp, 0:TW]
                kb.op("dve", lambda e: e.tensor_tensor(out=g("m1"), in0=g("bur"), in1=cc, op=ALU.mult), reads=[rg("bur"), r_tab], writes=[rg("m1")])
                kb.op("dve", lambda e: e.tensor_tensor(out=g("m2"), in0=g("bui"), in1=ss, op=ALU.mult), reads=[rg("bui"), r_tab], writes=[rg("m2")])
                kb.op("dve", lambda e: e.tensor_tensor(out=g("wr"), in0=g("m1"), in1=g("m2"), op=ALU.add), reads=[rg("m1"), rg("m2")], writes=[rg("wr")])
                kb.op("pool", lambda e: e.tensor_tensor(out=g("m3"), in0=g("bui"), in1=cc, op=ALU.mult), reads=[rg("bui"), r_tab], writes=[rg("m3")])
                kb.op("pool", lambda e: e.tensor_tensor(out=g("m4"), in0=g("bur"), in1=ss, op=ALU.mult), reads=[rg("bur"), r_tab], writes=[rg("m4")])
                kb.op("pool", lambda e: e.tensor_tensor(out=g("wi"), in0=g("m3"), in1=g("m4"), op=ALU.subtract), reads=[rg("m3"), rg("m4")], writes=[rg("wi")])
                if ck == 0:
                    ini_r, ini_i = 0.0, 0.0
                else:
                    ini_r, ini_i = carry[:, gp, 2:3], carry[:, gp, 3:4]
                kb.op("dve", lambda e: e.tensor_tensor_scan(out=g("zr"), data0=rtab[:, gp, :], data1=g("wr"), initial=ini_r, op0=ALU.mult, op1=ALU.add),
                      reads=[rg("wr"), r_tab, r_carry[gp]], writes=[rg("zr")])
                kb.op("dve", lambda e: e.tensor_tensor_scan(out=g("zi"), data0=rtab[:, gp, :], data1=g("wi"), initial=ini_i, op0=ALU.mult, op1=ALU.add),
                      reads=[rg("wi"), r_tab, r_carry[gp]], writes=[rg("zi")])
                cT = ctab[:, gp, TW:TW + 1]
                sT = stab[:, gp, TW:TW + 1]
                kb.op("dve", lambda e: e.tensor_tensor(out=carry[:, gp, 0:1], in0=T["zi"][s][:, TW - 1:TW], in1=sT, op=ALU.mult),
                      reads=[rg("zi"), r_tab, r_carry[gp]], writes=[r_carry[gp]])
                kb.op("dve", lambda e: e.scalar_tensor_tensor(out=carry[:, gp, 2:3], in0=T["zr"][s][:, TW - 1:TW], scalar=cT, in1=carry[:, gp, 0:1],
                                                              op0=ALU.mult, op1=ALU.subtract), reads=[rg("zr"), r_tab, r_carry[gp]], writes=[r_carry[gp]])
                kb.op("dve", lambda e: e.tensor_tensor(out=carry[:, gp, 1:2], in0=T["zi"][s][:, TW - 1:TW], in1=cT, op=ALU.mult),
                      reads=[rg("zi"), r_tab, r_carry[gp]], writes=[r_carry[gp]])
                kb.op("dve", lambda e: e.scalar_tensor_tensor(out=carry[:, gp, 3:4], in0=T["zr"][s][:, TW - 1:TW], scalar=sT, in1=carry[:, gp, 1:2],
                                                              op0=ALU.mult, op1=ALU.add), reads=[rg("zr"), r_tab, r_carry[gp]], writes=[r_carry[gp]])
                kb.op("dve", lambda e: e.tensor_tensor(out=g("m1"), in0=g("zr"), in1=cc, op=ALU.mult), reads=[rg("zr"), r_tab, rg("wr")], writes=[rg("m1")])
                kb.op("dve", lambda e: e.tensor_tensor(out=g("m2"), in0=g("zi"), in1=ss, op=ALU.mult), reads=[rg("zi"), r_tab, rg("wr")], writes=[rg("m2")])
                kb.op("dve", lambda e: e.tensor_tensor(out=g("xr"), in0=g("m1"), in1=g("m2"), op=ALU.subtract), reads=[rg("m1"), rg("m2")], writes=[rg("xr")])
                kb.op("pool", lambda e: e.tensor_tensor(out=g("m3"), in0=g("zr"), in1=ss, op=ALU.mult), reads=[rg("zr"), r_tab, rg("wi")], writes=[rg("m3")])
                kb.op("pool", lambda e: e.tensor_tensor(out=g("m4"), in0=g("zi"), in1=cc, op=ALU.mult), reads=[rg("zi"), r_tab, rg("wi")], writes=[rg("m4")])
                kb.op("pool", lambda e: e.tensor_tensor(out=g("xi"), in0=g("m3"), in1=g("m4"), op=ALU.add), reads=[rg("m3"), rg("m4")], writes=[rg("xi")])
                kb.op("pe", lambda e: e.matmul(py[us][:], lhsT=crp[:, gp, :], rhs=g("xr"), start=(gp == 0), stop=False),
                      reads=[r_cp, rg("xr")], writes=[r_py[us]])
                kb.op("pe", lambda e: e.matmul(py[us][:], lhsT=ncip[:, gp, :], rhs=g("xi"), start=False, stop=(gp == NG - 1)),
                      reads=[r_cp, rg("xi")], writes=[r_py[us]])
            kb.op("dve", lambda e: e.scalar_tensor_tensor(out=yo[us][:], in0=ut[us][:], scalar=dsk_t[:, 0:1], in1=py[us][:],
                                                          op0=ALU.mult, op1=ALU.add), reads=[r_ut[us], r_dsk, r_py[us]], writes=[r_yo[us]])
            kb.dma(y[b, :, ck * TW:(ck + 1) * TW], yo[us][:], reads=[r_yo[us]], q="sp", is_output=True)
    return kb.finish()


NFFT = 16384
MAGIC = 12582912.0
PI = 3.141592653589793


def build_hy2(NCB=16):
    kb = KB()
    u3 = kb.inp("u3", [3, 2, 64, 128, 2, 128])
    zext = kb.inp("zext", [33, NFFT])
    w1 = kb.inp("w1", [33, 64])
    w2 = kb.inp("w2", [64, 64])
    bf1 = kb.inp("bf1", [64, 3])
    w3 = kb.inp("w3", [64, 4, 128])
    decf = kb.inp("decf", [128, 128, 128])
    decb = kb.inp("decb", [128, 128, 128])
    skp = kb.inp("skp", [64, 2, 128])
    ftab = kb.inp("ftab", [128, 4, 256])
    fri = kb.inp("fri", [128, 2, 128])
    tw = kb.inp("tw", [128, 2, 128])
    zout = kb.outp("zout", [2, 64, 128, 2, 128])

    w1_t, r_w1 = small_load(kb, w1, [33, 64], "w1_t")
    w2_t, r_w2 = small_load(kb, w2, [64, 64], "w2_t")
    bf_t, r_bf = small_load(kb, bf1, [64, 3], "bf_t")
    sk_t, r_sk = small_load(kb, skp, [64, 2, 128], "sk_t")
    ft_t, r_ft = small_load(kb, ftab, [128, 4, 256], "ft_t")
    fr_t, r_fr = small_load(kb, fri, [128, 2, 128], "fr_t")
    tw_t, r_tw = small_load(kb, tw, [128, 2, 128], "tw_t")
    w3_t = kb.sb([64, 4, 128], BF16, "w3_t")
    r_w3 = Res()
    kb.dma(w3_t[:], w3, writes=[r_w3], q="pool")
    onesf = kb.sb([128, 128], F32, "onesf")
    r_ones = Res()
    kb.op("dve", lambda e: e.memset(onesf[:], 1.0), writes=[r_ones])

    ps1 = kb.ps("ps1", (128, 2048))
    ps2 = kb.ps("ps2", (128, 2048))
    r_ps1, r_ps2 = Res(), Res()

    a2T = kb.sb([64, NFFT], BF16, "a2T")
    r_a2 = Res()
    zc = [kb.sb([33, 512], F32, "zc%d" % i) for i in range(2)]
    r_zc = [Res(), Res()]
    arg = kb.sb([64, 512], F32, "arg")
    kf = kb.sb([64, 512], F32, "kfm")
    a1c = kb.sb([64, 512], F32, "a1c")
    r_arg, r_kf, r_a1c = Res(), Res(), Res()

    def sin_rr(src_ps, r_src, bcol, out_ap, r_out):
        kb.op("dve", lambda e: e.tensor_scalar(out=arg[:], in0=src_ps, scalar1=bf_t[:, bcol:bcol + 1], scalar2=bf_t[:, 2:3], op0=ALU.add, op1=ALU.mult),
              reads=[r_src, r_bf], writes=[r_arg])
        kb.op("dve", lambda e: e.tensor_scalar(out=kf[:], in0=arg[:], scalar1=1.0 / TWO_PI, scalar2=MAGIC, op0=ALU.mult, op1=ALU.add), reads=[r_arg], writes=[r_kf])
        kb.op("dve", lambda e: e.tensor_scalar(out=kf[:], in0=kf[:], scalar1=-MAGIC, scalar2=None, op0=ALU.add), reads=[r_kf], writes=[r_kf])
        kb.op("dve", lambda e: e.scalar_tensor_tensor(out=arg[:], in0=kf[:], scalar=-TWO_PI, in1=arg[:], op0=ALU.mult, op1=ALU.add),
              reads=[r_kf, r_arg], writes=[r_arg])
        kb.op("dve", lambda e: e.tensor_scalar(out=arg[:], in0=arg[:], scalar1=-PI, scalar2=PI, op0=ALU.max, op1=ALU.min), reads=[r_arg], writes=[r_arg])
        kb.op("act", lambda e: e.activation(out=out_ap, in_=arg[:], func=AF.Sin), reads=[r_arg], writes=[r_out])

    for ck in range(NFFT // 512):
        s = ck % 2
        kb.dma(zc[s][:], zext[:, ck * 512:(ck + 1) * 512], writes=[r_zc[s]])
        kb.op("pe", lambda e: e.matmul(ps1[0:64, 0:512], lhsT=w1_t[:], rhs=zc[s][:], start=True, stop=True), reads=[r_w1, r_zc[s]], writes=[r_ps1])
        sin_rr(ps1[0:64, 0:512], r_ps1, 0, a1c[:], r_a1c)
        kb.op("pe", lambda e: e.matmul(ps2[0:64, 0:512], lhsT=w2_t[:], rhs=a1c[:], start=True, stop=True), reads=[r_w2, r_a1c], writes=[r_ps2])
        sin_rr(ps2[0:64, 0:512], r_ps2, 1, a2T[:, ck * 512:(ck + 1) * 512], r_a2)

    dft = kb.sb([128, 8, 128], F32, "dft")
    dbt = kb.sb([128, 8, 128], F32, "dbt")
    r_dft, r_dbt = Res(), Res()
    kft = kb.sb([128, 2, 8, 128], F32, "kft")
    r_kft = Res()
    ktmp = kb.sb([128, 8, 64], F32, "ktmp")
    r_ktmp = Res()
    part = kb.sb([128, 16], F32, "part")
    rS = kb.sb([128, 16], F32, "rS")
    r_part, r_rS = Res(), Res()
    KR = kb.sb([128, 2, 8, 128], F32, "KR")
    KI = kb.sb([128, 2, 8, 128], F32, "KI")
    r_K = Res()
    A1 = kb.sb([128, 8, 2, 128], F32, "A1")
    A2 = kb.sb([128, 8, 2, 128], F32, "A2")
    r_A = Res()
    P1 = kb.sb([128, 8, 2, 128], F32, "P1")
    P2 = kb.sb([128, 8, 2, 128], F32, "P2")
    r_P = Res()
    M = [kb.sb([128, 8, 128], F32, "mm%d" % i) for i in range(4)]
    r_M = Res()
    X = kb.sb([64, 8, 2, 128], F32, "X")
    G = kb.sb([64, 8, 2, 128], F32, "G")
    Z1 = kb.sb([64, 8, 2, 128], F32, "Z1")
    Z2 = kb.sb([64, 8, 2, 128], F32, "Z2")
    T_ = kb.sb([64, 8, 2, 128], F32, "Tt")
    r_X, r_G, r_Z1, r_Z2, r_T = Res(), Res(), Res(), Res(), Res()
    twr_b = tw_t[:, 0, :].unsqueeze(1).to_broadcast([128, 8, 128])
    twi_b = tw_t[:, 1, :].unsqueeze(1).to_broadcast([128, 8, 128])

    def cplx_evac(ps, r_ps, tr, ti, r_t, conj, O1, O2, r_O, arr):
        v = ps.rearrange("k (a c n) -> k a c n", a=8, c=2)
        pre, pim = v[:, :, 0, :], v[:, :, 1, :]
        rd = [r_ps, r_t]
        kb.op("dve", lambda e: e.tensor_tensor(out=M[0][:], in0=pre, in1=tr, op=ALU.mult), reads=rd, writes=[r_M])
        kb.op("dve", lambda e: e.tensor_tensor(out=M[1][:], in0=pim, in1=ti, op=ALU.mult), reads=rd, writes=[r_M])
        kb.op("dve", lambda e: e.tensor_tensor(out=M[2][:], in0=pre, in1=ti, op=ALU.mult), reads=rd, writes=[r_M])
        kb.op("dve", lambda e: e.tensor_tensor(out=M[3][:], in0=pim, in1=tr, op=ALU.mult), reads=rd, writes=[r_M])
        if not conj:
            kb.op("pool", lambda e: e.tensor_tensor(out=O1[:, :, 0, :], in0=M[0][:], in1=M[1][:], op=ALU.subtract), reads=[r_M], writes=[r_O])
            kb.op("pool", lambda e: e.tensor_tensor(out=O1[:, :, 1, :], in0=M[2][:], in1=M[3][:], op=ALU.add), reads=[r_M], writes=[r_O])
        else:
            kb.op("pool", lambda e: e.tensor_tensor(out=O1[:, :, 0, :], in0=M[0][:], in1=M[1][:], op=ALU.add), reads=[r_M], writes=[r_O])
            kb.op("pool", lambda e: e.tensor_tensor(out=O1[:, :, 1, :], in0=M[3][:], in1=M[2][:], op=ALU.subtract), reads=[r_M], writes=[r_O])
        if arr == "fwd":
            kb.op("act", lambda e: e.activation(out=O2[:, :, 0, :], in_=O1[:, :, 1, :], func=AF.Copy, scale=-1.0), reads=[r_O], writes=[r_O])
            kb.op("act", lambda e: e.activation(out=O2[:, :, 1, :], in_=O1[:, :, 0, :], func=AF.Copy), reads=[r_O], writes=[r_O])
        else:
            kb.op("act", lambda e: e.activation(out=O2[:, :, 0, :], in_=O1[:, :, 1, :], func=AF.Copy), reads=[r_O], writes=[r_O])
            kb.op("act", lambda e: e.activation(out=O2[:, :, 1, :], in_=O1[:, :, 0, :], func=AF.Copy, scale=-1.0), reads=[r_O], writes=[r_O])

    def stage2(I1, I2, r_I, M_rows):
        f1 = I1.rearrange("k a c n -> k (a c n)")
        f2 = I2.rearrange("k a c n -> k (a c n)")
        for q in range(4):
            kb.op("pe", lambda e, q=q: e.matmul(ps2[0:M_rows, q * 512:(q + 1) * 512], lhsT=fr_t[:, 0, 0:M_rows], rhs=f1[:, q * 512:(q + 1) * 512],
                                                start=True, stop=False), reads=[r_fr, r_I], writes=[r_ps2])
            kb.op("pe", lambda e, q=q: e.matmul(ps2[0:M_rows, q * 512:(q + 1) * 512], lhsT=fr_t[:, 1, 0:M_rows], rhs=f2[:, q * 512:(q + 1) * 512],
                                                start=False, stop=True), reads=[r_fr, r_I], writes=[r_ps2])

    def conv_pass(src, r_src, o):
        for c in range(8):
            kb.op("pe", lambda e, c=c: e.matmul(ps1[:, c * 256:(c + 1) * 256], lhsT=src[:, c, 0, :], rhs=ft_t[0:64, 0, :], start=True, stop=False),
                  reads=[r_src, r_ft], writes=[r_ps1])
            kb.op("pe", lambda e, c=c: e.matmul(ps1[:, c * 256:(c + 1) * 256], lhsT=src[:, c, 1, :], rhs=ft_t[0:64, 1, :], start=False, stop=True),
                  reads=[r_src, r_ft], writes=[r_ps1])
        cplx_evac(ps1, r_ps1, twr_b, twi_b, r_tw, False, A1, A2, r_A, "fwd")
        stage2(A1, A2, r_A, 128)
        cplx_evac(ps2, r_ps2, KR[:, o], KI[:, o], r_K, False, P1, P2, r_P, "inv")
        for c in range(8):
            kb.op("pe", lambda e, c=c: e.matmul(ps1[:, c * 256:(c + 1) * 256], lhsT=P1[:, c, 0, :], rhs=ft_t[:, 2, :], start=True, stop=False),
                  reads=[r_P, r_ft], writes=[r_ps1])
            kb.op("pe", lambda e, c=c: e.matmul(ps1[:, c * 256:(c + 1) * 256], lhsT=P1[:, c, 1, :], rhs=ft_t[:, 3, :], start=False, stop=True),
                  reads=[r_P, r_ft], writes=[r_ps1])
        cplx_evac(ps1, r_ps1, twr_b, twi_b, r_tw, True, A1, A2, r_A, "inv")
        stage2(A1, A2, r_A, 64)

    for cb in range(NCB):
        ch0 = 8 * cb
        kb.dma(dft[:], decf[:, ch0:ch0 + 8, :], writes=[r_dft])
        kb.dma(dbt[:], decb[:, ch0:ch0 + 8, :], writes=[r_dbt])
        for p in range(128):
            psx, r_psx = (ps1, r_ps1) if p < 64 else (ps2, r_ps2)
            pp = p % 64
            kb.op("pe", lambda e, p=p, pp=pp, psx=psx: e.matmul(psx[:, pp * 32:(pp + 1) * 32], lhsT=a2T[:, p:NFFT:128], rhs=w3_t[:, :, ch0:ch0 + 8],
                                                              start=True, stop=True), reads=[r_a2, r_w3], writes=[r_psx])
        for o in range(2):
            for half in range(2):
                psx, r_psx = (ps1, r_ps1) if half == 0 else (ps2, r_ps2)
                vw = psx.rearrange("i (p q c) -> i q c p", q=4, c=8)
                hs = slice(half * 64, half * 64 + 64)
                kb.op("dve", lambda e, vw=vw, hs=hs, o=o: e.tensor_tensor(out=kft[:, o, :, hs], in0=vw[:, 2 * o], in1=dft[:, :, hs], op=ALU.mult),
                      reads=[r_psx, r_dft], writes=[r_kft])
                kb.op("dve", lambda e, vw=vw, hs=hs, o=o: e.tensor_tensor(out=ktmp[:], in0=vw[:, 2 * o + 1], in1=dbt[:, :, hs], op=ALU.mult),
                      reads=[r_psx, r_dbt], writes=[r_ktmp])
                kb.op("pool", lambda e, hs=hs, o=o: e.tensor_tensor(out=kft[:, o, :, hs], in0=kft[:, o, :, hs], in1=ktmp[:], op=ALU.add),
                      reads=[r_ktmp, r_kft], writes=[r_kft])
        kb.op("dve", lambda e: e.tensor_reduce(out=part[:], in_=kft[:].rearrange("i o c p -> i (o c) p"), axis=AX.X, op=ALU.add, apply_absolute_value=True),
              reads=[r_kft], writes=[r_part])
        kb.op("pe", lambda e: e.matmul(ps1[:, 0:16], lhsT=onesf[:], rhs=part[:], start=True, stop=True), reads=[r_ones, r_part], writes=[r_ps1])
        kb.op("dve", lambda e: e.tensor_scalar(out=rS[:], in0=ps1[:, 0:16], scalar1=float(NFFT), scalar2=None, op0=ALU.mult), reads=[r_ps1], writes=[r_rS])
        kb.op("dve", lambda e: e.reciprocal(out=rS[:], in_=rS[:]), reads=[r_rS], writes=[r_rS])
        for o in range(2):
            for c in range(8):
                kb.op("pe", lambda e, c=c, o=o: e.matmul(ps1[:, c * 256:(c + 1) * 256], lhsT=kft[:, o, c, :], rhs=ft_t[:, 0, :], start=True, stop=True),
                      reads=[r_kft, r_ft], writes=[r_ps1])
            cplx_evac(ps1, r_ps1, twr_b, twi_b, r_tw, False, A1, A2, r_A, "fwd")
            stage2(A1, A2, r_A, 128)
            v2 = ps2.rearrange("k (a c n) -> k a c n", a=8, c=2)
            rsb = rS[:, o * 8:(o + 1) * 8].unsqueeze(2).to_broadcast([128, 8, 128])
            kb.op("dve", lambda e, o=o, v2=v2, rsb=rsb: e.tensor_tensor(out=KR[:, o], in0=v2[:, :, 0, :], in1=rsb, op=ALU.mult),
                  reads=[r_ps2, r_rS], writes=[r_K])
            kb.op("dve", lambda e, o=o, v2=v2, rsb=rsb: e.tensor_tensor(out=KI[:, o], in0=v2[:, :, 1, :], in1=rsb, op=ALU.mult),
                  reads=[r_ps2, r_rS], writes=[r_K])
        for pr in range(2):
            kb.dma(X[:], u3[0, pr, :, ch0:ch0 + 8, :, :], writes=[r_X])
            kb.dma(G[:], u3[1, pr, :, ch0:ch0 + 8, :, :], writes=[r_G])
            conv_pass(X, r_X, 0)
            sk0 = sk_t[:, 0, ch0:ch0 + 8].unsqueeze(2).to_broadcast([64, 8, 256])
            sk1 = sk_t[:, 1, ch0:ch0 + 8].unsqueeze(2).to_broadcast([64, 8, 256])
            fl = lambda t: t[:].rearrange("i a c n -> i a (c n)")
            yv = ps2[0:64, :].rearrange("i (a n) -> i a n", a=8)
            kb.op("dve", lambda e: e.tensor_tensor(out=fl(T_), in0=fl(X), in1=sk0, op=ALU.mult), reads=[r_X, r_sk], writes=[r_T])
            kb.op("dve", lambda e: e.tensor_tensor(out=fl(T_), in0=fl(T_), in1=yv, op=ALU.add), reads=[r_T, r_ps2], writes=[r_T])
            kb.op("pool", lambda e: e.tensor_tensor(out=fl(Z1), in0=fl(T_), in1=fl(G), op=ALU.mult), reads=[r_T, r_G], writes=[r_Z1])
            kb.dma(G[:], u3[2, pr, :, ch0:ch0 + 8, :, :], writes=[r_G])
            conv_pass(Z1, r_Z1, 1)
            kb.op("dve", lambda e: e.tensor_tensor(out=fl(T_), in0=fl(Z1), in1=sk1, op=ALU.mult), reads=[r_Z1, r_sk], writes=[r_T])
            kb.op("dve", lambda e: e.tensor_tensor(out=fl(T_), in0=fl(T_), in1=yv, op=ALU.add), reads=[r_T, r_ps2], writes=[r_T])
            kb.op("pool", lambda e: e.tensor_tensor(out=fl(Z2), in0=fl(T_), in1=fl(G), op=ALU.mult), reads=[r_T, r_G], writes=[r_Z2])
            kb.dma(zout[pr, :, ch0:ch0 + 8, :, :], Z2[:], reads=[r_Z2], q="sp", is_output=True)
    return kb.finish()


def build_tok(kind, NT):
    kb = KB()
    halo = 2 if kind == "hy1" else 0
    TW = 256 if kind == "hy1" else 512
    W = TW + halo
    xT = kb.inp("xT", [128, 8, NT + halo])
    c_in = kb.inp("c2", [128, 8, 2])
    adaw = kb.inp("adaw", [1024, 3 * D])
    adab = kb.inp("adab", [128, 24])
    pss = kb.ps("pmod")
    r_pss = Res()
    mod, r_mod = emit_mod(kb, c_in, adaw, adab, 24, "mod", ps=pss, r_ps_in=r_pss)
    xt = [kb.sb([128, 8, W], F32, "xt%d" % i) for i in range(2)]
    r_xt = [Res(), Res()]
    tmp = kb.sb([128, W], F32, "tmp")
    r_tmp = Res()
    pa = [kb.ps("pa%d" % i) for i in range(2)]
    pb = [kb.ps("pb%d" % i) for i in range(2)]
    r_pa = [Res(), Res()]
    r_pb = [Res(), Res()]
    if kind in ("hy1", "s5a"):
        ng = kb.inp("ng", [128, 8])
        ng_t, r_ng = small_load(kb, ng, [128, 8], "ng_t")
        gs, r_gs = emit_gs(kb, mod, r_mod, 8, ng_t, r_ng, "gs")
        nrm = Norm(kb, W, "nrm").init_eps()
    if kind == "hy1":
        win = kb.inp("win", [D, 3 * D])
        cw = kb.inp("cw", [128, 72])
        cb = kb.inp("cb", [128, 24])
        msk = kb.inp("msk", [128, 2])
        out = kb.outp("out", [128, 24, NT])
        r_w = Res()
        w_t = load_cast_weight(kb, win, 8, 3 * D, "w_t", r_w)
        cw_t, r_cw = small_load(kb, cw, [128, 72], "cw_t")
        cb_t, r_cb = small_load(kb, cb, [128, 24], "cb_t")
        mk_t, r_mk = small_load(kb, msk, [128, 2], "mk_t")
        h = kb.sb([128, 8, W], BF16, "h")
        r_h = Res()
        uo = kb.sb([128, 24, TW], F32, "uo")
        r_uo = Res()
    elif kind == "s5a":
        out = kb.outp("out", [128, 8, NT])
        ho = kb.sb([128, 8, W], F32, "ho")
        r_ho = Res()
    elif kind == "hypost":
        zin = kb.inp("z", [128, 8, NT])
        wout = kb.inp("wout", [D, D])
        out = kb.outp("out", [128, 8, NT])
        r_w = Res()
        w_t = load_cast_weight(kb, wout, 8, D, "w_t", r_w)
        zt = kb.sb([128, 8, W], F32, "zt")
        zb = kb.sb([128, 8, W], BF16, "zb")
        r_zt, r_zb = Res(), Res()
        xo = kb.sb([128, 8, W], F32, "xo")
        r_xo = Res()
    elif kind == "s5post":
        yf = kb.inp("yf", [128, 8, NT])
        yb = kb.inp("yb", [128, 8, NT])
        wglu = kb.inp("wglu", [D, 2 * D])
        out = kb.outp("out", [128, 8, NT])
        r_w = Res()
        w_t = load_cast_weight(kb, wglu, 8, 2 * D, "w_t", r_w)
        y1 = kb.sb([128, 8, W], F32, "y1")
        y2 = kb.sb([128, 8, W], F32, "y2")
        r_y1, r_y2 = Res(), Res()
        t3 = kb.sb([128, 8, W], F32, "t3")
        r_t3 = Res()
        gl = kb.sb([128, 8, W], BF16, "gl")
        r_gl = Res()
        sg = kb.sb([128, W], F32, "sg")
        r_sg = Res()
        xo = kb.sb([128, 8, W], F32, "xo")
        r_xo = Res()
    ntiles = (NT + TW - 1) // TW
    for ti in range(ntiles):
        o0 = ti * TW
        o1 = min(NT, o0 + TW)
        w = o1 - o0 + halo
        wo_ = o1 - o0
        s = ti % 2
        kb.dma(xt[s][:, :, :w], xT[:, :, o0:o0 + w], writes=[r_xt[s]])
        if kind in ("hy1", "s5a"):
            rstd, r_rstd = nrm.emit(xt[s][:, :, :w], r_xt[s], w)
            for k in range(8):
                kb.op("dve", lambda e, k=k: e.scalar_tensor_tensor(out=tmp[:, :w], in0=xt[s][:, k, :w], scalar=gs[:, k:k + 1],
                                                                    in1=rstd, op0=ALU.mult, op1=ALU.mult),
                      reads=[r_xt[s], r_gs, r_rstd], writes=[r_tmp])
                if kind == "hy1":
                    kb.op("act", lambda e, k=k: e.activation(out=h[:, k, :w], in_=tmp[:, :w], func=AF.Identity, bias=mod[:, k:k + 1], scale=1.0),
                          reads=[r_tmp, r_mod], writes=[r_h])
                else:
                    kb.op("act", lambda e, k=k: e.activation(out=ho[:, k, :w], in_=tmp[:, :w], func=AF.Identity, bias=mod[:, k:k + 1], scale=1.0),
                          reads=[r_tmp, r_mod], writes=[r_ho])
        if kind == "s5a":
            kb.dma(out[:, :, o0:o1], ho[:, :, :w], reads=[r_ho], q="sp", is_output=True)
        elif kind == "hy1":
            if ti == 0:
                kb.op("dve", lambda e: e.tensor_scalar(out=h[:, :, 0:1], in0=h[:, :, 0:1], scalar1=mk_t[:, 0:1], scalar2=None, op0=ALU.mult),
                      reads=[r_mk, r_h], writes=[r_h])
            if ti == ntiles - 1:
                kb.op("dve", lambda e: e.tensor_scalar(out=h[:, :, w - 1:w], in0=h[:, :, w - 1:w], scalar1=mk_t[:, 1:2], scalar2=None, op0=ALU.mult),
                      reads=[r_mk, r_h], writes=[r_h])
            for j in range(24):
                q = j % 2
                for k in range(8):
                    kb.op("pe", lambda e, k=k, j=j, q=q: e.matmul(pa[q][:, :w], lhsT=w_t[:, k, j * 128:(j + 1) * 128], rhs=h[:, k, :w],
                                                                   start=(k == 0), stop=(k == 7)), reads=[r_w, r_h], writes=[r_pa[q]])
                kb.op("act", lambda e, j=j, q=q: e.activation(out=uo[:, j, :wo_], in_=pa[q][:, 1:w - 1], func=AF.Identity,
                                                              scale=cw_t[:, 3 * j + 1:3 * j + 2], bias=cb_t[:, j:j + 1]),
                      reads=[r_pa[q], r_cw, r_cb], writes=[r_uo])
                kb.op("dve", lambda e, j=j, q=q: e.scalar_tensor_tensor(out=uo[:, j, :wo_], in0=pa[q][:, 0:w - 2], scalar=cw_t[:, 3 * j:3 * j + 1],
                                                                        in1=uo[:, j, :wo_], op0=ALU.mult, op1=ALU.add),
                      reads=[r_pa[q], r_cw, r_uo], writes=[r_uo])
                kb.op("dve", lambda e, j=j, q=q: e.scalar_tensor_tensor(out=uo[:, j, :wo_], in0=pa[q][:, 2:w], scalar=cw_t[:, 3 * j + 2:3 * j + 3],
                                                                        in1=uo[:, j, :wo_], op0=ALU.mult, op1=ALU.add),
                      reads=[r_pa[q], r_cw, r_uo], writes=[r_uo])
            kb.dma(out[:, :, o0:o1], uo[:, :, :wo_], reads=[r_uo], q="sp", is_output=True)
        elif kind == "hypost":
            kb.dma(zt[:, :, :w], zin[:, :, o0:o1], writes=[r_zt])
            kb.op("act", lambda e: e.activation(out=zb[:, :, :w], in_=zt[:, :, :w], func=AF.Copy), reads=[r_zt], writes=[r_zb])
            for m in range(8):
                q = m % 2
                for k in range(8):
                    kb.op("pe", lambda e, k=k, m=m, q=q: e.matmul(pa[q][:, :w], lhsT=w_t[:, k, m * 128:(m + 1) * 128], rhs=zb[:, k, :w],
                                                                   start=(k == 0), stop=(k == 7)), reads=[r_w, r_zb], writes=[r_pa[q]])
                kb.op("dve", lambda e, m=m, q=q: e.scalar_tensor_tensor(out=xo[:, m, :w], in0=pa[q][:, :w], scalar=mod[:, 16 + m:17 + m],
                                                                        in1=xt[s][:, m, :w], op0=ALU.mult, op1=ALU.add),
                      reads=[r_pa[q], r_mod, r_xt[s]], writes=[r_xo])
            kb.dma(out[:, :, o0:o1], xo[:, :, :w], reads=[r_xo], q="sp", is_output=True)
        elif kind == "s5post":
            kb.dma(y1[:, :, :w], yf[:, :, o0:o1], writes=[r_y1])
            kb.dma(y2[:, :, :w], yb[:, :, o0:o1], writes=[r_y2])
            a3 = lambda t: t[:, :, :w]
            kb.op("dve", lambda e: e.tensor_tensor(out=a3(y1), in0=a3(y1), in1=a3(y2), op=ALU.add), reads=[r_y1, r_y2], writes=[r_y1])
            kb.op("pool", lambda e: e.tensor_tensor(out=a3(t3), in0=a3(y1), in1=a3(y1), op=ALU.mult), reads=[r_y1], writes=[r_t3])
            kb.op("dve", lambda e: e.tensor_scalar(out=a3(t3), in0=a3(t3), scalar1=0.044715, scalar2=1.0, op0=ALU.mult, op1=ALU.add), reads=[r_t3], writes=[r_t3])
            kb.op("pool", lambda e: e.tensor_tensor(out=a3(t3), in0=a3(t3), in1=a3(y1), op=ALU.mult), reads=[r_y1, r_t3], writes=[r_t3])
            kb.op("act", lambda e: e.activation(out=a3(t3), in_=a3(t3), func=AF.Sigmoid, scale=1.5957691216057308), reads=[r_t3], writes=[r_t3])
            kb.op("dve", lambda e: e.tensor_tensor(out=a3(gl), in0=a3(t3), in1=a3(y1), op=ALU.mult), reads=[r_y1, r_t3], writes=[r_gl])
            for m in range(8):
                q = m % 2
                for k in range(8):
                    kb.op("pe", lambda e, k=k, m=m, q=q: e.matmul(pa[q][:, :w], lhsT=w_t[:, k, m * 128:(m + 1) * 128], rhs=gl[:, k, :w],
                                                                   start=(k == 0), stop=(k == 7)), reads=[r_w, r_gl], writes=[r_pa[q]])
                for k in range(8):
                    kb.op("pe", lambda e, k=k, m=m, q=q: e.matmul(pb[q][:, :w], lhsT=w_t[:, k, D + m * 128:D + (m + 1) * 128], rhs=gl[:, k, :w],
                                                                   start=(k == 0), stop=(k == 7)), reads=[r_w, r_gl], writes=[r_pb[q]])
                kb.op("act", lambda e, q=q: e.activation(out=sg[:, :w], in_=pb[q][:, :w], func=AF.Sigmoid), reads=[r_pb[q]], writes=[r_sg])
                kb.op("dve", lambda e, q=q: e.tensor_tensor(out=sg[:, :w], in0=sg[:, :w], in1=pa[q][:, :w], op=ALU.mult), reads=[r_sg, r_pa[q]], writes=[r_sg])
                kb.op("dve", lambda e, m=m: e.scalar_tensor_tensor(out=xo[:, m, :w], in0=sg[:, :w], scalar=mod[:, 16 + m:17 + m],
                                                                   in1=xt[s][:, m, :w], op0=ALU.mult, op1=ALU.add),
                      reads=[r_sg, r_mod, r_xt[s]], writes=[r_xo])
            kb.dma(out[:, :, o0:o1], xo[:, :, :w], reads=[r_xo], q="sp", is_output=True)
    return kb.finish()


import math
import numpy as np
L = 8192; NFFT = 16384; Dm = 1024

def hy_consts():
    t = np.linspace(0.0, 1.0, L, dtype=np.float32)[:, None]
    bands = 16
    w = (2.0 * math.pi * np.arange(L, dtype=np.float32) / L).astype(np.float32)
    f = np.linspace(1e-4, bands - 1, bands, dtype=np.float32)
    ang = w[:, None] * f[None, :]
    z = np.concatenate([t, np.cos(ang), -np.sin(ang)], -1).astype(np.float32)
    deltas = np.abs(np.linspace(math.log(1e-2) / 1.5, math.log(1e-2) / 0.3, Dm, dtype=np.float32))
    decay = np.exp(-t * deltas[None, :]).astype(np.float32)
    idx = np.arange(NFFT)
    src = np.where(idx < L, idx, np.where(idx == L, 0, 2 * L - idx))
    zext = np.ascontiguousarray(z[src].T)
    dec_f = np.where((idx < L)[:, None], decay[src], 0.0).astype(np.float32)
    dec_b = np.where((idx > L)[:, None], decay[src], 0.0).astype(np.float32)
    k = np.arange(128)
    F = np.exp(-2j * np.pi * np.outer(k, k) / 128)
    Fr = F.real.astype(np.float32); Fi = F.imag.astype(np.float32)
    ftab = np.stack([np.concatenate([Fr, Fi], 1), np.concatenate([-Fi, Fr], 1),
                     np.concatenate([Fr, -Fi], 1), np.concatenate([Fi, Fr], 1)], 1)
    fri = np.stack([Fr, Fi], 1)
    T = np.exp(-2j * np.pi * np.outer(k, k) / NFFT)
    tw = np.stack([T.real.astype(np.float32), T.imag.astype(np.float32)], 1)
    return dict(z=z, decay=decay, zext=zext, dec_f=dec_f, dec_b=dec_b, ftab=np.ascontiguousarray(ftab.astype(np.float32)),
                fri=np.ascontiguousarray(fri), tw=np.ascontiguousarray(tw))

def dec_core(dec, core):
    return np.ascontiguousarray(dec[:, 128 * core:128 * core + 128].reshape(128, 128, 128).transpose(0, 2, 1))

def to_u3(u, core):
    a = u.reshape(2, 2, 64, 128, 3, Dm)[..., 128 * core:128 * core + 128]
    return np.ascontiguousarray(a.transpose(4, 0, 2, 5, 1, 3))

def from_zout(zs):
    out = np.zeros((4, L, Dm), np.float32)
    for core, zc in enumerate(zs):
        a = zc.transpose(0, 3, 1, 4, 2)
        out[:, :, 128 * core:128 * core + 128] = a.reshape(4, L, 128)
    return out


_PROGS = {}


def _prog(key, fn):
    if key not in _PROGS:
        _PROGS[key] = fn()
    return _PROGS[key]


def _fm(x):
    T, C = x.shape
    return np.ascontiguousarray(x.T.reshape(C // 128, 128, T).transpose(1, 0, 2))


def _unfm(a):
    return np.ascontiguousarray(a.transpose(2, 1, 0).reshape(a.shape[2], -1))


def _vfm(v):
    return np.ascontiguousarray(np.asarray(v, np.float32).reshape(-1, 128).T)


def _run(nc, in_maps):
    res = run_bass_kernel_spmd(nc, in_maps, core_ids=list(range(8)))
    return res.results


NTC = 4096
SEQ = 8192
NBATCH = 4
SWAP = np.arange(64) ^ 1


def _rope_fm(Ls):
    rows = Ls // 64
    nf = 16
    inv = (1.0 / (np.float32(10000.0) ** (np.arange(nf, dtype=np.float32) / np.float32(nf)))).astype(np.float32)
    r = np.arange(rows, dtype=np.float32)
    col = np.arange(64, dtype=np.float32)
    ang_r = np.broadcast_to(r[:, None, None] * inv, (rows, 64, nf))
    ang_c = np.broadcast_to(col[None, :, None] * inv, (rows, 64, nf))
    ang = np.concatenate([ang_r, ang_c], -1).reshape(Ls, 2 * nf).astype(np.float32)
    cos, sin = np.cos(ang), np.sin(ang)
    C = np.repeat(cos, 2, axis=1).T
    S = np.repeat(sin, 2, axis=1).T.copy()
    S[0::2] *= -1
    return np.ascontiguousarray(np.stack([np.concatenate([C, C], 0), np.concatenate([S, S], 0)], 0).astype(np.float32))


def _common(c, adaw, adab, b):
    return {"c2": np.ascontiguousarray(np.repeat(_vfm(c[b])[:, :, None], 2, axis=2)), "adaw": np.ascontiguousarray(adaw), "adab": _vfm(adab)}


def _halo_x(xs, b, half):
    xp = np.pad(xs[b], ((1, 1), (0, 0)))
    return _fm(xp[half * NTC: half * NTC + NTC + 2])


def _msk(half):
    return np.tile(np.array([[0.0 if half == 0 else 1.0, 1.0 if half == 0 else 0.0]], np.float32), (128, 1))


def _gather_tok(results, key="out"):
    xs = np.zeros((NBATCH, SEQ, D), np.float32)
    for core in range(8):
        b, half = core // 2, core % 2
        xs[b, half * NTC:(half + 1) * NTC] = _unfm(results[core][key])
    return xs


def run_attn(xs, c, adaw, adab, ng, w_qkv, w_o, qg, kg):
    nc = _prog("attn", lambda: build_attn(L=SEQ, NQ=NTC))
    wq_ = w_qkv[:, :1024].reshape(D, 16, 64)
    wk_ = w_qkv[:, 1024:1280].reshape(D, 4, 64)
    wv_ = w_qkv[:, 1280:]
    wq_p = wq_[:, QPERM, :]
    wq_all = np.ascontiguousarray(np.concatenate([wq_p.reshape(D, 1024), wq_p[:, :, SWAP].reshape(D, 1024)], 1))
    wkv = np.ascontiguousarray(np.concatenate([wk_.reshape(D, 256), wk_[:, :, SWAP].reshape(D, 256), wv_], 1))
    wo = np.ascontiguousarray(w_o)
    gains = np.ascontiguousarray(np.stack([np.tile(qg, 2), np.tile(qg[SWAP], 2), np.tile(kg, 2), np.tile(kg[SWAP], 2)], 1).astype(np.float32))
    rp = _rope_fm(SEQ)
    maps = []
    for core in range(8):
        b, half = core // 2, core % 2
        xf = _fm(xs[b])
        m = _common(c, adaw, adab, b)
        m.update({"xall": xf, "xq": np.ascontiguousarray(xf[:, :, half * NTC:(half + 1) * NTC]), "ng": _vfm(ng), "wkv": wkv, "wq": wq_all,
                  "wo": wo, "gains": gains, "ropek": rp, "ropeq": np.ascontiguousarray(rp[:, :, half * NTC:(half + 1) * NTC])})
        maps.append(m)
    return _gather_tok(_run(nc, maps))


def run_ffn(xs, c, adaw, adab, ng, wup, cw, cb, wdn, final_g=None):
    final = final_g is not None
    nc = _prog("ffn%d" % final, lambda: build_ffn(NTC, final=final))
    cwl = np.ascontiguousarray(cw.T.reshape(22, 128, 3).transpose(1, 0, 2).reshape(128, 66))
    maps = []
    for core in range(8):
        b, half = core // 2, core % 2
        m = _common(c, adaw, adab, b)
        m.update({"xT": _halo_x(xs, b, half), "ng": _vfm(ng), "wup": np.ascontiguousarray(wup), "wdn": np.ascontiguousarray(wdn),
                  "cw": cwl, "cb": _vfm(cb), "msk": _msk(half)})
        if final:
            m["fg"] = _vfm(final_g)
        maps.append(m)
    return _gather_tok(_run(nc, maps))


def run_hyena(xs, c, adaw, adab, ng, w_in, conv_w, conv_b, f_w1, f_b1, f_w2, f_b2, f_w3, f_freq, skip, w_out):
    nc1 = _prog("hy1", lambda: build_tok("hy1", NTC))
    cwl = np.ascontiguousarray(conv_w.T.reshape(24, 128, 3).transpose(1, 0, 2).reshape(128, 72))
    maps = []
    for core in range(8):
        b, half = core // 2, core % 2
        m = _common(c, adaw, adab, b)
        m.update({"xT": _halo_x(xs, b, half), "ng": _vfm(ng), "win": np.ascontiguousarray(w_in), "cw": cwl, "cb": _vfm(conv_b), "msk": _msk(half)})
        maps.append(m)
    u = _gather_tok_c(_run(nc1, maps), 3 * D)
    nc2 = _prog("hy2", lambda: build_hy2(NCB=16))
    C = hy_consts()
    maps = []
    for core in range(8):
        sl = slice(128 * core, 128 * core + 128)
        maps.append(dict(u3=to_u3(u, core), zext=C["zext"], w1=np.ascontiguousarray(f_w1), w2=np.ascontiguousarray(f_w2),
                         bf1=np.ascontiguousarray(np.stack([f_b1, f_b2, f_freq], 1)),
                         w3=np.ascontiguousarray(f_w3.reshape(64, 4, D)[:, :, sl]), decf=dec_core(C["dec_f"], core), decb=dec_core(C["dec_b"], core),
                         skp=np.ascontiguousarray(np.tile(skip[None, :, sl], (64, 1, 1))), ftab=C["ftab"], fri=C["fri"], tw=C["tw"]))
    r = _run(nc2, maps)
    z = from_zout([r[cc]["zout"] for cc in range(8)])
    nc3 = _prog("hypost", lambda: build_tok("hypost", NTC))
    maps = []
    for core in range(8):
        b, half = core // 2, core % 2
        sl = slice(half * NTC, (half + 1) * NTC)
        m = _common(c, adaw, adab, b)
        m.update({"xT": _fm(xs[b, sl]), "z": _fm(z[b, sl]), "wout": np.ascontiguousarray(w_out)})
        maps.append(m)
    return _gather_tok(_run(nc3, maps))


def _gather_tok_c(results, C_):
    o = np.zeros((NBATCH, SEQ, C_), np.float32)
    for core in range(8):
        b, half = core // 2, core % 2
        o[b, half * NTC:(half + 1) * NTC] = _unfm(results[core]["out"])
    return o


def _s5_params(A_re, A_im, log_dt, B_re, B_im, C_re, C_im, core, TW=512):
    gs = [8 * core + k for k in range(8)]
    are = np.zeros((128, 8), np.float32); aim = np.zeros((128, 8), np.float32); ldt = np.zeros((128, 8), np.float32)
    bre = np.zeros((128, 4, 128), np.float32); bim = np.zeros((128, 4, 128), np.float32)
    cre = np.zeros((128, 4, 128), np.float32); cim = np.zeros((128, 4, 128), np.float32)
    for gp in range(4):
        for g2 in range(2):
            g = gs[2 * gp + g2]
            sl = slice(64 * g2, 64 * g2 + 64)
            are[sl, gp] = A_re[g]; aim[sl, gp] = A_im[g]; ldt[sl, gp] = log_dt[g]
            rows = slice(16 * (2 * gp + g2), 16 * (2 * gp + g2) + 16)
            bre[rows, gp, sl] = B_re[g].T
            bim[rows, gp, sl] = B_im[g].T
            cre[sl, gp, rows] = C_re[g].T
            cim[sl, gp, rows] = C_im[g].T
    are[:, 4:] = are[:, :4]; aim[:, 4:] = aim[:, :4]; ldt[:, 4:] = ldt[:, :4]
    tau = np.tile(np.arange(TW + 1, dtype=np.float32)[None], (128, 1))
    return dict(are=are, aim=aim, ldt=ldt, bre=bre, bim=bim, cre=cre, cim=cim, tau=tau)


def run_s5(xs, c, adaw, adab, ng, A_re, A_im, log_dt, B_re, B_im, C_re, C_im, d_skip, w_glu):
    nc1 = _prog("s5a", lambda: build_tok("s5a", NTC))
    maps = []
    for core in range(8):
        b, half = core // 2, core % 2
        m = _common(c, adaw, adab, b)
        m.update({"xT": _fm(xs[b, half * NTC:(half + 1) * NTC]), "ng": _vfm(ng)})
        maps.append(m)
    h = _gather_tok(_run(nc1, maps))
    nc2 = _prog("s5", lambda: build_s5(L=SEQ, NB=NBATCH))
    ys = []
    for d in range(2):
        hd = h if d == 0 else h[:, ::-1]
        maps = []
        for core in range(8):
            m = _s5_params(A_re[d], A_im[d], log_dt[d], B_re[d], B_im[d], C_re[d], C_im[d], core)
            m["hT"] = np.ascontiguousarray(hd[:, :, 128 * core:128 * core + 128].transpose(0, 2, 1))
            dk = d_skip[128 * core:128 * core + 128, None] if d == 0 else np.zeros((128, 1), np.float32)
            m["dsk"] = np.ascontiguousarray(dk.astype(np.float32))
            maps.append(m)
        r = _run(nc2, maps)
        yd = np.concatenate([r[cc]["y"] for cc in range(8)], axis=1).transpose(0, 2, 1)
        ys.append(yd if d == 0 else yd[:, ::-1])
    nc3 = _prog("s5post", lambda: build_tok("s5post", NTC))
    maps = []
    for core in range(8):
        b, half = core // 2, core % 2
        sl = slice(half * NTC, (half + 1) * NTC)
        m = _common(c, adaw, adab, b)
        m.update({"xT": _fm(xs[b, sl]), "yf": _fm(ys[0][b, sl]), "yb": _fm(ys[1][b, sl]), "wglu": np.ascontiguousarray(w_glu)})
        maps.append(m)
    return _gather_tok(_run(nc3, maps))


def kernel(x, c, ada_w, ada_b, norm1_g, norm2_g, final_g,
           attn_w_qkv, attn_w_o, attn_q_gain, attn_k_gain,
           hy_w_in, hy_conv_w, hy_conv_b, hy_f_w1, hy_f_b1, hy_f_w2, hy_f_b2, hy_f_w3, hy_f_freq, hy_skip, hy_w_out,
           s5_A_re, s5_A_im, s5_log_dt, s5_B_re, s5_B_im, s5_C_re, s5_C_im, s5_D, s5_w_glu,
           ffn_w_up, ffn_conv_w, ffn_conv_b, ffn_w_down):
    A = lambda v: np.asarray(v, dtype=np.float32)
    xs = A(x)
    c = A(c)
    ada_w, ada_b = A(ada_w), A(ada_b)
    for i in range(4):
        m, j = i % 3, i // 3
        aw1, ab1 = ada_w[i][:, :3 * D], ada_b[i][:3 * D]
        aw2, ab2 = ada_w[i][:, 3 * D:], ada_b[i][3 * D:]
        if m == 0:
            xs = run_attn(xs, c, aw1, ab1, A(norm1_g)[i], A(attn_w_qkv)[j], A(attn_w_o)[j], A(attn_q_gain)[j], A(attn_k_gain)[j])
        elif m == 1:
            xs = run_hyena(xs, c, aw1, ab1, A(norm1_g)[i], A(hy_w_in)[j], A(hy_conv_w)[j], A(hy_conv_b)[j], A(hy_f_w1)[j], A(hy_f_b1)[j],
                           A(hy_f_w2)[j], A(hy_f_b2)[j], A(hy_f_w3)[j], A(hy_f_freq)[j], A(hy_skip)[j], A(hy_w_out)[j])
        else:
            xs = run_s5(xs, c, aw1, ab1, A(norm1_g)[i], A(s5_A_re)[j], A(s5_A_im)[j], A(s5_log_dt)[j], A(s5_B_re)[j], A(s5_B_im)[j],
                        A(s5_C_re)[j], A(s5_C_im)[j], A(s5_D)[j], A(s5_w_glu)[j])
        xs = run_ffn(xs, c, aw2, ab2, A(norm2_g)[i], A(ffn_w_up)[i], A(ffn_conv_w)[i], A(ffn_conv_b)[i], A(ffn_w_down)[i],
                     final_g=A(final_g) if i == 3 else None)
    return xs.astype(np.float32)
```

```python
import numpy as np
import ml_dtypes
import concourse.bass as bass
import concourse.mybir as mybir
from concourse.bass_utils import run_bass_kernel_spmd

F32 = mybir.dt.float32
BF16 = mybir.dt.bfloat16
I32 = mybir.dt.int32
AF = mybir.ActivationFunctionType
ALU = mybir.AluOpType
AX = mybir.AxisListType

D = 1024
DFF = 2816
EPS = 1e-6


class Res:
    __slots__ = ("w", "r")

    def __init__(self):
        self.w = None
        self.r = []


class KB:
    ENG = ("pe", "act", "dve", "pool", "sp")

    def __init__(self, n_dma_sems=24):
        nc = bass.Bass("TRN2", target_bir_lowering=False)
        self.nc = nc
        self.e = {"pe": nc.tensor, "act": nc.scalar, "dve": nc.vector, "pool": nc.gpsimd, "sp": nc.sync}
        self.sem = {}
        self.cnt = {}
        for k in self.ENG:
            self.sem[k] = nc.semaphore("s_" + k).__enter__()
            self.cnt[k] = 0
        self.dsem = []
        for i in range(n_dma_sems):
            key = "d%d" % i
            self.sem[key] = nc.semaphore("s_" + key).__enter__()
            self.cnt[key] = 0
            self.dsem.append(key)
        self.dnext = 0
        self.waited = {k: {} for k in self.ENG}
        self.out_events = []
        self.n_inst = 0
        self._names = 0

    def sb(self, shape, dt, name=None):
        self._names += 1
        return self.nc.sbuf_tensor(name or ("t%d" % self._names), list(shape), dt).__enter__()

    def ps(self, name=None, shape=(128, 512), dt=F32):
        self._names += 1
        return self.nc.psum_tensor(name or ("p%d" % self._names), list(shape), dt).__enter__()

    def dram(self, name, shape, dt, kind="Internal"):
        return self.nc.dram_tensor(name, list(shape), dt, kind=kind).ap()

    def inp(self, name, shape, dt=F32):
        return self.nc.dram_tensor(name, list(shape), dt, kind="ExternalInput").ap()

    def outp(self, name, shape, dt=F32):
        return self.nc.dram_tensor(name, list(shape), dt, kind="ExternalOutput").ap()

    def _wait(self, eng, ev):
        if ev is None:
            return
        key, val = ev
        if key == eng and eng == "pe":
            return
        if self.waited[eng].get(key, 0) >= val:
            return
        self.e[eng].wait_ge(self.sem[key], val)
        self.waited[eng][key] = val

    def _deps(self, eng, reads, writes):
        for r in reads:
            self._wait(eng, r.w)
        for w in writes:
            self._wait(eng, w.w)
            for ev in w.r:
                if ev[0] == eng:
                    continue
                self._wait(eng, ev)

    def _commit(self, ev, reads, writes):
        for r in reads:
            r.r.append(ev)
            if len(r.r) > 64:
                best = {}
                for k, v in r.r:
                    if best.get(k, 0) < v:
                        best[k] = v
                r.r = list(best.items())
        for w in writes:
            w.w = ev
            w.r = []

    def op(self, eng, fn, reads=(), writes=()):
        self._deps(eng, reads, writes)
        inst = fn(self.e[eng])
        self.cnt[eng] += 1
        inst.then_inc(self.sem[eng], 1)
        ev = (eng, self.cnt[eng])
        self._commit(ev, reads, writes)
        self.n_inst += 1
        return ev

    def dma(self, out, in_, reads=(), writes=(), q="sp", is_output=False, **kw):
        key = self.dsem[self.dnext]
        self.dnext = (self.dnext + 1) % len(self.dsem)
        if self.cnt[key] > 0:
            self._wait(q, (key, self.cnt[key]))
        self._deps(q, reads, writes)
        inst = self.e[q].dma_start(out=out, in_=in_, **kw)
        self.cnt[key] += 16
        inst.then_inc(self.sem[key], 16)
        ev = (key, self.cnt[key])
        self._commit(ev, reads, writes)
        if is_output:
            self.out_events.append(ev)
        self.n_inst += 1
        return ev

    def finish(self):
        for ev in self.out_events:
            self._wait("sp", ev)
        return self.nc


def bf(x):
    return np.asarray(x, dtype=np.float32).astype(ml_dtypes.bfloat16).astype(np.float32)


def load_cast_weight(kb, w_ap, kchunks, ncols, name, res, colblk=1024):
    wt = kb.sb([128, kchunks, ncols], BF16, name)
    src = w_ap.rearrange("(k p) n -> p k n", p=128)
    for k in range(kchunks):
        for c0 in range(0, ncols, colblk):
            c1 = min(ncols, c0 + colblk)
            kb.dma(wt[:, k, c0:c1], src[:, k, c0:c1], writes=[res], q="pool")
    return wt


def emit_mod(kb, c_ap, adaw_ap, adab_ap, nch, name, ps=None, r_ps_in=None):
    r_c, r_w, r_ps, r_mod, r_b = Res(), Res(), Res(), Res(), Res()
    ct = kb.sb([128, 8, 2], F32, name + "_c")
    sg = kb.sb([128, 8, 2], F32, name + "_sg")
    ca = kb.sb([128, 8, 2], F32, name + "_ca")
    bt = kb.sb([128, nch], F32, name + "_b")
    mod = kb.sb([128, nch], F32, name)
    kb.dma(ct[:], c_ap, writes=[r_c])
    kb.dma(bt[:], adab_ap, writes=[r_b])
    kb.op("act", lambda e: e.activation(out=sg[:], in_=ct[:], func=AF.Sigmoid), reads=[r_c], writes=[r_mod])
    kb.op("dve", lambda e: e.tensor_tensor(out=ca[:], in0=ct[:], in1=sg[:], op=ALU.mult), reads=[r_c, r_mod], writes=[r_ps])
    r_ca = r_ps
    src = adaw_ap.rearrange("(k p) n -> p k n", p=128)
    wts = [kb.sb([128, 8, 128], F32, name + "_w%d" % i) for i in range(2)]
    r_wts = [Res(), Res()]
    ps1 = ps if ps is not None else kb.ps(name + "_ps")
    r_ps1 = r_ps_in if r_ps_in is not None else Res()
    for j in range(nch):
        s = j % 2
        kb.dma(wts[s][:], src[:, :, j * 128:(j + 1) * 128], writes=[r_wts[s]])
        for k in range(8):
            kb.op("pe", lambda e, k=k, s=s, j=j: e.matmul(ps1[:, 2 * j:2 * j + 2], lhsT=wts[s][:, k, :], rhs=ca[:, k, :],
                                                      start=(k == 0), stop=(k == 7)),
                  reads=[r_wts[s], r_ca], writes=[r_ps1])
    kb.op("dve", lambda e: e.tensor_tensor(out=mod[:], in0=ps1[:, 0:2 * nch:2], in1=bt[:], op=ALU.add),
          reads=[r_ps1, r_b], writes=[r_mod])
    return mod, r_mod


def small_load(kb, ap, shape, name, dt=F32):
    t = kb.sb(shape, dt, name)
    r = Res()
    kb.dma(t[:], ap, writes=[r])
    return t, r


class Norm:
    def __init__(self, kb, W, name):
        self.kb = kb
        self.ones = kb.sb([128, 128], BF16, name + "_ones")
        self.r_ones = Res()
        kb.op("dve", lambda e: e.memset(self.ones[:], 1.0), writes=[self.r_ones])
        self.sq = kb.sb([128, 8, W], BF16, name + "_sq")
        self.r_sq = Res()
        self.ps = kb.ps(name + "_ps")
        self.r_ps = Res()
        self.sd = kb.sb([128, W], F32, name + "_sd")
        self.rstd = kb.sb([128, W], F32, name + "_rstd")
        self.r_sd = Res()
        self.r_rstd = Res()

    def emit(self, x3, r_x, w):
        kb = self
        kb = self.kb
        kb.op("act", lambda e: e.activation(out=self.sq[:, :, :w], in_=x3, func=AF.Square), reads=[r_x], writes=[self.r_sq])
        for k in range(8):
            kb.op("pe", lambda e, k=k: e.matmul(self.ps[:, :w], lhsT=self.ones[:], rhs=self.sq[:, k, :w], start=(k == 0), stop=(k == 7)),
                  reads=[self.r_sq, self.r_ones], writes=[self.r_ps])
        kb.op("act", lambda e: e.activation(out=self.sd[:, :w], in_=self.ps[:, :w], func=AF.Sqrt, scale=1.0 / D, bias=self.epsb[:]),
              reads=[self.r_ps, self.r_eps], writes=[self.r_sd])
        kb.op("dve", lambda e: e.reciprocal(out=self.rstd[:, :w], in_=self.sd[:, :w]), reads=[self.r_sd], writes=[self.r_rstd])
        return self.rstd[:, :w], self.r_rstd

    def init_eps(self):
        kb = self.kb
        self.epsb = kb.sb([128, 1], F32, "epsb%d" % id(self))
        self.r_eps = Res()
        kb.op("dve", lambda e: e.memset(self.epsb[:], EPS), writes=[self.r_eps])
        return self


def emit_gs(kb, mod, r_mod, sc_off, g_t, r_g, name):
    gs = kb.sb([128, 8], F32, name)
    r = Res()
    kb.op("dve", lambda e: e.scalar_tensor_tensor(out=gs[:], in0=mod[:, sc_off:sc_off + 8], scalar=1.0, in1=g_t[:],
                                                  op0=ALU.add, op1=ALU.mult), reads=[r_mod, r_g], writes=[r])
    return gs, r


def build_ffn(NT, final=False, TW=256):
    kb = KB()
    xT = kb.inp("xT", [128, 8, NT + 2])
    c_in = kb.inp("c2", [128, 8, 2])
    adaw = kb.inp("adaw", [1024, 3 * D])
    adab = kb.inp("adab", [128, 24])
    ng = kb.inp("ng", [128, 8])
    wup = kb.inp("wup", [D, 2 * DFF])
    wdn = kb.inp("wdn", [DFF, D])
    cw = kb.inp("cw", [128, 22 * 3])
    cb = kb.inp("cb", [128, 22])
    msk = kb.inp("msk", [128, 2])
    if final:
        fg = kb.inp("fg", [128, 8])
    out = kb.outp("out", [128, 8, NT])

    W = TW + 2
    r_wup, r_wdn = Res(), Res()
    wup_t = load_cast_weight(kb, wup, 8, 2 * DFF, "wup_t", r_wup, colblk=1408)
    wdn_t = load_cast_weight(kb, wdn, 22, D, "wdn_t", r_wdn)
    mod, r_mod = emit_mod(kb, c_in, adaw, adab, 24, "mod")
    ng_t, r_ng = small_load(kb, ng, [128, 8], "ng_t")
    cw_t, r_cw = small_load(kb, cw, [128, 66], "cw_t")
    cb_t, r_cb = small_load(kb, cb, [128, 22], "cb_t")
    mk_t, r_mk = small_load(kb, msk, [128, 2], "mk_t")
    if final:
        fg_t, r_fg = small_load(kb, fg, [128, 8], "fg_t")
    gs, r_gs = emit_gs(kb, mod, r_mod, 8, ng_t, r_ng, "gs")
    nrm = Norm(kb, W, "nrm").init_eps()

    xt = [kb.sb([128, 8, W], F32, "xt%d" % i) for i in range(2)]
    r_xt = [Res(), Res()]
    tmp = kb.sb([128, W], F32, "tmp")
    r_tmp = Res()
    h = kb.sb([128, 8, W], BF16, "h")
    r_h = Res()
    a = kb.sb([128, 22, W], BF16, "a")
    r_a = Res()
    cbuf = [kb.sb([128, W], F32, "cbuf%d" % i) for i in range(2)]
    r_cbuf = [Res(), Res()]
    sbuf_ = [kb.sb([128, W], F32, "sbuf%d" % i) for i in range(2)]
    r_sbuf = [Res(), Res()]
    xo = kb.sb([128, 8, W], F32, "xo")
    r_xo = Res()
    pg = [kb.ps("pg%d" % i) for i in range(2)]
    pv = [kb.ps("pv%d" % i) for i in range(2)]
    r_pg = [Res(), Res()]
    r_pv = [Res(), Res()]
    po = [kb.ps("po%d" % i) for i in range(2)]
    r_po = [Res(), Res()]

    ntiles = (NT + TW - 1) // TW
    for ti in range(ntiles):
        o0 = ti * TW
        o1 = min(NT, o0 + TW)
        w = o1 - o0 + 2
        s = ti % 2
        x3 = xt[s][:, :, :w]
        kb.dma(x3, xT[:, :, o0:o0 + w], writes=[r_xt[s]])
        rstd, r_rstd = nrm.emit(x3, r_xt[s], w)
        for k in range(8):
            kb.op("dve", lambda e, k=k: e.scalar_tensor_tensor(out=tmp[:, :w], in0=xt[s][:, k, :w], scalar=gs[:, k:k + 1],
                                                                in1=rstd, op0=ALU.mult, op1=ALU.mult),
                  reads=[r_xt[s], r_gs, r_rstd], writes=[r_tmp])
            kb.op("act", lambda e, k=k: e.activation(out=h[:, k, :w], in_=tmp[:, :w], func=AF.Identity,
                                                     bias=mod[:, k:k + 1], scale=1.0),
                  reads=[r_tmp, r_mod], writes=[r_h])
        if ti == 0:
            kb.op("dve", lambda e: e.tensor_scalar(out=h[:, :, 0:1], in0=h[:, :, 0:1], scalar1=mk_t[:, 0:1], scalar2=None, op0=ALU.mult),
                  reads=[r_mk, r_h], writes=[r_h])
        if ti == ntiles - 1:
            kb.op("dve", lambda e: e.tensor_scalar(out=h[:, :, w - 1:w], in0=h[:, :, w - 1:w], scalar1=mk_t[:, 1:2], scalar2=None, op0=ALU.mult),
                  reads=[r_mk, r_h], writes=[r_h])
        for j in range(22):
            q = j % 2
            for k in range(8):
                kb.op("pe", lambda e, k=k, j=j, q=q: e.matmul(pg[q][:, :w], lhsT=wup_t[:, k, j * 128:(j + 1) * 128], rhs=h[:, k, :w],
                                                               start=(k == 0), stop=(k == 7)),
                      reads=[r_wup, r_h], writes=[r_pg[q]])
            for k in range(8):
                kb.op("pe", lambda e, k=k, j=j, q=q: e.matmul(pv[q][:, :w], lhsT=wup_t[:, k, DFF + j * 128:DFF + (j + 1) * 128], rhs=h[:, k, :w],
                                                               start=(k == 0), stop=(k == 7)),
                      reads=[r_wup, r_h], writes=[r_pv[q]])
            cbq = cbuf[q]
            kb.op("act", lambda e, j=j, q=q, cbq=cbq: e.activation(out=cbq[:, 1:w - 1], in_=pg[q][:, 1:w - 1], func=AF.Identity,
                                                                 scale=cw_t[:, 3 * j + 1:3 * j + 2], bias=cb_t[:, j:j + 1]),
                  reads=[r_pg[q], r_cw, r_cb], writes=[r_cbuf[q]])
            kb.op("dve", lambda e, j=j, q=q, cbq=cbq: e.scalar_tensor_tensor(out=cbq[:, 1:w - 1], in0=pg[q][:, 0:w - 2], scalar=cw_t[:, 3 * j:3 * j + 1],
                                                                           in1=cbq[:, 1:w - 1], op0=ALU.mult, op1=ALU.add),
                  reads=[r_pg[q], r_cw, r_cbuf[q]], writes=[r_cbuf[q]])
            kb.op("dve", lambda e, j=j, q=q, cbq=cbq: e.scalar_tensor_tensor(out=cbq[:, 1:w - 1], in0=pg[q][:, 2:w], scalar=cw_t[:, 3 * j + 2:3 * j + 3],
                                                                           in1=cbq[:, 1:w - 1], op0=ALU.mult, op1=ALU.add),
                  reads=[r_pg[q], r_cw, r_cbuf[q]], writes=[r_cbuf[q]])
            sbq = sbuf_[q]
            kb.op("act", lambda e, q=q, cbq=cbq, sbq=sbq: e.activation(out=sbq[:, 1:w - 1], in_=cbq[:, 1:w - 1], func=AF.Silu),
                  reads=[r_cbuf[q]], writes=[r_sbuf[q]])
            kb.op("dve", lambda e, j=j, q=q, sbq=sbq: e.tensor_tensor(out=a[:, j, 1:w - 1], in0=sbq[:, 1:w - 1], in1=pv[q][:, 1:w - 1], op=ALU.mult),
                  reads=[r_sbuf[q], r_pv[q]], writes=[r_a])
        for m in range(8):
            q = m % 2
            for j in range(22):
                kb.op("pe", lambda e, m=m, j=j, q=q: e.matmul(po[q][:, :w - 2], lhsT=wdn_t[:, j, m * 128:(m + 1) * 128], rhs=a[:, j, 1:w - 1],
                                                               start=(j == 0), stop=(j == 21)),
                      reads=[r_wdn, r_a], writes=[r_po[q]])
            kb.op("dve", lambda e, m=m, q=q: e.scalar_tensor_tensor(out=xo[:, m, :w - 2], in0=po[q][:, :w - 2], scalar=mod[:, 16 + m:17 + m],
                                                                    in1=xt[s][:, m, 1:w - 1], op0=ALU.mult, op1=ALU.add),
                  reads=[r_po[q], r_mod, r_xt[s]], writes=[r_xo])
        if final:
            rstd2, r_rstd2 = nrm.emit(xo[:, :, :w - 2], r_xo, w - 2)
            for m in range(8):
                kb.op("dve", lambda e, m=m: e.scalar_tensor_tensor(out=xo[:, m, :w - 2], in0=xo[:, m, :w - 2], scalar=fg_t[:, m:m + 1],
                                                                   in1=rstd2, op0=ALU.mult, op1=ALU.mult),
                      reads=[r_xo, r_fg, r_rstd2], writes=[r_xo])
        kb.dma(out[:, :, o0:o1], xo[:, :, :w - 2], reads=[r_xo], q="sp", is_output=True)
    return kb.finish()


HD = 64
QPERM = [0, 4, 1, 5, 2, 6, 3, 7, 8, 12, 9, 13, 10, 14, 11, 15]


def build_attn(L=8192, NQ=4096, TW=512):
    kb = KB()
    nc = kb.nc
    xall = kb.inp("xall", [128, 8, L])
    xq = kb.inp("xq", [128, 8, NQ])
    c_in = kb.inp("c2", [128, 8, 2])
    adaw = kb.inp("adaw", [1024, 3 * D])
    adab = kb.inp("adab", [128, 24])
    ng = kb.inp("ng", [128, 8])
    wkv = kb.inp("wkv", [D, 768])
    wq = kb.inp("wq", [D, 2048])
    wo = kb.inp("wo", [D, D])
    gains = kb.inp("gains", [128, 4])
    ropek = kb.inp("ropek", [2, 128, L])
    ropeq = kb.inp("ropeq", [2, 128, NQ])
    out = kb.outp("out", [128, 8, NQ])

    kT = kb.sb([128, 2, L], BF16, "kT")
    r_kT = Res()
    NKT = L // 128
    vaug = kb.sb([128, NKT, 4, 65], BF16, "vaug")
    r_v = Res()
    kb.op("pool", lambda e: e.memset(vaug[:, :, :, 64:65], 1.0), writes=[r_v])
    pss = kb.ps("pss")
    r_pss = Res()
    mod, r_mod = emit_mod(kb, c_in, adaw, adab, 24, "mod", ps=pss, r_ps_in=r_pss)
    ng_t, r_ng = small_load(kb, ng, [128, 8], "ng_t")
    gn_t, r_gn = small_load(kb, gains, [128, 4], "gn_t")
    gs, r_gs = emit_gs(kb, mod, r_mod, 8, ng_t, r_ng, "gs")
    nrm = Norm(kb, TW, "nrm").init_eps()
    bones = kb.sb([128, 128], BF16, "bones")
    r_bones = Res()
    kb.op("dve", lambda e: e.memset(bones[:], 0.0), writes=[r_bones])
    kb.op("dve", lambda e: e.memset(bones[0:64, 0:64], 1.0), writes=[r_bones])
    kb.op("dve", lambda e: e.memset(bones[64:128, 64:128], 1.0), writes=[r_bones])
    sel = kb.sb([65, 64], F32, "sel")
    r_sel = Res()
    kb.op("dve", lambda e: e.memset(sel[:], 0.0), writes=[r_sel])
    kb.op("dve", lambda e: e.memset(sel[64:65, :], 1.0), writes=[r_sel])

    xt = kb.sb([128, 8, TW], F32, "xt")
    r_xt = Res()
    h = kb.sb([128, 8, TW], BF16, "h")
    r_h = Res()
    tmp = kb.sb([128, TW], F32, "tmp")
    r_tmp = Res()
    ctab = kb.sb([128, 2, TW], F32, "ctab")
    r_ctab = Res()
    sqh = kb.sb([128, TW], BF16, "sqh")
    r_sqh = Res()
    t1 = kb.sb([128, TW], F32, "t1")
    t2 = kb.sb([128, TW], F32, "t2")
    r_t1, r_t2 = Res(), Res()
    rs = kb.sb([128, TW], F32, "rs")
    r_rs = Res()
    pa = [kb.ps("pa%d" % i) for i in range(2)]
    r_pa = [Res(), Res()]

    def modnorm(src_ap, w):
        kb.dma(xt[:, :, :w], src_ap, writes=[r_xt])
        rstd, r_rstd = nrm.emit(xt[:, :, :w], r_xt, w)
        for k in range(8):
            kb.op("dve", lambda e, k=k: e.scalar_tensor_tensor(out=tmp[:, :w], in0=xt[:, k, :w], scalar=gs[:, k:k + 1],
                                                                in1=rstd, op0=ALU.mult, op1=ALU.mult),
                  reads=[r_xt, r_gs, r_rstd], writes=[r_tmp])
            kb.op("act", lambda e, k=k: e.activation(out=h[:, k, :w], in_=tmp[:, :w], func=AF.Identity,
                                                     bias=mod[:, k:k + 1], scale=1.0),
                  reads=[r_tmp, r_mod], writes=[r_h])

    def proj_rope(wt, r_wt, col, col_sw, gcol, dst_ap, r_dst, w, scale):
        for k in range(8):
            kb.op("pe", lambda e, k=k: e.matmul(pa[0][:, :w], lhsT=wt[:, k, col:col + 128], rhs=h[:, k, :w], start=(k == 0), stop=(k == 7)),
                  reads=[r_wt, r_h], writes=[r_pa[0]])
        for k in range(8):
            kb.op("pe", lambda e, k=k: e.matmul(pa[1][:, :w], lhsT=wt[:, k, col_sw:col_sw + 128], rhs=h[:, k, :w], start=(k == 0), stop=(k == 7)),
                  reads=[r_wt, r_h], writes=[r_pa[1]])
        kb.op("act", lambda e: e.activation(out=sqh[:, :w], in_=pa[0][:, :w], func=AF.Square), reads=[r_pa[0]], writes=[r_sqh])
        kb.op("pe", lambda e: e.matmul(pss[:, :w], lhsT=bones[:], rhs=sqh[:, :w], start=True, stop=True),
              reads=[r_sqh, r_bones], writes=[r_pss])
        kb.op("act", lambda e: e.activation(out=rs[:, :w], in_=pss[:, :w], func=AF.Sqrt, scale=1.0 / HD, bias=nrm.epsb[:]),
              reads=[r_pss, nrm.r_eps], writes=[r_rs])
        kb.op("dve", lambda e: e.reciprocal(out=rs[:, :w], in_=rs[:, :w]), reads=[r_rs], writes=[r_rs])
        kb.op("dve", lambda e: e.scalar_tensor_tensor(out=t1[:, :w], in0=pa[0][:, :w], scalar=gn_t[:, gcol:gcol + 1], in1=ctab[:, 0, :w],
                                                      op0=ALU.mult, op1=ALU.mult), reads=[r_pa[0], r_gn, r_ctab], writes=[r_t1])
        kb.op("dve", lambda e: e.scalar_tensor_tensor(out=t2[:, :w], in0=pa[1][:, :w], scalar=gn_t[:, gcol + 1:gcol + 2], in1=ctab[:, 1, :w],
                                                      op0=ALU.mult, op1=ALU.mult), reads=[r_pa[1], r_gn, r_ctab], writes=[r_t2])
        kb.op("pool", lambda e: e.tensor_tensor(out=t1[:, :w], in0=t1[:, :w], in1=t2[:, :w], op=ALU.add), reads=[r_t1, r_t2], writes=[r_t1])
        if isinstance(dst_ap, tuple):
            for (dap, lo) in dst_ap:
                kb.op("dve", lambda e, dap=dap, lo=lo: e.scalar_tensor_tensor(out=dap, in0=t1[lo:lo + 64, :w], scalar=float(scale), in1=rs[lo:lo + 64, :w],
                                                                              op0=ALU.mult, op1=ALU.mult), reads=[r_t1, r_rs], writes=[r_dst])
        else:
            kb.op("dve", lambda e: e.scalar_tensor_tensor(out=dst_ap, in0=t1[:, :w], scalar=float(scale), in1=rs[:, :w],
                                                          op0=ALU.mult, op1=ALU.mult), reads=[r_t1, r_rs], writes=[r_dst])

    r_wkv = Res()
    g_wkv = nc.sbuf_tensor("wkv_t", [128, 8, 768], BF16)
    wkv_t = g_wkv.__enter__()
    srckv = wkv.rearrange("(k p) n -> p k n", p=128)
    for k in range(8):
        kb.dma(wkv_t[:, k, :], srckv[:, k, :], writes=[r_wkv], q="pool")
    pvp = kb.ps("pvp")
    r_pvp = Res()
    for ti in range(L // TW):
        t0 = ti * TW
        w = TW
        modnorm(xall[:, :, t0:t0 + w], w)
        kb.dma(ctab[:, :, :w], ropek[:, :, t0:t0 + w].rearrange("c p t -> p c t"), writes=[r_ctab])
        for kc in range(2):
            proj_rope(wkv_t, r_wkv, kc * 128, 256 + kc * 128, 2, kT[:, kc, t0:t0 + w], r_kT, w, 1.0)
        for ts in range(w // 128):
            kt = (t0 // 128) + ts
            for k in range(8):
                kb.op("pe", lambda e, k=k, ts=ts: e.matmul(pvp[:, 0:256], lhsT=h[:, k, ts * 128:(ts + 1) * 128], rhs=wkv_t[:, k, 512:768],
                                                           start=(k == 0), stop=(k == 7)), reads=[r_h, r_wkv], writes=[r_pvp])
            kb.op("act", lambda e, kt=kt: e.activation(out=vaug[:, kt, :, 0:64], in_=pvp[:, 0:256].rearrange("p (a b) -> p a b", a=4),
                                                       func=AF.Copy), reads=[r_pvp], writes=[r_v])
    r_free = r_wkv
    g_wkv.__exit__(None, None, None)

    r_wq, r_wo = Res(), Res()
    r_wq.r = list(r_free.r)
    r_wq.w = r_free.w
    r_wo.r = list(r_free.r)
    r_wo.w = r_free.w
    wq_t = kb.sb([128, 8, 2048], BF16, "wq_t")
    srcq = wq.rearrange("(k p) n -> p k n", p=128)
    for k in range(8):
        for c0 in range(0, 2048, 1024):
            kb.dma(wq_t[:, k, c0:c0 + 1024], srcq[:, k, c0:c0 + 1024], writes=[r_wq], q="pool")
    wo_t = kb.sb([128, 8, D], BF16, "wo_t")
    srco = wo.rearrange("(k p) n -> p k n", p=128)
    for k in range(8):
        kb.dma(wo_t[:, k, :], srco[:, k, :], writes=[r_wo], q="pool")
    qT = kb.sb([128, 16, TW], BF16, "qT")
    r_qT = Res()
    kb.op("pool", lambda e: e.memset(qT[:], 0.0), writes=[r_qT])
    oT = kb.sb([128, 8, TW], BF16, "oT")
    r_oT = Res()
    otmp = [kb.sb([64, TW], BF16, "otmp%d" % i) for i in range(2)]
    r_otmp = [Res(), Res()]
    pT = [kb.sb([128, TW], BF16, "pT%d" % i) for i in range(3)]
    r_pT = [Res() for _ in range(3)]
    oacc = kb.sb([65, TW], F32, "oacc")
    r_oacc = Res()
    rec = kb.sb([64, TW], F32, "rec")
    r_rec = Res()
    psc = [kb.ps("psc%d" % i) for i in range(2)] + [pa[1]]
    r_psc = [Res(), Res(), r_pa[1]]
    pso = kb.ps("pso")
    r_pso = Res()
    pmisc = pvp
    r_pmisc = r_pvp

    for qi in range(NQ // TW):
        t0 = qi * TW
        w = TW
        modnorm(xq[:, :, t0:t0 + w], w)
        kb.dma(ctab[:, :, :w], ropeq[:, :, t0:t0 + w].rearrange("c p t -> p c t"), writes=[r_ctab])
        for j in range(8):
            proj_rope(wq_t, r_wq, j * 128, 1024 + j * 128, 0, ((qT[0:64, 2 * j, :w], 0), (qT[64:128, 2 * j + 1, :w], 64)), r_qT, w, 0.125)
        for j in range(8):
            for hb in range(2):
                hq = QPERM[2 * j + hb]
                kvh = hq // 4
                assert kvh % 2 == hb
                kc = kvh // 2
                b0 = 64 * hb
                hidx = 2 * j + hb
                pso_c, r_pso_c = (pso, r_pso) if hidx % 2 == 0 else (pa[0], r_pa[0])

                def qk(kt):
                    sl = kt % 3
                    kb.op("pe", lambda e, kt=kt, sl=sl: e.matmul(psc[sl][:, :w], lhsT=kT[:, kc, kt * 128:(kt + 1) * 128],
                                                                 rhs=qT[:, 2 * j + hb, :w], start=True, stop=True),
                          reads=[r_kT, r_qT], writes=[r_psc[sl]])

                def ex(kt):
                    sl = kt % 3
                    kb.op("act", lambda e, sl=sl: e.activation(out=pT[sl][:, :w], in_=psc[sl][:, :w], func=AF.Exp),
                          reads=[r_psc[sl]], writes=[r_pT[sl]])

                def pv(kt):
                    sl = kt % 3
                    kb.op("pe", lambda e, kt=kt, sl=sl: e.matmul(pso_c[0:65, :w], lhsT=vaug[:, kt, kvh, :], rhs=pT[sl][:, :w],
                                                                 start=(kt == 0), stop=(kt == NKT - 1)),
                          reads=[r_v, r_pT[sl]], writes=[r_pso_c])
                qk(0)
                if NKT > 1:
                    qk(1)
                for kt in range(NKT):
                    ex(kt)
                    if kt + 2 < NKT:
                        qk(kt + 2)
                    pv(kt)
                kb.op("dve", lambda e: e.tensor_copy(out=oacc[:, :w], in_=pso_c[0:65, :w]), reads=[r_pso_c], writes=[r_oacc])
                kb.op("pe", lambda e: e.matmul(pmisc[0:64, :w], lhsT=sel[:], rhs=oacc[:, :w], start=True, stop=True),
                      reads=[r_sel, r_oacc], writes=[r_pmisc])
                kb.op("dve", lambda e: e.reciprocal(out=rec[:, :w], in_=pmisc[0:64, :w]), reads=[r_pmisc], writes=[r_rec])
                if hq % 2 == 0:
                    kb.op("dve", lambda e, hq=hq: e.tensor_tensor(out=oT[0:64, hq // 2, :w], in0=oacc[0:64, :w], in1=rec[:, :w], op=ALU.mult),
                          reads=[r_oacc, r_rec], writes=[r_oT])
                else:
                    osl = (hq // 2) % 2
                    kb.op("dve", lambda e, osl=osl: e.tensor_tensor(out=otmp[osl][:, :w], in0=oacc[0:64, :w], in1=rec[:, :w], op=ALU.mult),
                          reads=[r_oacc, r_rec], writes=[r_otmp[osl]])
                    kb.dma(oT[64:128, hq // 2, :w], otmp[osl][:, :w], reads=[r_otmp[osl]], writes=[r_oT], q="sp")
        for m in range(8):
            for k in range(8):
                kb.op("pe", lambda e, m=m, k=k: e.matmul(pmisc[:, :w], lhsT=wo_t[:, k, m * 128:(m + 1) * 128], rhs=oT[:, k, :w],
                                                         start=(k == 0), stop=(k == 7)), reads=[r_wo, r_oT], writes=[r_pmisc])
            kb.op("dve", lambda e, m=m: e.scalar_tensor_tensor(out=xt[:, m, :w], in0=pmisc[:, :w], scalar=mod[:, 16 + m:17 + m],
                                                               in1=xt[:, m, :w], op0=ALU.mult, op1=ALU.add),
                  reads=[r_pmisc, r_mod, r_xt], writes=[r_xt])
        kb.dma(out[:, :, t0:t0 + w], xt[:, :, :w], reads=[r_xt], q="sp", is_output=True)
    return kb.finish()


TWO_PI = 6.283185307179586


def emit_sincos(kb, ang, r_ang, sin_out, cos_out, r_out, shape, name):
    PI = 3.141592653589793
    MAGIC = 12582912.0
    kf = kb.sb(shape, F32, name + "_kf")
    mk = kb.sb(shape, F32, name + "_mk")
    r2, r3 = Res(), Res()
    kb.op("dve", lambda e: e.tensor_scalar(out=kf[:], in0=ang, scalar1=1.0 / TWO_PI, scalar2=MAGIC, op0=ALU.mult, op1=ALU.add), reads=[r_ang], writes=[r2])
    kb.op("dve", lambda e: e.tensor_scalar(out=kf[:], in0=kf[:], scalar1=-MAGIC, scalar2=None, op0=ALU.add), reads=[r2], writes=[r2])
    kb.op("dve", lambda e: e.scalar_tensor_tensor(out=ang, in0=kf[:], scalar=-TWO_PI, in1=ang, op0=ALU.mult, op1=ALU.add),
          reads=[r2, r_ang], writes=[r_ang])
    kb.op("dve", lambda e: e.tensor_scalar(out=kf[:], in0=ang, scalar1=-PI, scalar2=PI, op0=ALU.max, op1=ALU.min), reads=[r_ang], writes=[r2])
    kb.op("act", lambda e: e.activation(out=sin_out, in_=kf[:], func=AF.Sin), reads=[r2], writes=[r_out])
    kb.op("dve", lambda e: e.tensor_scalar(out=ang, in0=ang, scalar1=PI / 2, scalar2=None, op0=ALU.add), reads=[r_ang, r2], writes=[r_ang])
    kb.op("dve", lambda e: e.tensor_scalar(out=mk[:], in0=ang, scalar1=PI, scalar2=None, op0=ALU.is_gt), reads=[r_ang], writes=[r3])
    kb.op("dve", lambda e: e.scalar_tensor_tensor(out=ang, in0=mk[:], scalar=-TWO_PI, in1=ang, op0=ALU.mult, op1=ALU.add),
          reads=[r3, r_ang], writes=[r_ang])
    kb.op("dve", lambda e: e.tensor_scalar(out=kf[:], in0=ang, scalar1=-PI, scalar2=PI, op0=ALU.max, op1=ALU.min), reads=[r_ang, r_out], writes=[r2])
    kb.op("act", lambda e: e.activation(out=cos_out, in_=kf[:], func=AF.Sin), reads=[r2], writes=[r_out])


def build_s5(L=8192, NB=4, TW=512):
    kb = KB()
    hT = kb.inp("hT", [NB, 128, L])
    are = kb.inp("are", [128, 8])
    aim = kb.inp("aim", [128, 8])
    ldt = kb.inp("ldt", [128, 8])
    bre = kb.inp("bre", [128, 4, 128])
    bim = kb.inp("bim", [128, 4, 128])
    cre = kb.inp("cre", [128, 4, 128])
    cim = kb.inp("cim", [128, 4, 128])
    dsk = kb.inp("dsk", [128, 1])
    tau = kb.inp("tau", [128, TW + 1])
    y = kb.outp("y", [NB, 128, L])
    NG = 4
    are_t, r_are = small_load(kb, are, [128, 8], "are_t")
    aim_t, r_aim = small_load(kb, aim, [128, 8], "aim_t")
    ldt_t, r_ldt = small_load(kb, ldt, [128, 8], "ldt_t")
    bre_t, r_bre = small_load(kb, bre, [128, 4, 128], "bre_t")
    bim_t, r_bim = small_load(kb, bim, [128, 4, 128], "bim_t")
    cre_t, r_cre = small_load(kb, cre, [128, 4, 128], "cre_t")
    cim_t, r_cim = small_load(kb, cim, [128, 4, 128], "cim_t")
    dsk_t, r_dsk = small_load(kb, dsk, [128, 1], "dsk_t")
    tau_t, r_tau = small_load(kb, tau, [128, TW + 1], "tau_t")
    P = {}
    rP = Res()

    def sm(name):
        P[name] = kb.sb([128, 8], F32, "p_" + name)
        return P[name]
    for n in ("lre", "dt", "a", "th", "r", "s1", "c1", "lbr1", "lbi", "den", "fr", "fi", "t"):
        sm(n)
    V = lambda n: P[n][:]
    rr_all = [r_are, r_aim, r_ldt, rP]
    kb.op("dve", lambda e: e.tensor_scalar(out=V("lre"), in0=are_t[:], scalar1=-1e-4, scalar2=None, op0=ALU.min), reads=rr_all, writes=[rP])
    kb.op("act", lambda e: e.activation(out=V("dt"), in_=ldt_t[:], func=AF.Exp), reads=rr_all, writes=[rP])
    kb.op("dve", lambda e: e.tensor_tensor(out=V("a"), in0=V("lre"), in1=V("dt"), op=ALU.mult), reads=rr_all, writes=[rP])
    kb.op("dve", lambda e: e.tensor_tensor(out=V("th"), in0=aim_t[:], in1=V("dt"), op=ALU.mult), reads=rr_all, writes=[rP])
    kb.op("act", lambda e: e.activation(out=V("r"), in_=V("a"), func=AF.Exp), reads=rr_all, writes=[rP])
    kb.op("dve", lambda e: e.tensor_copy(out=V("t"), in_=V("th")), reads=rr_all, writes=[rP])
    emit_sincos(kb, V("t"), rP, V("s1"), V("c1"), rP, [128, 8], "sc0")
    kb.op("dve", lambda e: e.tensor_tensor(out=V("lbr1"), in0=V("r"), in1=V("c1"), op=ALU.mult), reads=[rP], writes=[rP])
    kb.op("dve", lambda e: e.tensor_scalar(out=V("lbr1"), in0=V("lbr1"), scalar1=-1.0, scalar2=None, op0=ALU.add), reads=[rP], writes=[rP])
    kb.op("dve", lambda e: e.tensor_tensor(out=V("lbi"), in0=V("r"), in1=V("s1"), op=ALU.mult), reads=[rP], writes=[rP])
    kb.op("dve", lambda e: e.tensor_tensor(out=V("den"), in0=V("lre"), in1=V("lre"), op=ALU.mult), reads=[rP], writes=[rP])
    kb.op("dve", lambda e: e.tensor_tensor(out=V("t"), in0=aim_t[:], in1=aim_t[:], op=ALU.mult), reads=[rP, r_aim], writes=[rP])
    kb.op("dve", lambda e: e.tensor_tensor(out=V("den"), in0=V("den"), in1=V("t"), op=ALU.add), reads=[rP], writes=[rP])
    kb.op("dve", lambda e: e.reciprocal(out=V("den"), in_=V("den")), reads=[rP], writes=[rP])
    kb.op("dve", lambda e: e.tensor_tensor(out=V("fr"), in0=V("lbr1"), in1=V("lre"), op=ALU.mult), reads=[rP], writes=[rP])
    kb.op("dve", lambda e: e.tensor_tensor(out=V("t"), in0=V("lbi"), in1=aim_t[:], op=ALU.mult), reads=[rP], writes=[rP])
    kb.op("dve", lambda e: e.tensor_tensor(out=V("fr"), in0=V("fr"), in1=V("t"), op=ALU.add), reads=[rP], writes=[rP])
    kb.op("dve", lambda e: e.tensor_tensor(out=V("fr"), in0=V("fr"), in1=V("den"), op=ALU.mult), reads=[rP], writes=[rP])
    kb.op("dve", lambda e: e.tensor_tensor(out=V("fi"), in0=V("lbi"), in1=V("lre"), op=ALU.mult), reads=[rP], writes=[rP])
    kb.op("dve", lambda e: e.tensor_tensor(out=V("t"), in0=V("lbr1"), in1=aim_t[:], op=ALU.mult), reads=[rP], writes=[rP])
    kb.op("dve", lambda e: e.tensor_tensor(out=V("fi"), in0=V("fi"), in1=V("t"), op=ALU.subtract), reads=[rP], writes=[rP])
    kb.op("dve", lambda e: e.tensor_tensor(out=V("fi"), in0=V("fi"), in1=V("den"), op=ALU.mult), reads=[rP], writes=[rP])
    crp = kb.sb([128, 4, 128], F32, "crp")
    ncip = kb.sb([128, 4, 128], F32, "ncip")
    ctmp = kb.sb([128, 128], F32, "ctmp")
    r_cp = Res()
    for j in range(NG):
        kb.op("dve", lambda e, j=j: e.tensor_scalar(out=ctmp[:], in0=cim_t[:, j, :], scalar1=P["fi"][:, j:j + 1], scalar2=None, op0=ALU.mult),
              reads=[rP, r_cim, r_cp], writes=[r_cp])
        kb.op("dve", lambda e, j=j: e.scalar_tensor_tensor(out=crp[:, j, :], in0=cre_t[:, j, :], scalar=P["fr"][:, j:j + 1], in1=ctmp[:],
                                                           op0=ALU.mult, op1=ALU.subtract), reads=[rP, r_cre, r_cp], writes=[r_cp])
        kb.op("dve", lambda e, j=j: e.tensor_scalar(out=ctmp[:], in0=cim_t[:, j, :], scalar1=P["fr"][:, j:j + 1], scalar2=None, op0=ALU.mult),
              reads=[rP, r_cim, r_cp], writes=[r_cp])
        kb.op("dve", lambda e, j=j: e.scalar_tensor_tensor(out=ncip[:, j, :], in0=cre_t[:, j, :], scalar=P["fi"][:, j:j + 1], in1=ctmp[:],
                                                           op0=ALU.mult, op1=ALU.add), reads=[rP, r_cre, r_cp], writes=[r_cp])
        kb.op("dve", lambda e, j=j: e.tensor_scalar(out=ncip[:, j, :], in0=ncip[:, j, :], scalar1=-1.0, scalar2=None, op0=ALU.mult),
              reads=[r_cp], writes=[r_cp])
    ctab = kb.sb([128, NG, TW + 1], F32, "s5ctab")
    stab = kb.sb([128, NG, TW + 1], F32, "s5stab")
    rtab = kb.sb([128, NG, TW], F32, "s5rtab")
    angt = kb.sb([128, TW + 1], F32, "angt")
    r_tab, r_angt = Res(), Res()
    for j in range(NG):
        kb.op("dve", lambda e, j=j: e.tensor_scalar(out=angt[:], in0=tau_t[:], scalar1=P["th"][:, j:j + 1], scalar2=None, op0=ALU.mult),
              reads=[rP, r_tau, r_angt], writes=[r_angt])
        emit_sincos(kb, angt[:], r_angt, stab[:, j, :], ctab[:, j, :], r_tab, [128, TW + 1], "sc%d" % (j + 1))
        kb.op("dve", lambda e, j=j: e.tensor_scalar(out=rtab[:, j, :], in0=tau_t[:, 0:TW], scalar1=0.0, scalar2=P["r"][:, j:j + 1],
                                                    op0=ALU.mult, op1=ALU.add), reads=[rP, r_tau], writes=[r_tab])
    ut = [kb.sb([128, TW], F32, "ut%d" % i) for i in range(2)]
    r_ut = [Res(), Res()]
    pbr = [kb.ps("pbr%d" % i) for i in range(2)]
    pbi = [kb.ps("pbi%d" % i) for i in range(2)]
    r_pbr = [Res(), Res()]
    r_pbi = [Res(), Res()]
    py = [kb.ps("py%d" % i) for i in range(2)]
    r_py = [Res(), Res()]
    names = ["bur", "bui", "m1", "m2", "m3", "m4", "wr", "wi", "zr", "zi", "xr", "xi"]
    T = {n: [kb.sb([128, TW], F32, "s5_%s%d" % (n, i)) for i in range(2)] for n in names}
    R = {n: [Res(), Res()] for n in names}
    carry = kb.sb([128, NG, 4], F32, "carry")
    r_carry = [Res() for _ in range(NG)]
    yo = [kb.sb([128, TW], F32, "yo%d" % i) for i in range(2)]
    r_yo = [Res(), Res()]
    it = 0
    for b in range(NB):
        for ck in range(L // TW):
            us = ck % 2
            kb.dma(ut[us][:], hT[b, :, ck * TW:(ck + 1) * TW], writes=[r_ut[us]])
            for gp in range(NG):
                s = it % 2
                it += 1
                g = lambda n: T[n][s][:]
                rg = lambda n: R[n][s]
                kb.op("pe", lambda e: e.matmul(pbr[s][:], lhsT=bre_t[:, gp, :], rhs=ut[us][:], start=True, stop=True),
                      reads=[r_bre, r_ut[us]], writes=[r_pbr[s]])
                kb.op("pe", lambda e: e.matmul(pbi[s][:], lhsT=bim_t[:, gp, :], rhs=ut[us][:], start=True, stop=True),
                      reads=[r_bim, r_ut[us]], writes=[r_pbi[s]])
                kb.op("act", lambda e: e.activation(out=g("bur"), in_=pbr[s][:], func=AF.Copy), reads=[r_pbr[s]], writes=[rg("bur")])
                kb.op("act", lambda e: e.activation(out=g("bui"), in_=pbi[s][:], func=AF.Copy), reads=[r_pbi[s]], writes=[rg("bui")])
                cc = ctab[:, gp, 0:TW]
                ss = stab[:, gp, 0:TW]
                kb.op("dve", lambda e: e.tensor_tensor(out=g("m1"), in0=g("bur"), in1=cc, op=ALU.mult), reads=[rg("bur"), r_tab], writes=[rg("m1")])
                kb.op("dve", lambda e: e.tensor_tensor(out=g("m2"), in0=g("bui"), in1=ss, op=ALU.mult), reads=[rg("bui"), r_tab], writes=[rg("m2")])
                kb.op("dve", lambda e: e.tensor_tensor(out=g("wr"), in0=g("m1"), in1=g("m2"), op=ALU.add), reads=[rg("m1"), rg("m2")], writes=[rg("wr")])
                kb.op("pool", lambda e: e.tensor_tensor(out=g("m3"), in0=g("bui"), in1=cc, op=ALU.mult), reads=[rg("bui"), r_tab], writes=[rg("m3")])
                kb.op("pool", lambda e: e.tensor_tensor(out=g("m4"), in0=g("bur"), in1=ss, op=ALU.mult), reads=[rg("bur"), r_tab], writes=[rg("m4")])
                kb.op("pool", lambda e: e.tensor_tensor(out=g("wi"), in0=g("m3"), in1=g("m4"), op=ALU.subtract), reads=[rg("m3"), rg("m4")], writes=[rg("wi")])
                if ck == 0:
                    ini_r, ini_i = 0.0, 0.0
                else:
                    ini_r, ini_i = carry[:, gp, 2:3], carry[:, gp, 3:4]
                kb.op("dve", lambda e: e.tensor_tensor_scan(out=g("zr"), data0=rtab[:, gp, :], data1=g("wr"), initial=ini_r, op0=ALU.mult, op1=ALU.add),
                      reads=[rg("wr"), r_tab, r_carry[gp]], writes=[rg("zr")])
                kb.op("dve", lambda e: e.tensor_tensor_scan(out=g("zi"), data0=rtab[:, gp, :], data1=g("wi"), initial=ini_i, op0=ALU.mult, op1=ALU.add),
                      reads=[rg("wi"), r_tab, r_carry[gp]], writes=[rg("zi")])
                cT = ctab[:, gp, TW:TW + 1]
                sT = stab[:, gp, TW:TW + 1]
                kb.op("dve", lambda e: e.tensor_tensor(out=carry[:, gp, 0:1], in0=T["zi"][s][:, TW - 1:TW], in1=sT, op=ALU.mult),
                      reads=[rg("zi"), r_tab, r_carry[gp]], writes=[r_carry[gp]])
                kb.op("dve", lambda e: e.scalar_tensor_tensor(out=carry[:, gp, 2:3], in0=T["zr"][s][:, TW - 1:TW], scalar=cT, in1=carry[:, gp, 0:1],
                                                              op0=ALU.mult, op1=ALU.subtract), reads=[rg("zr"), r_tab, r_carry[gp]], writes=[r_carry[gp]])
                kb.op("dve", lambda e: e.tensor_tensor(out=carry[:, gp, 1:2], in0=T["zi"][s][:, TW - 1:TW], in1=cT, op=ALU.mult),
                      reads=[rg("zi"), r_tab, r_carry[gp]], writes=[r_carry[gp]])
                kb.op("dve", lambda e: e.scalar_tensor_tensor(out=carry[:, gp, 3:4], in0=T["zr"][s][:, TW - 1:TW], scalar=sT, in1=carry[:, gp, 1:2],
                                                              op0=ALU.mult, op1=ALU.add), reads=[rg("zr"), r_tab, r_carry[gp]], writes=[r_carry[gp]])
                kb.op("dve", lambda e: e.tensor_tensor(out=g("m1"), in0=g("zr"), in1=cc, op=ALU.mult), reads=[rg("zr"), r_tab, rg("wr")], writes=[rg("m1")])
                kb.op("dve", lambda e: e.tensor_tensor(out=g("m2"), in0=g("zi"), in1=ss, op=ALU.mult), reads=[rg("zi"), r_tab, rg("wr")], writes=[rg("m2")])
                kb.op("dve", lambda e: e.tensor_tensor(out=g("xr"), in0=g("m1"), in1=g("m2"), op=ALU.subtract), reads=[rg("m1"), rg("m2")], writes=[rg("xr")])
                kb.op("pool", lambda e: e.tensor_tensor(out=g("m3"), in0=g("zr"), in1=ss, op=ALU.mult), reads=[rg("zr"), r_tab, rg("wi")], writes=[rg("m3")])
                kb.op("pool", lambda e: e.tensor_tensor(out=g("m4"), in0=g("zi"), in1=cc, op=ALU.mult), reads=[rg("zi"), r_tab, rg("wi")], writes=[rg("m4")])
                kb.op("pool", lambda e: e.tensor_tensor(out=g("xi"), in0=g("m3"), in1=g("m4"), op=ALU.add), reads=[rg("m3"), rg("m4")], writes=[rg("xi")])
                kb.op("pe", lambda e: e.matmul(py[us][:], lhsT=crp[:, gp, :], rhs=g("xr"), start=(gp == 0), stop=False),
                      reads=[r_cp, rg("xr")], writes=[r_py[us]])
                kb.op("pe", lambda e: e.matmul(py[us][:], lhsT=ncip[:, gp, :], rhs=g("xi"), start=False, stop=(gp == NG - 1)),
                      reads=[r_cp, rg("xi")], writes=[r_py[us]])
            kb.op("dve", lambda e: e.scalar_tensor_tensor(out=yo[us][:], in0=ut[us][:], scalar=dsk_t[:, 0:1], in1=py[us][:],
                                                          op0=ALU.mult, op1=ALU.add), reads=[r_ut[us], r_dsk, r_py[us]], writes=[r_yo[us]])
            kb.dma(y[b, :, ck * TW:(ck + 1) * TW], yo[us][:], reads=[r_yo[us]], q="sp", is_output=True)
    return kb.finish()


NFFT = 16384
MAGIC = 12582912.0
PI = 3.141592653589793


def build_hy2(NCB=16):
    kb = KB()
    u3 = kb.inp("u3", [3, 2, 64, 128, 2, 128])
    zext = kb.inp("zext", [33, NFFT])
    w1 = kb.inp("w1", [33, 64])
    w2 = kb.inp("w2", [64, 64])
    bf1 = kb.inp("bf1", [64, 3])
    w3 = kb.inp("w3", [64, 4, 128])
    decf = kb.inp("decf", [128, 128, 128])
    decb = kb.inp("decb", [128, 128, 128])
    skp = kb.inp("skp", [64, 2, 128])
    ftab = kb.inp("ftab", [128, 4, 256])
    fri = kb.inp("fri", [128, 2, 128])
    tw = kb.inp("tw", [128, 2, 128])
    zout = kb.outp("zout", [2, 64, 128, 2, 128])

    w1_t, r_w1 = small_load(kb, w1, [33, 64], "w1_t")
    w2_t, r_w2 = small_load(kb, w2, [64, 64], "w2_t")
    bf_t, r_bf = small_load(kb, bf1, [64, 3], "bf_t")
    sk_t, r_sk = small_load(kb, skp, [64, 2, 128], "sk_t")
    ft_t, r_ft = small_load(kb, ftab, [128, 4, 256], "ft_t")
    fr_t, r_fr = small_load(kb, fri, [128, 2, 128], "fr_t")
    tw_t, r_tw = small_load(kb, tw, [128, 2, 128], "tw_t")
    w3_t = kb.sb([64, 4, 128], BF16, "w3_t")
    r_w3 = Res()
    kb.dma(w3_t[:], w3, writes=[r_w3], q="pool")
    onesf = kb.sb([128, 128], F32, "onesf")
    r_ones = Res()
    kb.op("dve", lambda e: e.memset(onesf[:], 1.0), writes=[r_ones])

    ps1 = kb.ps("ps1", (128, 2048))
    ps2 = kb.ps("ps2", (128, 2048))
    r_ps1, r_ps2 = Res(), Res()

    a2T = kb.sb([64, NFFT], BF16, "a2T")
    r_a2 = Res()
    zc = [kb.sb([33, 512], F32, "zc%d" % i) for i in range(2)]
    r_zc = [Res(), Res()]
    arg = kb.sb([64, 512], F32, "arg")
    kf = kb.sb([64, 512], F32, "kfm")
    a1c = kb.sb([64, 512], F32, "a1c")
    r_arg, r_kf, r_a1c = Res(), Res(), Res()

    def sin_rr(src_ps, r_src, bcol, out_ap, r_out):
        kb.op("dve", lambda e: e.tensor_scalar(out=arg[:], in0=src_ps, scalar1=bf_t[:, bcol:bcol + 1], scalar2=bf_t[:, 2:3], op0=ALU.add, op1=ALU.mult),
              reads=[r_src, r_bf], writes=[r_arg])
        kb.op("dve", lambda e: e.tensor_scalar(out=kf[:], in0=arg[:], scalar1=1.0 / TWO_PI, scalar2=MAGIC, op0=ALU.mult, op1=ALU.add), reads=[r_arg], writes=[r_kf])
        kb.op("dve", lambda e: e.tensor_scalar(out=kf[:], in0=kf[:], scalar1=-MAGIC, scalar2=None, op0=ALU.add), reads=[r_kf], writes=[r_kf])
        kb.op("dve", lambda e: e.scalar_tensor_tensor(out=arg[:], in0=kf[:], scalar=-TWO_PI, in1=arg[:], op0=ALU.mult, op1=ALU.add),
              reads=[r_kf, r_arg], writes=[r_arg])
        kb.op("dve", lambda e: e.tensor_scalar(out=arg[:], in0=arg[:], scalar1=-PI, scalar2=PI, op0=ALU.max, op1=ALU.min), reads=[r_arg], writes=[r_arg])
        kb.op("act", lambda e: e.activation(out=out_ap, in_=arg[:], func=AF.Sin), reads=[r_arg], writes=[r_out])

    for ck in range(NFFT // 512):
        s = ck % 2
        kb.dma(zc[s][:], zext[:, ck * 512:(ck + 1) * 512], writes=[r_zc[s]])
        kb.op("pe", lambda e: e.matmul(ps1[0:64, 0:512], lhsT=w1_t[:], rhs=zc[s][:], start=True, stop=True), reads=[r_w1, r_zc[s]], writes=[r_ps1])
        sin_rr(ps1[0:64, 0:512], r_ps1, 0, a1c[:], r_a1c)
        kb.op("pe", lambda e: e.matmul(ps2[0:64, 0:512], lhsT=w2_t[:], rhs=a1c[:], start=True, stop=True), reads=[r_w2, r_a1c], writes=[r_ps2])
        sin_rr(ps2[0:64, 0:512], r_ps2, 1, a2T[:, ck * 512:(ck + 1) * 512], r_a2)

    dft = kb.sb([128, 8, 128], F32, "dft")
    dbt = kb.sb([128, 8, 128], F32, "dbt")
    r_dft, r_dbt = Res(), Res()
    kft = kb.sb([128, 2, 8, 128], F32, "kft")
    r_kft = Res()
    ktmp = kb.sb([128, 8, 64], F32, "ktmp")
    r_ktmp = Res()
    part = kb.sb([128, 16], F32, "part")
    rS = kb.sb([128, 16], F32, "rS")
    r_part, r_rS = Res(), Res()
    KR = kb.sb([128, 2, 8, 128], F32, "KR")
    KI = kb.sb([128, 2, 8, 128], F32, "KI")
    r_K = Res()
    A1 = kb.sb([128, 8, 2, 128], F32, "A1")
    A2 = kb.sb([128, 8, 2, 128], F32, "A2")
    r_A = Res()
    P1 = kb.sb([128, 8, 2, 128], F32, "P1")
    P2 = kb.sb([128, 8, 2, 128], F32, "P2")
    r_P = Res()
    M = [kb.sb([128, 8, 128], F32, "mm%d" % i) for i in range(4)]
    r_M = Res()
    X = kb.sb([64, 8, 2, 128], F32, "X")
    G = kb.sb([64, 8, 2, 128], F32, "G")
    Z1 = kb.sb([64, 8, 2, 128], F32, "Z1")
    Z2 = kb.sb([64, 8, 2, 128], F32, "Z2")
    T_ = kb.sb([64, 8, 2, 128], F32, "Tt")
    r_X, r_G, r_Z1, r_Z2, r_T = Res(), Res(), Res(), Res(), Res()
    twr_b = tw_t[:, 0, :].unsqueeze(1).to_broadcast([128, 8, 128])
    twi_b = tw_t[:, 1, :].unsqueeze(1).to_broadcast([128, 8, 128])

    def cplx_evac(ps, r_ps, tr, ti, r_t, conj, O1, O2, r_O, arr):
        v = ps.rearrange("k (a c n) -> k a c n", a=8, c=2)
        pre, pim = v[:, :, 0, :], v[:, :, 1, :]
        rd = [r_ps, r_t]
        kb.op("dve", lambda e: e.tensor_tensor(out=M[0][:], in0=pre, in1=tr, op=ALU.mult), reads=rd, writes=[r_M])
        kb.op("dve", lambda e: e.tensor_tensor(out=M[1][:], in0=pim, in1=ti, op=ALU.mult), reads=rd, writes=[r_M])
        kb.op("dve", lambda e: e.tensor_tensor(out=M[2][:], in0=pre, in1=ti, op=ALU.mult), reads=rd, writes=[r_M])
        kb.op("dve", lambda e: e.tensor_tensor(out=M[3][:], in0=pim, in1=tr, op=ALU.mult), reads=rd, writes=[r_M])
        if not conj:
            kb.op("pool", lambda e: e.tensor_tensor(out=O1[:, :, 0, :], in0=M[0][:], in1=M[1][:], op=ALU.subtract), reads=[r_M], writes=[r_O])
            kb.op("pool", lambda e: e.tensor_tensor(out=O1[:, :, 1, :], in0=M[2][:], in1=M[3][:], op=ALU.add), reads=[r_M], writes=[r_O])
        else:
            kb.op("pool", lambda e: e.tensor_tensor(out=O1[:, :, 0, :], in0=M[0][:], in1=M[1][:], op=ALU.add), reads=[r_M], writes=[r_O])
            kb.op("pool", lambda e: e.tensor_tensor(out=O1[:, :, 1, :], in0=M[3][:], in1=M[2][:], op=ALU.subtract), reads=[r_M], writes=[r_O])
        if arr == "fwd":
            kb.op("act", lambda e: e.activation(out=O2[:, :, 0, :], in_=O1[:, :, 1, :], func=AF.Copy, scale=-1.0), reads=[r_O], writes=[r_O])
            kb.op("act", lambda e: e.activation(out=O2[:, :, 1, :], in_=O1[:, :, 0, :], func=AF.Copy), reads=[r_O], writes=[r_O])
        else:
            kb.op("act", lambda e: e.activation(out=O2[:, :, 0, :], in_=O1[:, :, 1, :], func=AF.Copy), reads=[r_O], writes=[r_O])
            kb.op("act", lambda e: e.activation(out=O2[:, :, 1, :], in_=O1[:, :, 0, :], func=AF.Copy, scale=-1.0), reads=[r_O], writes=[r_O])

    def stage2(I1, I2, r_I, M_rows):
        f1 = I1.rearrange("k a c n -> k (a c n)")
        f2 = I2.rearrange("k a c n -> k (a c n)")
        for q in range(4):
            kb.op("pe", lambda e, q=q: e.matmul(ps2[0:M_rows, q * 512:(q + 1) * 512], lhsT=fr_t[:, 0, 0:M_rows], rhs=f1[:, q * 512:(q + 1) * 512],
                                                start=True, stop=False), reads=[r_fr, r_I], writes=[r_ps2])
            kb.op("pe", lambda e, q=q: e.matmul(ps2[0:M_rows, q * 512:(q + 1) * 512], lhsT=fr_t[:, 1, 0:M_rows], rhs=f2[:, q * 512:(q + 1) * 512],
                                                start=False, stop=True), reads=[r_fr, r_I], writes=[r_ps2])

    def conv_pass(src, r_src, o):
        for c in range(8):
            kb.op("pe", lambda e, c=c: e.matmul(ps1[:, c * 256:(c + 1) * 256], lhsT=src[:, c, 0, :], rhs=ft_t[0:64, 0, :], start=True, stop=False),
                  reads=[r_src, r_ft], writes=[r_ps1])
            kb.op("pe", lambda e, c=c: e.matmul(ps1[:, c * 256:(c + 1) * 256], lhsT=src[:, c, 1, :], rhs=ft_t[0:64, 1, :], start=False, stop=True),
                  reads=[r_src, r_ft], writes=[r_ps1])
        cplx_evac(ps1, r_ps1, twr_b, twi_b, r_tw, False, A1, A2, r_A, "fwd")
        stage2(A1, A2, r_A, 128)
        cplx_evac(ps2, r_ps2, KR[:, o], KI[:, o], r_K, False, P1, P2, r_P, "inv")
        for c in range(8):
            kb.op("pe", lambda e, c=c: e.matmul(ps1[:, c * 256:(c + 1) * 256], lhsT=P1[:, c, 0, :], rhs=ft_t[:, 2, :], start=True, stop=False),
                  reads=[r_P, r_ft], writes=[r_ps1])
            kb.op("pe", lambda e, c=c: e.matmul(ps1[:, c * 256:(c + 1) * 256], lhsT=P1[:, c, 1, :], rhs=ft_t[:, 3, :], start=False, stop=True),
                  reads=[r_P, r_ft], writes=[r_ps1])
        cplx_evac(ps1, r_ps1, twr_b, twi_b, r_tw, True, A1, A2, r_A, "inv")
        stage2(A1, A2, r_A, 64)

    for cb in range(NCB):
        ch0 = 8 * cb
        kb.dma(dft[:], decf[:, ch0:ch0 + 8, :], writes=[r_dft])
        kb.dma(dbt[:], decb[:, ch0:ch0 + 8, :], writes=[r_dbt])
        for p in range(128):
            psx, r_psx = (ps1, r_ps1) if p < 64 else (ps2, r_ps2)
            pp = p % 64
            kb.op("pe", lambda e, p=p, pp=pp, psx=psx: e.matmul(psx[:, pp * 32:(pp + 1) * 32], lhsT=a2T[:, p:NFFT:128], rhs=w3_t[:, :, ch0:ch0 + 8],
                                                              start=True, stop=True), reads=[r_a2, r_w3], writes=[r_psx])
        for o in range(2):
            for half in range(2):
                psx, r_psx = (ps1, r_ps1) if half == 0 else (ps2, r_ps2)
                vw = psx.rearrange("i (p q c) -> i q c p", q=4, c=8)
                hs = slice(half * 64, half * 64 + 64)
                kb.op("dve", lambda e, vw=vw, hs=hs, o=o: e.tensor_tensor(out=kft[:, o, :, hs], in0=vw[:, 2 * o], in1=dft[:, :, hs], op=ALU.mult),
                      reads=[r_psx, r_dft], writes=[r_kft])
                kb.op("dve", lambda e, vw=vw, hs=hs, o=o: e.tensor_tensor(out=ktmp[:], in0=vw[:, 2 * o + 1], in1=dbt[:, :, hs], op=ALU.mult),
                      reads=[r_psx, r_dbt], writes=[r_ktmp])
                kb.op("pool", lambda e, hs=hs, o=o: e.tensor_tensor(out=kft[:, o, :, hs], in0=kft[:, o, :, hs], in1=ktmp[:], op=ALU.add),
                      reads=[r_ktmp, r_kft], writes=[r_kft])
        kb.op("dve", lambda e: e.tensor_reduce(out=part[:], in_=kft[:].rearrange("i o c p -> i (o c) p"), axis=AX.X, op=ALU.add, apply_absolute_value=True),
              reads=[r_kft], writes=[r_part])
        kb.op("pe", lambda e: e.matmul(ps1[:, 0:16], lhsT=onesf[:], rhs=part[:], start=True, stop=True), reads=[r_ones, r_part], writes=[r_ps1])
        kb.op("dve", lambda e: e.tensor_scalar(out=rS[:], in0=ps1[:, 0:16], scalar1=float(NFFT), scalar2=None, op0=ALU.mult), reads=[r_ps1], writes=[r_rS])
        kb.op("dve", lambda e: e.reciprocal(out=rS[:], in_=rS[:]), reads=[r_rS], writes=[r_rS])
        for o in range(2):
            for c in range(8):
                kb.op("pe", lambda e, c=c, o=o: e.matmul(ps1[:, c * 256:(c + 1) * 256], lhsT=kft[:, o, c, :], rhs=ft_t[:, 0, :], start=True, stop=True),
                      reads=[r_kft, r_ft], writes=[r_ps1])
            cplx_evac(ps1, r_ps1, twr_b, twi_b, r_tw, False, A1, A2, r_A, "fwd")
            stage2(A1, A2, r_A, 128)
            v2 = ps2.rearrange("k (a c n) -> k a c n", a=8, c=2)
            rsb = rS[:, o * 8:(o + 1) * 8].unsqueeze(2).to_broadcast([128, 8, 128])
            kb.op("dve", lambda e, o=o, v2=v2, rsb=rsb: e.tensor_tensor(out=KR[:, o], in0=v2[:, :, 0, :], in1=rsb, op=ALU.mult),
                  reads=[r_ps2, r_rS], writes=[r_K])
            kb.op("dve", lambda e, o=o, v2=v2, rsb=rsb: e.tensor_tensor(out=KI[:, o], in0=v2[:, :, 1, :], in1=rsb, op=ALU.mult),
                  reads=[r_ps2, r_rS], writes=[r_K])
        for pr in range(2):
            kb.dma(X[:], u3[0, pr, :, ch0:ch0 + 8, :, :], writes=[r_X])
            kb.dma(G[:], u3[1, pr, :, ch0:ch0 + 8, :, :], writes=[r_G])
            conv_pass(X, r_X, 0)
            sk0 = sk_t[:, 0, ch0:ch0 + 8].unsqueeze(2).to_broadcast([64, 8, 256])
            sk1 = sk_t[:, 1, ch0:ch0 + 8].unsqueeze(2).to_broadcast([64, 8, 256])
            fl = lambda t: t[:].rearrange("i a c n -> i a (c n)")
            yv = ps2[0:64, :].rearrange("i (a n) -> i a n", a=8)
            kb.op("dve", lambda e: e.tensor_tensor(out=fl(T_), in0=fl(X), in1=sk0, op=ALU.mult), reads=[r_X, r_sk], writes=[r_T])
            kb.op("dve", lambda e: e.tensor_tensor(out=fl(T_), in0=fl(T_), in1=yv, op=ALU.add), reads=[r_T, r_ps2], writes=[r_T])
            kb.op("pool", lambda e: e.tensor_tensor(out=fl(Z1), in0=fl(T_), in1=fl(G), op=ALU.mult), reads=[r_T, r_G], writes=[r_Z1])
            kb.dma(G[:], u3[2, pr, :, ch0:ch0 + 8, :, :], writes=[r_G])
            conv_pass(Z1, r_Z1, 1)
            kb.op("dve", lambda e: e.tensor_tensor(out=fl(T_), in0=fl(Z1), in1=sk1, op=ALU.mult), reads=[r_Z1, r_sk], writes=[r_T])
            kb.op("dve", lambda e: e.tensor_tensor(out=fl(T_), in0=fl(T_), in1=yv, op=ALU.add), reads=[r_T, r_ps2], writes=[r_T])
            kb.op("pool", lambda e: e.tensor_tensor(out=fl(Z2), in0=fl(T_), in1=fl(G), op=ALU.mult), reads=[r_T, r_G], writes=[r_Z2])
            kb.dma(zout[pr, :, ch0:ch0 + 8, :, :], Z2[:], reads=[r_Z2], q="sp", is_output=True)
    return kb.finish()


def build_tok(kind, NT):
    kb = KB()
    halo = 2 if kind == "hy1" else 0
    TW = 256 if kind == "hy1" else 512
    W = TW + halo
    xT = kb.inp("xT", [128, 8, NT + halo])
    c_in = kb.inp("c2", [128, 8, 2])
    adaw = kb.inp("adaw", [1024, 3 * D])
    adab = kb.inp("adab", [128, 24])
    pss = kb.ps("pmod")
    r_pss = Res()
    mod, r_mod = emit_mod(kb, c_in, adaw, adab, 24, "mod", ps=pss, r_ps_in=r_pss)
    xt = [kb.sb([128, 8, W], F32, "xt%d" % i) for i in range(2)]
    r_xt = [Res(), Res()]
    tmp = kb.sb([128, W], F32, "tmp")
    r_tmp = Res()
    pa = [kb.ps("pa%d" % i) for i in range(2)]
    pb = [kb.ps("pb%d" % i) for i in range(2)]
    r_pa = [Res(), Res()]
    r_pb = [Res(), Res()]
    if kind in ("hy1", "s5a"):
        ng = kb.inp("ng", [128, 8])
        ng_t, r_ng = small_load(kb, ng, [128, 8], "ng_t")
        gs, r_gs = emit_gs(kb, mod, r_mod, 8, ng_t, r_ng, "gs")
        nrm = Norm(kb, W, "nrm").init_eps()
    if kind == "hy1":
        win = kb.inp("win", [D, 3 * D])
        cw = kb.inp("cw", [128, 72])
        cb = kb.inp("cb", [128, 24])
        msk = kb.inp("msk", [128, 2])
        out = kb.outp("out", [128, 24, NT])
        r_w = Res()
        w_t = load_cast_weight(kb, win, 8, 3 * D, "w_t", r_w)
        cw_t, r_cw = small_load(kb, cw, [128, 72], "cw_t")
        cb_t, r_cb = small_load(kb, cb, [128, 24], "cb_t")
        mk_t, r_mk = small_load(kb, msk, [128, 2], "mk_t")
        h = kb.sb([128, 8, W], BF16, "h")
        r_h = Res()
        uo = kb.sb([128, 24, TW], F32, "uo")
        r_uo = Res()
    elif kind == "s5a":
        out = kb.outp("out", [128, 8, NT])
        ho = kb.sb([128, 8, W], F32, "ho")
        r_ho = Res()
    elif kind == "hypost":
        zin = kb.inp("z", [128, 8, NT])
        wout = kb.inp("wout", [D, D])
        out = kb.outp("out", [128, 8, NT])
        r_w = Res()
        w_t = load_cast_weight(kb, wout, 8, D, "w_t", r_w)
        zt = kb.sb([128, 8, W], F32, "zt")
        zb = kb.sb([128, 8, W], BF16, "zb")
        r_zt, r_zb = Res(), Res()
        xo = kb.sb([128, 8, W], F32, "xo")
        r_xo = Res()
    elif kind == "s5post":
        yf = kb.inp("yf", [128, 8, NT])
        yb = kb.inp("yb", [128, 8, NT])
        wglu = kb.inp("wglu", [D, 2 * D])
        out = kb.outp("out", [128, 8, NT])
        r_w = Res()
        w_t = load_cast_weight(kb, wglu, 8, 2 * D, "w_t", r_w)
        y1 = kb.sb([128, 8, W], F32, "y1")
        y2 = kb.sb([128, 8, W], F32, "y2")
        r_y1, r_y2 = Res(), Res()
        t3 = kb.sb([128, 8, W], F32, "t3")
        r_t3 = Res()
        gl = kb.sb([128, 8, W], BF16, "gl")
        r_gl = Res()
        sg = kb.sb([128, W], F32, "sg")
        r_sg = Res()
        xo = kb.sb([128, 8, W], F32, "xo")
        r_xo = Res()
    ntiles = (NT + TW - 1) // TW
    for ti in range(ntiles):
        o0 = ti * TW
        o1 = min(NT, o0 + TW)
        w = o1 - o0 + halo
        wo_ = o1 - o0
        s = ti % 2
        kb.dma(xt[s][:, :, :w], xT[:, :, o0:o0 + w], writes=[r_xt[s]])
        if kind in ("hy1", "s5a"):
            rstd, r_rstd = nrm.emit(xt[s][:, :, :w], r_xt[s], w)
            for k in range(8):
                kb.op("dve", lambda e, k=k: e.scalar_tensor_tensor(out=tmp[:, :w], in0=xt[s][:, k, :w], scalar=gs[:, k:k + 1],
                                                                    in1=rstd, op0=ALU.mult, op1=ALU.mult),
                      reads=[r_xt[s], r_gs, r_rstd], writes=[r_tmp])
                if kind == "hy1":
                    kb.op("act", lambda e, k=k: e.activation(out=h[:, k, :w], in_=tmp[:, :w], func=AF.Identity, bias=mod[:, k:k + 1], scale=1.0),
                          reads=[r_tmp, r_mod], writes=[r_h])
                else:
                    kb.op("act", lambda e, k=k: e.activation(out=ho[:, k, :w], in_=tmp[:, :w], func=AF.Identity, bias=mod[:, k:k + 1], scale=1.0),
                          reads=[r_tmp, r_mod], writes=[r_ho])
        if kind == "s5a":
            kb.dma(out[:, :, o0:o1], ho[:, :, :w], reads=[r_ho], q="sp", is_output=True)
        elif kind == "hy1":
            if ti == 0:
                kb.op("dve", lambda e: e.tensor_scalar(out=h[:, :, 0:1], in0=h[:, :, 0:1], scalar1=mk_t[:, 0:1], scalar2=None, op0=ALU.mult),
                      reads=[r_mk, r_h], writes=[r_h])
            if ti == ntiles - 1:
                kb.op("dve", lambda e: e.tensor_scalar(out=h[:, :, w - 1:w], in0=h[:, :, w - 1:w], scalar1=mk_t[:, 1:2], scalar2=None, op0=ALU.mult),
                      reads=[r_mk, r_h], writes=[r_h])
            for j in range(24):
                q = j % 2
                for k in range(8):
                    kb.op("pe", lambda e, k=k, j=j, q=q: e.matmul(pa[q][:, :w], lhsT=w_t[:, k, j * 128:(j + 1) * 128], rhs=h[:, k, :w],
                                                                   start=(k == 0), stop=(k == 7)), reads=[r_w, r_h], writes=[r_pa[q]])
                kb.op("act", lambda e, j=j, q=q: e.activation(out=uo[:, j, :wo_], in_=pa[q][:, 1:w - 1], func=AF.Identity,
                                                              scale=cw_t[:, 3 * j + 1:3 * j + 2], bias=cb_t[:, j:j + 1]),
                      reads=[r_pa[q], r_cw, r_cb], writes=[r_uo])
                kb.op("dve", lambda e, j=j, q=q: e.scalar_tensor_tensor(out=uo[:, j, :wo_], in0=pa[q][:, 0:w - 2], scalar=cw_t[:, 3 * j:3 * j + 1],
                                                                        in1=uo[:, j, :wo_], op0=ALU.mult, op1=ALU.add),
                      reads=[r_pa[q], r_cw, r_uo], writes=[r_uo])
                kb.op("dve", lambda e, j=j, q=q: e.scalar_tensor_tensor(out=uo[:, j, :wo_], in0=pa[q][:, 2:w], scalar=cw_t[:, 3 * j + 2:3 * j + 3],
                                                                        in1=uo[:, j, :wo_], op0=ALU.mult, op1=ALU.add),
                      reads=[r_pa[q], r_cw, r_uo], writes=[r_uo])
            kb.dma(out[:, :, o0:o1], uo[:, :, :wo_], reads=[r_uo], q="sp", is_output=True)
        elif kind == "hypost":
            kb.dma(zt[:, :, :w], zin[:, :, o0:o1], writes=[r_zt])
            kb.op("act", lambda e: e.activation(out=zb[:, :, :w], in_=zt[:, :, :w], func=AF.Copy), reads=[r_zt], writes=[r_zb])
            for m in range(8):
                q = m % 2
                for k in range(8):
                    kb.op("pe", lambda e, k=k, m=m, q=q: e.matmul(pa[q][:, :w], lhsT=w_t[:, k, m * 128:(m + 1) * 128], rhs=zb[:, k, :w],
                                                                   start=(k == 0), stop=(k == 7)), reads=[r_w, r_zb], writes=[r_pa[q]])
                kb.op("dve", lambda e, m=m, q=q: e.scalar_tensor_tensor(out=xo[:, m, :w], in0=pa[q][:, :w], scalar=mod[:, 16 + m:17 + m],
                                                                        in1=xt[s][:, m, :w], op0=ALU.mult, op1=ALU.add),
                      reads=[r_pa[q], r_mod, r_xt[s]], writes=[r_xo])
            kb.dma(out[:, :, o0:o1], xo[:, :, :w], reads=[r_xo], q="sp", is_output=True)
        elif kind == "s5post":
            kb.dma(y1[:, :, :w], yf[:, :, o0:o1], writes=[r_y1])
            kb.dma(y2[:, :, :w], yb[:, :, o0:o1], writes=[r_y2])
            a3 = lambda t: t[:, :, :w]
            kb.op("dve", lambda e: e.tensor_tensor(out=a3(y1), in0=a3(y1), in1=a3(y2), op=ALU.add), reads=[r_y1, r_y2], writes=[r_y1])
            kb.op("pool", lambda e: e.tensor_tensor(out=a3(t3), in0=a3(y1), in1=a3(y1), op=ALU.mult), reads=[r_y1], writes=[r_t3])
            kb.op("dve", lambda e: e.tensor_scalar(out=a3(t3), in0=a3(t3), scalar1=0.044715, scalar2=1.0, op0=ALU.mult, op1=ALU.add), reads=[r_t3], writes=[r_t3])
            kb.op("pool", lambda e: e.tensor_tensor(out=a3(t3), in0=a3(t3), in1=a3(y1), op=ALU.mult), reads=[r_y1, r_t3], writes=[r_t3])
            kb.op("act", lambda e: e.activation(out=a3(t3), in_=a3(t3), func=AF.Sigmoid, scale=1.5957691216057308), reads=[r_t3], writes=[r_t3])
            kb.op("dve", lambda e: e.tensor_tensor(out=a3(gl), in0=a3(t3), in1=a3(y1), op=ALU.mult), reads=[r_y1, r_t3], writes=[r_gl])
            for m in range(8):
                q = m % 2
                for k in range(8):
                    kb.op("pe", lambda e, k=k, m=m, q=q: e.matmul(pa[q][:, :w], lhsT=w_t[:, k, m * 128:(m + 1) * 128], rhs=gl[:, k, :w],
                                                                   start=(k == 0), stop=(k == 7)), reads=[r_w, r_gl], writes=[r_pa[q]])
                for k in range(8):
                    kb.op("pe", lambda e, k=k, m=m, q=q: e.matmul(pb[q][:, :w], lhsT=w_t[:, k, D + m * 128:D + (m + 1) * 128], rhs=gl[:, k, :w],
                                                                   start=(k == 0), stop=(k == 7)), reads=[r_w, r_gl], writes=[r_pb[q]])
                kb.op("act", lambda e, q=q: e.activation(out=sg[:, :w], in_=pb[q][:, :w], func=AF.Sigmoid), reads=[r_pb[q]], writes=[r_sg])
                kb.op("dve", lambda e, q=q: e.tensor_tensor(out=sg[:, :w], in0=sg[:, :w], in1=pa[q][:, :w], op=ALU.mult), reads=[r_sg, r_pa[q]], writes=[r_sg])
                kb.op("dve", lambda e, m=m: e.scalar_tensor_tensor(out=xo[:, m, :w], in0=sg[:, :w], scalar=mod[:, 16 + m:17 + m],
                                                                   in1=xt[s][:, m, :w], op0=ALU.mult, op1=ALU.add),
                      reads=[r_sg, r_mod, r_xt[s]], writes=[r_xo])
            kb.dma(out[:, :, o0:o1], xo[:, :, :w], reads=[r_xo], q="sp", is_output=True)
    return kb.finish()


import math
import numpy as np
L = 8192; NFFT = 16384; Dm = 1024

def hy_consts():
    t = np.linspace(0.0, 1.0, L, dtype=np.float32)[:, None]
    bands = 16
    w = (2.0 * math.pi * np.arange(L, dtype=np.float32) / L).astype(np.float32)
    f = np.linspace(1e-4, bands - 1, bands, dtype=np.float32)
    ang = w[:, None] * f[None, :]
    z = np.concatenate([t, np.cos(ang), -np.sin(ang)], -1).astype(np.float32)
    deltas = np.abs(np.linspace(math.log(1e-2) / 1.5, math.log(1e-2) / 0.3, Dm, dtype=np.float32))
    decay = np.exp(-t * deltas[None, :]).astype(np.float32)
    idx = np.arange(NFFT)
    src = np.where(idx < L, idx, np.where(idx == L, 0, 2 * L - idx))
    zext = np.ascontiguousarray(z[src].T)
    dec_f = np.where((idx < L)[:, None], decay[src], 0.0).astype(np.float32)
    dec_b = np.where((idx > L)[:, None], decay[src], 0.0).astype(np.float32)
    k = np.arange(128)
    F = np.exp(-2j * np.pi * np.outer(k, k) / 128)
    Fr = F.real.astype(np.float32); Fi = F.imag.astype(np.float32)
    ftab = np.stack([np.concatenate([Fr, Fi], 1), np.concatenate([-Fi, Fr], 1),
                     np.concatenate([Fr, -Fi], 1), np.concatenate([Fi, Fr], 1)], 1)
    fri = np.stack([Fr, Fi], 1)
    T = np.exp(-2j * np.pi * np.outer(k, k) / NFFT)
    tw = np.stack([T.real.astype(np.float32), T.imag.astype(np.float32)], 1)
    return dict(z=z, decay=decay, zext=zext, dec_f=dec_f, dec_b=dec_b, ftab=np.ascontiguousarray(ftab.astype(np.float32)),
                fri=np.ascontiguousarray(fri), tw=np.ascontiguousarray(tw))

def dec_core(dec, core):
    return np.ascontiguousarray(dec[:, 128 * core:128 * core + 128].reshape(128, 128, 128).transpose(0, 2, 1))

def to_u3(u, core):
    a = u.reshape(2, 2, 64, 128, 3, Dm)[..., 128 * core:128 * core + 128]
    return np.ascontiguousarray(a.transpose(4, 0, 2, 5, 1, 3))

def from_zout(zs):
    out = np.zeros((4, L, Dm), np.float32)
    for core, zc in enumerate(zs):
        a = zc.transpose(0, 3, 1, 4, 2)
        out[:, :, 128 * core:128 * core + 128] = a.reshape(4, L, 128)
    return out


_PROGS = {}


def _prog(key, fn):
    if key not in _PROGS:
        _PROGS[key] = fn()
    return _PROGS[key]


def _fm(x):
    T, C = x.shape
    return np.ascontiguousarray(x.T.reshape(C // 128, 128, T).transpose(1, 0, 2))


def _unfm(a):
    return np.ascontiguousarray(a.transpose(2, 1, 0).reshape(a.shape[2], -1))


def _vfm(v):
    return np.ascontiguousarray(np.asarray(v, np.float32).reshape(-1, 128).T)


def _run(nc, in_maps):
    res = run_bass_kernel_spmd(nc, in_maps, core_ids=list(range(8)))
    return res.results


NTC = 4096
SEQ = 8192
NBATCH = 4
SWAP = np.arange(64) ^ 1


def _rope_fm(Ls):
    rows = Ls // 64
    nf = 16
    inv = (1.0 / (np.float32(10000.0) ** (np.arange(nf, dtype=np.float32) / np.float32(nf)))).astype(np.float32)
    r = np.arange(rows, dtype=np.float32)
    col = np.arange(64, dtype=np.float32)
    ang_r = np.broadcast_to(r[:, None, None] * inv, (rows, 64, nf))
    ang_c = np.broadcast_to(col[None, :, None] * inv, (rows, 64, nf))
    ang = np.concatenate([ang_r, ang_c], -1).reshape(Ls, 2 * nf).astype(np.float32)
    cos, sin = np.cos(ang), np.sin(ang)
    C = np.repeat(cos, 2, axis=1).T
    S = np.repeat(sin, 2, axis=1).T.copy()
    S[0::2] *= -1
    return np.ascontiguousarray(np.stack([np.concatenate([C, C], 0), np.concatenate([S, S], 0)], 0).astype(np.float32))


def _common(c, adaw, adab, b):
    return {"c2": np.ascontiguousarray(np.repeat(_vfm(c[b])[:, :, None], 2, axis=2)), "adaw": np.ascontiguousarray(adaw), "adab": _vfm(adab)}


def _halo_x(xs, b, half):
    xp = np.pad(xs[b], ((1, 1), (0, 0)))
    return _fm(xp[half * NTC: half * NTC + NTC + 2])


def _msk(half):
    return np.tile(np.array([[0.0 if half == 0 else 1.0, 1.0 if half == 0 else 0.0]], np.float32), (128, 1))


def _gather_tok(results, key="out"):
    xs = np.zeros((NBATCH, SEQ, D), np.float32)
    for core in range(8):
        b, half = core // 2, core % 2
        xs[b, half * NTC:(half + 1) * NTC] = _unfm(results[core][key])
    return xs


def run_attn(xs, c, adaw, adab, ng, w_qkv, w_o, qg, kg):
    nc = _prog("attn", lambda: build_attn(L=SEQ, NQ=NTC))
    wq_ = w_qkv[:, :1024].reshape(D, 16, 64)
    wk_ = w_qkv[:, 1024:1280].reshape(D, 4, 64)
    wv_ = w_qkv[:, 1280:]
    wq_p = wq_[:, QPERM, :]
    wq_all = np.ascontiguousarray(np.concatenate([wq_p.reshape(D, 1024), wq_p[:, :, SWAP].reshape(D, 1024)], 1))
    wkv = np.ascontiguousarray(np.concatenate([wk_.reshape(D, 256), wk_[:, :, SWAP].reshape(D, 256), wv_], 1))
    wo = np.ascontiguousarray(w_o)
    gains = np.ascontiguousarray(np.stack([np.tile(qg, 2), np.tile(qg[SWAP], 2), np.tile(kg, 2), np.tile(kg[SWAP], 2)], 1).astype(np.float32))
    rp = _rope_fm(SEQ)
    maps = []
    for core in range(8):
        b, half = core // 2, core % 2
        xf = _fm(xs[b])
        m = _common(c, adaw, adab, b)
        m.update({"xall": xf, "xq": np.ascontiguousarray(xf[:, :, half * NTC:(half + 1) * NTC]), "ng": _vfm(ng), "wkv": wkv, "wq": wq_all,
                  "wo": wo, "gains": gains, "ropek": rp, "ropeq": np.ascontiguousarray(rp[:, :, half * NTC:(half + 1) * NTC])})
        maps.append(m)
    return _gather_tok(_run(nc, maps))


def run_ffn(xs, c, adaw, adab, ng, wup, cw, cb, wdn, final_g=None):
    final = final_g is not None
    nc = _prog("ffn%d" % final, lambda: build_ffn(NTC, final=final))
    cwl = np.ascontiguousarray(cw.T.reshape(22, 128, 3).transpose(1, 0, 2).reshape(128, 66))
    maps = []
    for core in range(8):
        b, half = core // 2, core % 2
        m = _common(c, adaw, adab, b)
        m.update({"xT": _halo_x(xs, b, half), "ng": _vfm(ng), "wup": np.ascontiguousarray(wup), "wdn": np.ascontiguousarray(wdn),
                  "cw": cwl, "cb": _vfm(cb), "msk": _msk(half)})
        if final:
            m["fg"] = _vfm(final_g)
        maps.append(m)
    return _gather_tok(_run(nc, maps))


def run_hyena(xs, c, adaw, adab, ng, w_in, conv_w, conv_b, f_w1, f_b1, f_w2, f_b2, f_w3, f_freq, skip, w_out):
    nc1 = _prog("hy1", lambda: build_tok("hy1", NTC))
    cwl = np.ascontiguousarray(conv_w.T.reshape(24, 128, 3).transpose(1, 0, 2).reshape(128, 72))
    maps = []
    for core in range(8):
        b, half = core // 2, core % 2
        m = _common(c, adaw, adab, b)
        m.update({"xT": _halo_x(xs, b, half), "ng": _vfm(ng), "win": np.ascontiguousarray(w_in), "cw": cwl, "cb": _vfm(conv_b), "msk": _msk(half)})
        maps.append(m)
    u = _gather_tok_c(_run(nc1, maps), 3 * D)
    nc2 = _prog("hy2", lambda: build_hy2(NCB=16))
    C = hy_consts()
    maps = []
    for core in range(8):
        sl = slice(128 * core, 128 * core + 128)
        maps.append(dict(u3=to_u3(u, core), zext=C["zext"], w1=np.ascontiguousarray(f_w1), w2=np.ascontiguousarray(f_w2),
                         bf1=np.ascontiguousarray(np.stack([f_b1, f_b2, f_freq], 1)),
                         w3=np.ascontiguousarray(f_w3.reshape(64, 4, D)[:, :, sl]), decf=dec_core(C["dec_f"], core), decb=dec_core(C["dec_b"], core),
                         skp=np.ascontiguousarray(np.tile(skip[None, :, sl], (64, 1, 1))), ftab=C["ftab"], fri=C["fri"], tw=C["tw"]))
    r = _run(nc2, maps)
    z = from_zout([r[cc]["zout"] for cc in range(8)])
    nc3 = _prog("hypost", lambda: build_tok("hypost", NTC))
    maps = []
    for core in range(8):
        b, half = core // 2, core % 2
        sl = slice(half * NTC, (half + 1) * NTC)
        m = _common(c, adaw, adab, b)
        m.update({"xT": _fm(xs[b, sl]), "z": _fm(z[b, sl]), "wout": np.ascontiguousarray(w_out)})
        maps.append(m)
    return _gather_tok(_run(nc3, maps))


def _gather_tok_c(results, C_):
    o = np.zeros((NBATCH, SEQ, C_), np.float32)
    for core in range(8):
        b, half = core // 2, core % 2
        o[b, half * NTC:(half + 1) * NTC] = _unfm(results[core]["out"])
    return o


def _s5_params(A_re, A_im, log_dt, B_re, B_im, C_re, C_im, core, TW=512):
    gs = [8 * core + k for k in range(8)]
    are = np.zeros((128, 8), np.float32); aim = np.zeros((128, 8), np.float32); ldt = np.zeros((128, 8), np.float32)
    bre = np.zeros((128, 4, 128), np.float32); bim = np.zeros((128, 4, 128), np.float32)
    cre = np.zeros((128, 4, 128), np.float32); cim = np.zeros((128, 4, 128), np.float32)
    for gp in range(4):
        for g2 in range(2):
            g = gs[2 * gp + g2]
            sl = slice(64 * g2, 64 * g2 + 64)
            are[sl, gp] = A_re[g]; aim[sl, gp] = A_im[g]; ldt[sl, gp] = log_dt[g]
            rows = slice(16 * (2 * gp + g2), 16 * (2 * gp + g2) + 16)
            bre[rows, gp, sl] = B_re[g].T
            bim[rows, gp, sl] = B_im[g].T
            cre[sl, gp, rows] = C_re[g].T
            cim[sl, gp, rows] = C_im[g].T
    are[:, 4:] = are[:, :4]; aim[:, 4:] = aim[:, :4]; ldt[:, 4:] = ldt[:, :4]
    tau = np.tile(np.arange(TW + 1, dtype=np.float32)[None], (128, 1))
    return dict(are=are, aim=aim, ldt=ldt, bre=bre, bim=bim, cre=cre, cim=cim, tau=tau)


def run_s5(xs, c, adaw, adab, ng, A_re, A_im, log_dt, B_re, B_im, C_re, C_im, d_skip, w_glu):
    nc1 = _prog("s5a", lambda: build_tok("s5a", NTC))
    maps = []
    for core in range(8):
        b, half = core // 2, core % 2
        m = _common(c, adaw, adab, b)
        m.update({"xT": _fm(xs[b, half * NTC:(half + 1) * NTC]), "ng": _vfm(ng)})
        maps.append(m)
    h = _gather_tok(_run(nc1, maps))
    nc2 = _prog("s5", lambda: build_s5(L=SEQ, NB=NBATCH))
    ys = []
    for d in range(2):
        hd = h if d == 0 else h[:, ::-1]
        maps = []
        for core in range(8):
            m = _s5_params(A_re[d], A_im[d], log_dt[d], B_re[d], B_im[d], C_re[d], C_im[d], core)
            m["hT"] = np.ascontiguousarray(hd[:, :, 128 * core:128 * core + 128].transpose(0, 2, 1))
            dk = d_skip[128 * core:128 * core + 128, None] if d == 0 else np.zeros((128, 1), np.float32)
            m["dsk"] = np.ascontiguousarray(dk.astype(np.float32))
            maps.append(m)
        r = _run(nc2, maps)
        yd = np.concatenate([r[cc]["y"] for cc in range(8)], axis=1).transpose(0, 2, 1)
        ys.append(yd if d == 0 else yd[:, ::-1])
    nc3 = _prog("s5post", lambda: build_tok("s5post", NTC))
    maps = []
    for core in range(8):
        b, half = core // 2, core % 2
        sl = slice(half * NTC, (half + 1) * NTC)
        m = _common(c, adaw, adab, b)
        m.update({"xT": _fm(xs[b, sl]), "yf": _fm(ys[0][b, sl]), "yb": _fm(ys[1][b, sl]), "wglu": np.ascontiguousarray(w_glu)})
        maps.append(m)
    return _gather_tok(_run(nc3, maps))


def kernel(x, c, ada_w, ada_b, norm1_g, norm2_g, final_g,
           attn_w_qkv, attn_w_o, attn_q_gain, attn_k_gain,
           hy_w_in, hy_conv_w, hy_conv_b, hy_f_w1, hy_f_b1, hy_f_w2, hy_f_b2, hy_f_w3, hy_f_freq, hy_skip, hy_w_out,
           s5_A_re, s5_A_im, s5_log_dt, s5_B_re, s5_B_im, s5_C_re, s5_C_im, s5_D, s5_w_glu,
           ffn_w_up, ffn_conv_w, ffn_conv_b, ffn_w_down):
    A = lambda v: np.asarray(v, dtype=np.float32)
    xs = A(x)
    c = A(c)
    ada_w, ada_b = A(ada_w), A(ada_b)
    for i in range(4):
        m, j = i % 3, i // 3
        aw1, ab1 = ada_w[i][:, :3 * D], ada_b[i][:3 * D]
        aw2, ab2 = ada_w[i][:, 3 * D:], ada_b[i][3 * D:]
        if m == 0:
            xs = run_attn(xs, c, aw1, ab1, A(norm1_g)[i], A(attn_w_qkv)[j], A(attn_w_o)[j], A(attn_q_gain)[j], A(attn_k_gain)[j])
        elif m == 1:
            xs = run_hyena(xs, c, aw1, ab1, A(norm1_g)[i], A(hy_w_in)[j], A(hy_conv_w)[j], A(hy_conv_b)[j], A(hy_f_w1)[j], A(hy_f_b1)[j],
                           A(hy_f_w2)[j], A(hy_f_b2)[j], A(hy_f_w3)[j], A(hy_f_freq)[j], A(hy_skip)[j], A(hy_w_out)[j])
        else:
            xs = run_s5(xs, c, aw1, ab1, A(norm1_g)[i], A(s5_A_re)[j], A(s5_A_im)[j], A(s5_log_dt)[j], A(s5_B_re)[j], A(s5_B_im)[j],
                        A(s5_C_re)[j], A(s5_C_im)[j], A(s5_D)[j], A(s5_w_glu)[j])
        xs = run_ffn(xs, c, aw2, ab2, A(norm2_g)[i], A(ffn_w_up)[i], A(ffn_conv_w)[i], A(ffn_conv_b)[i], A(ffn_w_down)[i],
                     final_g=A(final_g) if i == 3 else None)
    return xs.astype(np.float32)
```

```python
import numpy as np
import ml_dtypes
import concourse.bass as bass
import concourse.mybir as mybir
from concourse.bass_utils import run_bass_kernel_spmd

F32 = mybir.dt.float32
BF16 = mybir.dt.bfloat16
I32 = mybir.dt.int32
AF = mybir.ActivationFunctionType
ALU = mybir.AluOpType
AX = mybir.AxisListType

D = 1024
DFF = 2816
EPS = 1e-6


class Res:
    __slots__ = ("w", "r")

    def __init__(self):
        self.w = None
        self.r = []


class KB:
    ENG = ("pe", "act", "dve", "pool", "sp")

    def __init__(self, n_dma_sems=24):
        nc = bass.Bass("TRN2", target_bir_lowering=False)
        self.nc = nc
        self.e = {"pe": nc.tensor, "act": nc.scalar, "dve": nc.vector, "pool": nc.gpsimd, "sp": nc.sync}
        self.sem = {}
        self.cnt = {}
        for k in self.ENG:
            self.sem[k] = nc.semaphore("s_" + k).__enter__()
            self.cnt[k] = 0
        self.dsem = []
        for i in range(n_dma_sems):
            key = "d%d" % i
            self.sem[key] = nc.semaphore("s_" + key).__enter__()
            self.cnt[key] = 0
            self.dsem.append(key)
        self.dnext = 0
        self.waited = {k: {} for k in self.ENG}
        self.out_events = []
        self.n_inst = 0
        self._names = 0

    def sb(self, shape, dt, name=None):
        self._names += 1
        return self.nc.sbuf_tensor(name or ("t%d" % self._names), list(shape), dt).__enter__()

    def ps(self, name=None, shape=(128, 512), dt=F32):
        self._names += 1
        return self.nc.psum_tensor(name or ("p%d" % self._names), list(shape), dt).__enter__()

    def dram(self, name, shape, dt, kind="Internal"):
        return self.nc.dram_tensor(name, list(shape), dt, kind=kind).ap()

    def inp(self, name, shape, dt=F32):
        return self.nc.dram_tensor(name, list(shape), dt, kind="ExternalInput").ap()

    def outp(self, name, shape, dt=F32):
        return self.nc.dram_tensor(name, list(shape), dt, kind="ExternalOutput").ap()

    def _wait(self, eng, ev):
        if ev is None:
            return
        key, val = ev
        if key == eng and eng == "pe":
            return
        if self.waited[eng].get(key, 0) >= val:
            return
        self.e[eng].wait_ge(self.sem[key], val)
        self.waited[eng][key] = val

    def _deps(self, eng, reads, writes):
        for r in reads:
            self._wait(eng, r.w)
        for w in writes:
            self._wait(eng, w.w)
            for ev in w.r:
                if ev[0] == eng:
                    continue
                self._wait(eng, ev)

    def _commit(self, ev, reads, writes):
        for r in reads:
            r.r.append(ev)
            if len(r.r) > 64:
                best = {}
                for k, v in r.r:
                    if best.get(k, 0) < v:
                        best[k] = v
                r.r = list(best.items())
        for w in writes:
            w.w = ev
            w.r = []

    def op(self, eng, fn, reads=(), writes=()):
        self._deps(eng, reads, writes)
        inst = fn(self.e[eng])
        self.cnt[eng] += 1
        inst.then_inc(self.sem[eng], 1)
        ev = (eng, self.cnt[eng])
        self._commit(ev, reads, writes)
        self.n_inst += 1
        return ev

    def dma(self, out, in_, reads=(), writes=(), q="sp", is_output=False, **kw):
        key = self.dsem[self.dnext]
        self.dnext = (self.dnext + 1) % len(self.dsem)
        if self.cnt[key] > 0:
            self._wait(q, (key, self.cnt[key]))
        self._deps(q, reads, writes)
        inst = self.e[q].dma_start(out=out, in_=in_, **kw)
        self.cnt[key] += 16
        inst.then_inc(self.sem[key], 16)
        ev = (key, self.cnt[key])
        self._commit(ev, reads, writes)
        if is_output:
            self.out_events.append(ev)
        self.n_inst += 1
        return ev

    def finish(self):
        for ev in self.out_events:
            self._wait("sp", ev)
        return self.nc


def bf(x):
    return np.asarray(x, dtype=np.float32).astype(ml_dtypes.bfloat16).astype(np.float32)


def load_cast_weight(kb, w_ap, kchunks, ncols, name, res, colblk=1024):
    wt = kb.sb([128, kchunks, ncols], BF16, name)
    src = w_ap.rearrange("(k p) n -> p k n", p=128)
    for k in range(kchunks):
        for c0 in range(0, ncols, colblk):
            c1 = min(ncols, c0 + colblk)
            kb.dma(wt[:, k, c0:c1], src[:, k, c0:c1], writes=[res], q="pool")
    return wt


def emit_mod(kb, c_ap, adaw_ap, adab_ap, nch, name, ps=None, r_ps_in=None):
    r_c, r_w, r_ps, r_mod, r_b = Res(), Res(), Res(), Res(), Res()
    ct = kb.sb([128, 8, 2], F32, name + "_c")
    sg = kb.sb([128, 8, 2], F32, name + "_sg")
    ca = kb.sb([128, 8, 2], F32, name + "_ca")
    bt = kb.sb([128, nch], F32, name + "_b")
    mod = kb.sb([128, nch], F32, name)
    kb.dma(ct[:], c_ap, writes=[r_c])
    kb.dma(bt[:], adab_ap, writes=[r_b])
    kb.op("act", lambda e: e.activation(out=sg[:], in_=ct[:], func=AF.Sigmoid), reads=[r_c], writes=[r_mod])
    kb.op("dve", lambda e: e.tensor_tensor(out=ca[:], in0=ct[:], in1=sg[:], op=ALU.mult), reads=[r_c, r_mod], writes=[r_ps])
    r_ca = r_ps
    src = adaw_ap.rearrange("(k p) n -> p k n", p=128)
    wts = [kb.sb([128, 8, 128], F32, name + "_w%d" % i) for i in range(2)]
    r_wts = [Res(), Res()]
    ps1 = ps if ps is not None else kb.ps(name + "_ps")
    r_ps1 = r_ps_in if r_ps_in is not None else Res()
    for j in range(nch):
        s = j % 2
        kb.dma(wts[s][:], src[:, :, j * 128:(j + 1) * 128], writes=[r_wts[s]])
        for k in range(8):
            kb.op("pe", lambda e, k=k, s=s, j=j: e.matmul(ps1[:, 2 * j:2 * j + 2], lhsT=wts[s][:, k, :], rhs=ca[:, k, :],
                                                      start=(k == 0), stop=(k == 7)),
                  reads=[r_wts[s], r_ca], writes=[r_ps1])
    kb.op("dve", lambda e: e.tensor_tensor(out=mod[:], in0=ps1[:, 0:2 * nch:2], in1=bt[:], op=ALU.add),
          reads=[r_ps1, r_b], writes=[r_mod])
    return mod, r_mod


def small_load(kb, ap, shape, name, dt=F32):
    t = kb.sb(shape, dt, name)
    r = Res()
    kb.dma(t[:], ap, writes=[r])
    return t, r


class Norm:
    def __init__(self, kb, W, name):
        self.kb = kb
        self.ones = kb.sb([128, 128], BF16, name + "_ones")
        self.r_ones = Res()
        kb.op("dve", lambda e: e.memset(self.ones[:], 1.0), writes=[self.r_ones])
        self.sq = kb.sb([128, 8, W], BF16, name + "_sq")
        self.r_sq = Res()
        self.ps = kb.ps(name + "_ps")
        self.r_ps = Res()
        self.sd = kb.sb([128, W], F32, name + "_sd")
        self.rstd = kb.sb([128, W], F32, name + "_rstd")
        self.r_sd = Res()
        self.r_rstd = Res()

    def emit(self, x3, r_x, w):
        kb = self
        kb = self.kb
        kb.op("act", lambda e: e.activation(out=self.sq[:, :, :w], in_=x3, func=AF.Square), reads=[r_x], writes=[self.r_sq])
        for k in range(8):
            kb.op("pe", lambda e, k=k: e.matmul(self.ps[:, :w], lhsT=self.ones[:], rhs=self.sq[:, k, :w], start=(k == 0), stop=(k == 7)),
                  reads=[self.r_sq, self.r_ones], writes=[self.r_ps])
        kb.op("act", lambda e: e.activation(out=self.sd[:, :w], in_=self.ps[:, :w], func=AF.Sqrt, scale=1.0 / D, bias=self.epsb[:]),
              reads=[self.r_ps, self.r_eps], writes=[self.r_sd])
        kb.op("dve", lambda e: e.reciprocal(out=self.rstd[:, :w], in_=self.sd[:, :w]), reads=[self.r_sd], writes=[self.r_rstd])
        return self.rstd[:, :w], self.r_rstd

    def init_eps(self):
        kb = self.kb
        self.epsb = kb.sb([128, 1], F32, "epsb%d" % id(self))
        self.r_eps = Res()
        kb.op("dve", lambda e: e.memset(self.epsb[:], EPS), writes=[self.r_eps])
        return self


def emit_gs(kb, mod, r_mod, sc_off, g_t, r_g, name):
    gs = kb.sb([128, 8], F32, name)
    r = Res()
    kb.op("dve", lambda e: e.scalar_tensor_tensor(out=gs[:], in0=mod[:, sc_off:sc_off + 8], scalar=1.0, in1=g_t[:],
                                                  op0=ALU.add, op1=ALU.mult), reads=[r_mod, r_g], writes=[r])
    return gs, r


def build_ffn(NT, final=False, TW=256):
    kb = KB()
    xT = kb.inp("xT", [128, 8, NT + 2])
    c_in = kb.inp("c2", [128, 8, 2])
    adaw = kb.inp("adaw", [1024, 3 * D])
    adab = kb.inp("adab", [128, 24])
    ng = kb.inp("ng", [128, 8])
    wup = kb.inp("wup", [D, 2 * DFF])
    wdn = kb.inp("wdn", [DFF, D])
    cw = kb.inp("cw", [128, 22 * 3])
    cb = kb.inp("cb", [128, 22])
    msk = kb.inp("msk", [128, 2])
    if final:
        fg = kb.inp("fg", [128, 8])
    out = kb.outp("out", [128, 8, NT])

    W = TW + 2
    r_wup, r_wdn = Res(), Res()
    wup_t = load_cast_weight(kb, wup, 8, 2 * DFF, "wup_t", r_wup, colblk=1408)
    wdn_t = load_cast_weight(kb, wdn, 22, D, "wdn_t", r_wdn)
    mod, r_mod = emit_mod(kb, c_in, adaw, adab, 24, "mod")
    ng_t, r_ng = small_load(kb, ng, [128, 8], "ng_t")
    cw_t, r_cw = small_load(kb, cw, [128, 66], "cw_t")
    cb_t, r_cb = small_load(kb, cb, [128, 22], "cb_t")
    mk_t, r_mk = small_load(kb, msk, [128, 2], "mk_t")
    if final:
        fg_t, r_fg = small_load(kb, fg, [128, 8], "fg_t")
    gs, r_gs = emit_gs(kb, mod, r_mod, 8, ng_t, r_ng, "gs")
    nrm = Norm(kb, W, "nrm").init_eps()

    xt = [kb.sb([128, 8, W], F32, "xt%d" % i) for i in range(2)]
    r_xt = [Res(), Res()]
    tmp = kb.sb([128, W], F32, "tmp")
    r_tmp = Res()
    h = kb.sb([128, 8, W], BF16, "h")
    r_h = Res()
    a = kb.sb([128, 22, W], BF16, "a")
    r_a = Res()
    cbuf = [kb.sb([128, W], F32, "cbuf%d" % i) for i in range(2)]
    r_cbuf = [Res(), Res()]
    sbuf_ = [kb.sb([128, W], F32, "sbuf%d" % i) for i in range(2)]
    r_sbuf = [Res(), Res()]
    xo = kb.sb([128, 8, W], F32, "xo")
    r_xo = Res()
    pg = [kb.ps("pg%d" % i) for i in range(2)]
    pv = [kb.ps("pv%d" % i) for i in range(2)]
    r_pg = [Res(), Res()]
    r_pv = [Res(), Res()]
    po = [kb.ps("po%d" % i) for i in range(2)]
    r_po = [Res(), Res()]

    ntiles = (NT + TW - 1) // TW
    for ti in range(ntiles):
        o0 = ti * TW
        o1 = min(NT, o0 + TW)
        w = o1 - o0 + 2
        s = ti % 2
        x3 = xt[s][:, :, :w]
        kb.dma(x3, xT[:, :, o0:o0 + w], writes=[r_xt[s]])
        rstd, r_rstd = nrm.emit(x3, r_xt[s], w)
        for k in range(8):
            kb.op("dve", lambda e, k=k: e.scalar_tensor_tensor(out=tmp[:, :w], in0=xt[s][:, k, :w], scalar=gs[:, k:k + 1],
                                                                in1=rstd, op0=ALU.mult, op1=ALU.mult),
                  reads=[r_xt[s], r_gs, r_rstd], writes=[r_tmp])
            kb.op("act", lambda e, k=k: e.activation(out=h[:, k, :w], in_=tmp[:, :w], func=AF.Identity,
                                                     bias=mod[:, k:k + 1], scale=1.0),
                  reads=[r_tmp, r_mod], writes=[r_h])
        if ti == 0:
            kb.op("dve", lambda e: e.tensor_scalar(out=h[:, :, 0:1], in0=h[:, :, 0:1], scalar1=mk_t[:, 0:1], scalar2=None, op0=ALU.mult),
                  reads=[r_mk, r_h], writes=[r_h])
        if ti == ntiles - 1:
            kb.op("dve", lambda e: e.tensor_scalar(out=h[:, :, w - 1:w], in0=h[:, :, w - 1:w], scalar1=mk_t[:, 1:2], scalar2=None, op0=ALU.mult),
                  reads=[r_mk, r_h], writes=[r_h])
        for j in range(22):
            q = j % 2
            for k in range(8):
                kb.op("pe", lambda e, k=k, j=j, q=q: e.matmul(pg[q][:, :w], lhsT=wup_t[:, k, j * 128:(j + 1) * 128], rhs=h[:, k, :w],
                                                               start=(k == 0), stop=(k == 7)),
                      reads=[r_wup, r_h], writes=[r_pg[q]])
            for k in range(8):
                kb.op("pe", lambda e, k=k, j=j, q=q: e.matmul(pv[q][:, :w], lhsT=wup_t[:, k, DFF + j * 128:DFF + (j + 1) * 128], rhs=h[:, k, :w],
                                                               start=(k == 0), stop=(k == 7)),
                      reads=[r_wup, r_h], writes=[r_pv[q]])
            cbq = cbuf[q]
            kb.op("act", lambda e, j=j, q=q, cbq=cbq: e.activation(out=cbq[:, 1:w - 1], in_=pg[q][:, 1:w - 1], func=AF.Identity,
                                                                 scale=cw_t[:, 3 * j + 1:3 * j + 2], bias=cb_t[:, j:j + 1]),
                  reads=[r_pg[q], r_cw, r_cb], writes=[r_cbuf[q]])
            kb.op("dve", lambda e, j=j, q=q, cbq=cbq: e.scalar_tensor_tensor(out=cbq[:, 1:w - 1], in0=pg[q][:, 0:w - 2], scalar=cw_t[:, 3 * j:3 * j + 1],
                                                                           in1=cbq[:, 1:w - 1], op0=ALU.mult, op1=ALU.add),
                  reads=[r_pg[q], r_cw, r_cbuf[q]], writes=[r_cbuf[q]])
            kb.op("dve", lambda e, j=j, q=q, cbq=cbq: e.scalar_tensor_tensor(out=cbq[:, 1:w - 1], in0=pg[q][:, 2:w], scalar=cw_t[:, 3 * j + 2:3 * j + 3],
                                                                           in1=cbq[:, 1:w - 1], op0=ALU.mult, op1=ALU.add),
                  reads=[r_pg[q], r_cw, r_cbuf[q]], writes=[r_cbuf[q]])
            sbq = sbuf_[q]
            kb.op("act", lambda e, q=q, cbq=cbq, sbq=sbq: e.activation(out=sbq[:, 1:w - 1], in_=cbq[:, 1:w - 1], func=AF.Silu),
                  reads=[r_cbuf[q]], writes=[r_sbuf[q]])
            kb.op("dve", lambda e, j=j, q=q, sbq=sbq: e.tensor_tensor(out=a[:, j, 1:w - 1], in0=sbq[:, 1:w - 1], in1=pv[q][:, 1:w - 1], op=ALU.mult),
                  reads=[r_sbuf[q], r_pv[q]], writes=[r_a])
        for m in range(8):
            q = m % 2
            for j in range(22):
                kb.op("pe", lambda e, m=m, j=j, q=q: e.matmul(po[q][:, :w - 2], lhsT=wdn_t[:, j, m * 128:(m + 1) * 128], rhs=a[:, j, 1:w - 1],
                                                               start=(j == 0), stop=(j == 21)),
                      reads=[r_wdn, r_a], writes=[r_po[q]])
            kb.op("dve", lambda e, m=m, q=q: e.scalar_tensor_tensor(out=xo[:, m, :w - 2], in0=po[q][:, :w - 2], scalar=mod[:, 16 + m:17 + m],
                                                                    in1=xt[s][:, m, 1:w - 1], op0=ALU.mult, op1=ALU.add),
                  reads=[r_po[q], r_mod, r_xt[s]], writes=[r_xo])
        if final:
            rstd2, r_rstd2 = nrm.emit(xo[:, :, :w - 2], r_xo, w - 2)
            for m in range(8):
                kb.op("dve", lambda e, m=m: e.scalar_tensor_tensor(out=xo[:, m, :w - 2], in0=xo[:, m, :w - 2], scalar=fg_t[:, m:m + 1],
                                                                   in1=rstd2, op0=ALU.mult, op1=ALU.mult),
                      reads=[r_xo, r_fg, r_rstd2], writes=[r_xo])
        kb.dma(out[:, :, o0:o1], xo[:, :, :w - 2], reads=[r_xo], q="sp", is_output=True)
    return kb.finish()


HD = 64
QPERM = [0, 4, 1, 5, 2, 6, 3, 7, 8, 12, 9, 13, 10, 14, 11, 15]


def build_attn(L=8192, NQ=4096, TW=512):
    kb = KB()
    nc = kb.nc
    xall = kb.inp("xall", [128, 8, L])
    xq = kb.inp("xq", [128, 8, NQ])
    c_in = kb.inp("c2", [128, 8, 2])
    adaw = kb.inp("adaw", [1024, 3 * D])
    adab = kb.inp("adab", [128, 24])
    ng = kb.inp("ng", [128, 8])
    wkv = kb.inp("wkv", [D, 768])
    wq = kb.inp("wq", [D, 2048])
    wo = kb.inp("wo", [D, D])
    gains = kb.inp("gains", [128, 4])
    ropek = kb.inp("ropek", [2, 128, L])
    ropeq = kb.inp("ropeq", [2, 128, NQ])
    out = kb.outp("out", [128, 8, NQ])

    kT = kb.sb([128, 2, L], BF16, "kT")
    r_kT = Res()
    NKT = L // 128
    vaug = kb.sb([128, NKT, 4, 65], BF16, "vaug")
    r_v = Res()
    kb.op("pool", lambda e: e.memset(vaug[:, :, :, 64:65], 1.0), writes=[r_v])
    pss = kb.ps("pss")
    r_pss = Res()
    mod, r_mod = emit_mod(kb, c_in, adaw, adab, 24, "mod", ps=pss, r_ps_in=r_pss)
    ng_t, r_ng = small_load(kb, ng, [128, 8], "ng_t")
    gn_t, r_gn = small_load(kb, gains, [128, 4], "gn_t")
    gs, r_gs = emit_gs(kb, mod, r_mod, 8, ng_t, r_ng, "gs")
    nrm = Norm(kb, TW, "nrm").init_eps()
    bones = kb.sb([128, 128], BF16, "bones")
    r_bones = Res()
    kb.op("dve", lambda e: e.memset(bones[:], 0.0), writes=[r_bones])
    kb.op("dve", lambda e: e.memset(bones[0:64, 0:64], 1.0), writes=[r_bones])
    kb.op("dve", lambda e: e.memset(bones[64:128, 64:128], 1.0), writes=[r_bones])
    sel = kb.sb([65, 64], F32, "sel")
    r_sel = Res()
    kb.op("dve", lambda e: e.memset(sel[:], 0.0), writes=[r_sel])
    kb.op("dve", lambda e: e.memset(sel[64:65, :], 1.0), writes=[r_sel])

    xt = kb.sb([128, 8, TW], F32, "xt")
    r_xt = Res()
    h = kb.sb([128, 8, TW], BF16, "h")
    r_h = Res()
    tmp = kb.sb([128, TW], F32, "tmp")
    r_tmp = Res()
    ctab = kb.sb([128, 2, TW], F32, "ctab")
    r_ctab = Res()
    sqh = kb.sb([128, TW], BF16, "sqh")
    r_sqh = Res()
    t1 = kb.sb([128, TW], F32, "t1")
    t2 = kb.sb([128, TW], F32, "t2")
    r_t1, r_t2 = Res(), Res()
    rs = kb.sb([128, TW], F32, "rs")
    r_rs = Res()
    pa = [kb.ps("pa%d" % i) for i in range(2)]
    r_pa = [Res(), Res()]

    def modnorm(src_ap, w):
        kb.dma(xt[:, :, :w], src_ap, writes=[r_xt])
        rstd, r_rstd = nrm.emit(xt[:, :, :w], r_xt, w)
        for k in range(8):
            kb.op("dve", lambda e, k=k: e.scalar_tensor_tensor(out=tmp[:, :w], in0=xt[:, k, :w], scalar=gs[:, k:k + 1],
                                                                in1=rstd, op0=ALU.mult, op1=ALU.mult),
                  reads=[r_xt, r_gs, r_rstd], writes=[r_tmp])
            kb.op("act", lambda e, k=k: e.activation(out=h[:, k, :w], in_=tmp[:, :w], func=AF.Identity,
                                                     bias=mod[:, k:k + 1], scale=1.0),
                  reads=[r_tmp, r_mod], writes=[r_h])

    def proj_rope(wt, r_wt, col, col_sw, gcol, dst_ap, r_dst, w, scale):
        for k in range(8):
            kb.op("pe", lambda e, k=k: e.matmul(pa[0][:, :w], lhsT=wt[:, k, col:col + 128], rhs=h[:, k, :w], start=(k == 0), stop=(k == 7)),
                  reads=[r_wt, r_h], writes=[r_pa[0]])
        for k in range(8):
            kb.op("pe", lambda e, k=k: e.matmul(pa[1][:, :w], lhsT=wt[:, k, col_sw:col_sw + 128], rhs=h[:, k, :w], start=(k == 0), stop=(k == 7)),
                  reads=[r_wt, r_h], writes=[r_pa[1]])
        kb.op("act", lambda e: e.activation(out=sqh[:, :w], in_=pa[0][:, :w], func=AF.Square), reads=[r_pa[0]], writes=[r_sqh])
        kb.op("pe", lambda e: e.matmul(pss[:, :w], lhsT=bones[:], rhs=sqh[:, :w], start=True, stop=True),
              reads=[r_sqh, r_bones], writes=[r_pss])
        kb.op("act", lambda e: e.activation(out=rs[:, :w], in_=pss[:, :w], func=AF.Sqrt, scale=1.0 / HD, bias=nrm.epsb[:]),
              reads=[r_pss, nrm.r_eps], writes=[r_rs])
        kb.op("dve", lambda e: e.reciprocal(out=rs[:, :w], in_=rs[:, :w]), reads=[r_rs], writes=[r_rs])
        kb.op("dve", lambda e: e.scalar_tensor_tensor(out=t1[:, :w], in0=pa[0][:, :w], scalar=gn_t[:, gcol:gcol + 1], in1=ctab[:, 0, :w],
                                                      op0=ALU.mult, op1=ALU.mult), reads=[r_pa[0], r_gn, r_ctab], writes=[r_t1])
        kb.op("dve", lambda e: e.scalar_tensor_tensor(out=t2[:, :w], in0=pa[1][:, :w], scalar=gn_t[:, gcol + 1:gcol + 2], in1=ctab[:, 1, :w],
                                                      op0=ALU.mult, op1=ALU.mult), reads=[r_pa[1], r_gn, r_ctab], writes=[r_t2])
        kb.op("pool", lambda e: e.tensor_tensor(out=t1[:, :w], in0=t1[:, :w], in1=t2[:, :w], op=ALU.add), reads=[r_t1, r_t2], writes=[r_t1])
        if isinstance(dst_ap, tuple):
            for (dap, lo) in dst_ap:
                kb.op("dve", lambda e, dap=dap, lo=lo: e.scalar_tensor_tensor(out=dap, in0=t1[lo:lo + 64, :w], scalar=float(scale), in1=rs[lo:lo + 64, :w],
                                                                              op0=ALU.mult, op1=ALU.mult), reads=[r_t1, r_rs], writes=[r_dst])
        else:
            kb.op("dve", lambda e: e.scalar_tensor_tensor(out=dst_ap, in0=t1[:, :w], scalar=float(scale), in1=rs[:, :w],
                                                          op0=ALU.mult, op1=ALU.mult), reads=[r_t1, r_rs], writes=[r_dst])

    r_wkv = Res()
    g_wkv = nc.sbuf_tensor("wkv_t", [128, 8, 768], BF16)
    wkv_t = g_wkv.__enter__()
    srckv = wkv.rearrange("(k p) n -> p k n", p=128)
    for k in range(8):
        kb.dma(wkv_t[:, k, :], srckv[:, k, :], writes=[r_wkv], q="pool")
    pvp = kb.ps("pvp")
    r_pvp = Res()
    for ti in range(L // TW):
        t0 = ti * TW
        w = TW
        modnorm(xall[:, :, t0:t0 + w], w)
        kb.dma(ctab[:, :, :w], ropek[:, :, t0:t0 + w].rearrange("c p t -> p c t"), writes=[r_ctab])
        for kc in range(2):
            proj_rope(wkv_t, r_wkv, kc * 128, 256 + kc * 128, 2, kT[:, kc, t0:t0 + w], r_kT, w, 1.0)
        for ts in range(w // 128):
            kt = (t0 // 128) + ts
            for k in range(8):
                kb.op("pe", lambda e, k=k, ts=ts: e.matmul(pvp[:, 0:256], lhsT=h[:, k, ts * 128:(ts + 1) * 128], rhs=wkv_t[:, k, 512:768],
                                                           start=(k == 0), stop=(k == 7)), reads=[r_h, r_wkv], writes=[r_pvp])
            kb.op("act", lambda e, kt=kt: e.activation(out=vaug[:, kt, :, 0:64], in_=pvp[:, 0:256].rearrange("p (a b) -> p a b", a=4),
                                                       func=AF.Copy), reads=[r_pvp], writes=[r_v])
    r_free = r_wkv
    g_wkv.__exit__(None, None, None)

    r_wq, r_wo = Res(), Res()
    r_wq.r = list(r_free.r)
    r_wq.w = r_free.w
    r_wo.r = list(r_free.r)
    r_wo.w = r_free.w
    wq_t = kb.sb([128, 8, 2048], BF16, "wq_t")
    srcq = wq.rearrange("(k p) n -> p k n", p=128)
    for k in range(8):
        for c0 in range(0, 2048, 1024):
            kb.dma(wq_t[:, k, c0:c0 + 1024], srcq[:, k, c0:c0 + 1024], writes=[r_wq], q="pool")
    wo_t = kb.sb([128, 8, D], BF16, "wo_t")
    srco = wo.rearrange("(k p) n -> p k n", p=128)
    for k in range(8):
        kb.dma(wo_t[:, k, :], srco[:, k, :], writes=[r_wo], q="pool")
    qT = kb.sb([128, 16, TW], BF16, "qT")
    r_qT = Res()
    kb.op("pool", lambda e: e.memset(qT[:], 0.0), writes=[r_qT])
    oT = kb.sb([128, 8, TW], BF16, "oT")
    r_oT = Res()
    otmp = [kb.sb([64, TW], BF16, "otmp%d" % i) for i in range(2)]
    r_otmp = [Res(), Res()]
    pT = [kb.sb([128, TW], BF16, "pT%d" % i) for i in range(3)]
    r_pT = [Res() for _ in range(3)]
    oacc = kb.sb([65, TW], F32, "oacc")
    r_oacc = Res()
    rec = kb.sb([64, TW], F32, "rec")
    r_rec = Res()
    psc = [kb.ps("psc%d" % i) for i in range(2)] + [pa[1]]
    r_psc = [Res(), Res(), r_pa[1]]
    pso = kb.ps("pso")
    r_pso = Res()
    pmisc = pvp
    r_pmisc = r_pvp

    for qi in range(NQ // TW):
        t0 = qi * TW
        w = TW
        modnorm(xq[:, :, t0:t0 + w], w)
        kb.dma(ctab[:, :, :w], ropeq[:, :, t0:t0 + w].rearrange("c p t -> p c t"), writes=[r_ctab])
        for j in range(8):
            proj_rope(wq_t, r_wq, j * 128, 1024 + j * 128, 0, ((qT[0:64, 2 * j, :w], 0), (qT[64:128, 2 * j + 1, :w], 64)), r_qT, w, 0.125)
        for j in range(8):
            for hb in range(2):
                hq = QPERM[2 * j + hb]
                kvh = hq // 4
                assert kvh % 2 == hb
                kc = kvh // 2
                b0 = 64 * hb
                hidx = 2 * j + hb
                pso_c, r_pso_c = (pso, r_pso) if hidx % 2 == 0 else (pa[0], r_pa[0])

                def qk(kt):
                    sl = kt % 3
                    kb.op("pe", lambda e, kt=kt, sl=sl: e.matmul(psc[sl][:, :w], lhsT=kT[:, kc, kt * 128:(kt + 1) * 128],
                                                                 rhs=qT[:, 2 * j + hb, :w], start=True, stop=True),
                          reads=[r_kT, r_qT], writes=[r_psc[sl]])

                def ex(kt):
                    sl = kt % 3
                    kb.op("act", lambda e, sl=sl: e.activation(out=pT[sl][:, :w], in_=psc[sl][:, :w], func=AF.Exp),
                          reads=[r_psc[sl]], writes=[r_pT[sl]])

                def pv(kt):
                    sl = kt % 3
                    kb.op("pe", lambda e, kt=kt, sl=sl: e.matmul(pso_c[0:65, :w], lhsT=vaug[:, kt, kvh, :], rhs=pT[sl][:, :w],
                                                                 start=(kt == 0), stop=(kt == NKT - 1)),
                          reads=[r_v, r_pT[sl]], writes=[r_pso_c])
                qk(0)
                if NKT > 1:
                    qk(1)
                for kt in range(NKT):
                    ex(kt)
                    if kt + 2 < NKT:
                        qk(kt + 2)
                    pv(kt)
                kb.op("dve", lambda e: e.tensor_copy(out=oacc[:, :w], in_=pso_c[0:65, :w]), reads=[r_pso_c], writes=[r_oacc])
                kb.op("pe", lambda e: e.matmul(pmisc[0:64, :w], lhsT=sel[:], rhs=oacc[:, :w], start=True, stop=True),
                      reads=[r_sel, r_oacc], writes=[r_pmisc])
                kb.op("dve", lambda e: e.reciprocal(out=rec[:, :w], in_=pmisc[0:64, :w]), reads=[r_pmisc], writes=[r_rec])
                if hq % 2 == 0:
                    kb.op("dve", lambda e, hq=hq: e.tensor_tensor(out=oT[0:64, hq // 2, :w], in0=oacc[0:64, :w], in1=rec[:, :w], op=ALU.mult),
                          reads=[r_oacc, r_rec], writes=[r_oT])
                else:
                    osl = (hq // 2) % 2
                    kb.op("dve", lambda e, osl=osl: e.tensor_tensor(out=otmp[osl][:, :w], in0=oacc[0:64, :w], in1=rec[:, :w], op=ALU.mult),
                          reads=[r_oacc, r_rec], writes=[r_otmp[osl]])
                    kb.dma(oT[64:128, hq // 2, :w], otmp[osl][:, :w], reads=[r_otmp[osl]], writes=[r_oT], q="sp")
        for m in range(8):
            for k in range(8):
                kb.op("pe", lambda e, m=m, k=k: e.matmul(pmisc[:, :w], lhsT=wo_t[:, k, m * 128:(m + 1) * 128], rhs=oT[:, k, :w],
                                                         start=(k == 0), stop=(k == 7)), reads=[r_wo, r_oT], writes=[r_pmisc])
            kb.op("dve", lambda e, m=m: e.scalar_tensor_tensor(out=xt[:, m, :w], in0=pmisc[:, :w], scalar=mod[:, 16 + m:17 + m],
                                                               in1=xt[:, m, :w], op0=ALU.mult, op1=ALU.add),
                  reads=[r_pmisc, r_mod, r_xt], writes=[r_xt])
        kb.dma(out[:, :, t0:t0 + w], xt[:, :, :w], reads=[r_xt], q="sp", is_output=True)
    return kb.finish()


TWO_PI = 6.283185307179586


def emit_sincos(kb, ang, r_ang, sin_out, cos_out, r_out, shape, name):
    PI = 3.141592653589793
    MAGIC = 12582912.0
    kf = kb.sb(shape, F32, name + "_kf")
    mk = kb.sb(shape, F32, name + "_mk")
    r2, r3 = Res(), Res()
    kb.op("dve", lambda e: e.tensor_scalar(out=kf[:], in0=ang, scalar1=1.0 / TWO_PI, scalar2=MAGIC, op0=ALU.mult, op1=ALU.add), reads=[r_ang], writes=[r2])
    kb.op("dve", lambda e: e.tensor_scalar(out=kf[:], in0=kf[:], scalar1=-MAGIC, scalar2=None, op0=ALU.add), reads=[r2], writes=[r2])
    kb.op("dve", lambda e: e.scalar_tensor_tensor(out=ang, in0=kf[:], scalar=-TWO_PI, in1=ang, op0=ALU.mult, op1=ALU.add),
          reads=[r2, r_ang], writes=[r_ang])
    kb.op("dve", lambda e: e.tensor_scalar(out=kf[:], in0=ang, scalar1=-PI, scalar2=PI, op0=ALU.max, op1=ALU.min), reads=[r_ang], writes=[r2])
    kb.op("act", lambda e: e.activation(out=sin_out, in_=kf[:], func=AF.Sin), reads=[r2], writes=[r_out])
    kb.op("dve", lambda e: e.tensor_scalar(out=ang, in0=ang, scalar1=PI / 2, scalar2=None, op0=ALU.add), reads=[r_ang, r2], writes=[r_ang])
    kb.op("dve", lambda e: e.tensor_scalar(out=mk[:], in0=ang, scalar1=PI, scalar2=None, op0=ALU.is_gt), reads=[r_ang], writes=[r3])
    kb.op("dve", lambda e: e.scalar_tensor_tensor(out=ang, in0=mk[:], scalar=-TWO_PI, in1=ang, op0=ALU.mult, op1=ALU.add),
          reads=[r3, r_ang], writes=[r_ang])
    kb.op("dve", lambda e: e.tensor_scalar(out=kf[:], in0=ang, scalar1=-PI, scalar2=PI, op0=ALU.max, op1=ALU.min), reads=[r_ang, r_out], writes=[r2])
    kb.op("act", lambda e: e.activation(out=cos_out, in_=kf[:], func=AF.Sin), reads=[r2], writes=[r_out])


def build_s5(L=8192, NB=4, TW=512):
    kb = KB()
    hT = kb.inp("hT", [NB, 128, L])
    are = kb.inp("are", [128, 8])
    aim = kb.inp("aim", [128, 8])
    ldt = kb.inp("ldt", [128, 8])
    bre = kb.inp("bre", [128, 4, 128])
    bim = kb.inp("bim", [128, 4, 128])
    cre = kb.inp("cre", [128, 4, 128])
    cim = kb.inp("cim", [128, 4, 128])
    dsk = kb.inp("dsk", [128, 1])
    tau = kb.inp("tau", [128, TW + 1])
    y = kb.outp("y", [NB, 128, L])
    NG = 4
    are_t, r_are = small_load(kb, are, [128, 8], "are_t")
    aim_t, r_aim = small_load(kb, aim, [128, 8], "aim_t")
    ldt_t, r_ldt = small_load(kb, ldt, [128, 8], "ldt_t")
    bre_t, r_bre = small_load(kb, bre, [128, 4, 128], "bre_t")
    bim_t, r_bim = small_load(kb, bim, [128, 4, 128], "bim_t")
    cre_t, r_cre = small_load(kb, cre, [128, 4, 128], "cre_t")
    cim_t, r_cim = small_load(kb, cim, [128, 4, 128], "cim_t")
    dsk_t, r_dsk = small_load(kb, dsk, [128, 1], "dsk_t")
    tau_t, r_tau = small_load(kb, tau, [128, TW + 1], "tau_t")
    P = {}
    rP = Res()

    def sm(name):
        P[name] = kb.sb([128, 8], F32, "p_" + name)
        return P[name]
    for n in ("lre", "dt", "a", "th", "r", "s1", "c1", "lbr1", "lbi", "den", "fr", "fi", "t"):
        sm(n)
    V = lambda n: P[n][:]
    rr_all = [r_are, r_aim, r_ldt, rP]
    kb.op("dve", lambda e: e.tensor_scalar(out=V("lre"), in0=are_t[:], scalar1=-1e-4, scalar2=None, op0=ALU.min), reads=rr_all, writes=[rP])
    kb.op("act", lambda e: e.activation(out=V("dt"), in_=ldt_t[:], func=AF.Exp), reads=rr_all, writes=[rP])
    kb.op("dve", lambda e: e.tensor_tensor(out=V("a"), in0=V("lre"), in1=V("dt"), op=ALU.mult), reads=rr_all, writes=[rP])
    kb.op("dve", lambda e: e.tensor_tensor(out=V("th"), in0=aim_t[:], in1=V("dt"), op=ALU.mult), reads=rr_all, writes=[rP])
    kb.op("act", lambda e: e.activation(out=V("r"), in_=V("a"), func=AF.Exp), reads=rr_all, writes=[rP])
    kb.op("dve", lambda e: e.tensor_copy(out=V("t"), in_=V("th")), reads=rr_all, writes=[rP])
    emit_sincos(kb, V("t"), rP, V("s1"), V("c1"), rP, [128, 8], "sc0")
    kb.op("dve", lambda e: e.tensor_tensor(out=V("lbr1"), in0=V("r"), in1=V("c1"), op=ALU.mult), reads=[rP], writes=[rP])
    kb.op("dve", lambda e: e.tensor_scalar(out=V("lbr1"), in0=V("lbr1"), scalar1=-1.0, scalar2=None, op0=ALU.add), reads=[rP], writes=[rP])
    kb.op("dve", lambda e: e.tensor_tensor(out=V("lbi"), in0=V("r"), in1=V("s1"), op=ALU.mult), reads=[rP], writes=[rP])
    kb.op("dve", lambda e: e.tensor_tensor(out=V("den"), in0=V("lre"), in1=V("lre"), op=ALU.mult), reads=[rP], writes=[rP])
    kb.op("dve", lambda e: e.tensor_tensor(out=V("t"), in0=aim_t[:], in1=aim_t[:], op=ALU.mult), reads=[rP, r_aim], writes=[rP])
    kb.op("dve", lambda e: e.tensor_tensor(out=V("den"), in0=V("den"), in1=V("t"), op=ALU.add), reads=[rP], writes=[rP])
    kb.op("dve", lambda e: e.reciprocal(out=V("den"), in_=V("den")), reads=[rP], writes=[rP])
    kb.op("dve", lambda e: e.tensor_tensor(out=V("fr"), in0=V("lbr1"), in1=V("lre"), op=ALU.mult), reads=[rP], writes=[rP])
    kb.op("dve", lambda e: e.tensor_tensor(out=V("t"), in0=V("lbi"), in1=aim_t[:], op=ALU.mult), reads=[rP], writes=[rP])
    kb.op("dve", lambda e: e.tensor_tensor(out=V("fr"), in0=V("fr"), in1=V("t"), op=ALU.add), reads=[rP], writes=[rP])
    kb.op("dve", lambda e: e.tensor_tensor(out=V("fr"), in0=V("fr"), in1=V("den"), op=ALU.mult), reads=[rP], writes=[rP])
    kb.op("dve", lambda e: e.tensor_tensor(out=V("fi"), in0=V("lbi"), in1=V("lre"), op=ALU.mult), reads=[rP], writes=[rP])
    kb.op("dve", lambda e: e.tensor_tensor(out=V("t"), in0=V("lbr1"), in1=aim_t[:], op=ALU.mult), reads=[rP], writes=[rP])
    kb.op("dve", lambda e: e.tensor_tensor(out=V("fi"), in0=V("fi"), in1=V("t"), op=ALU.subtract), reads=[rP], writes=[rP])
    kb.op("dve", lambda e: e.tensor_tensor(out=V("fi"), in0=V("fi"), in1=V("den"), op=ALU.mult), reads=[rP], writes=[rP])
    crp = kb.sb([128, 4, 128], F32, "crp")
    ncip = kb.sb([128, 4, 128], F32, "ncip")
    ctmp = kb.sb([128, 128], F32, "ctmp")
    r_cp = Res()
    for j in range(NG):
        kb.op("dve", lambda e, j=j: e.tensor_scalar(out=ctmp[:], in0=cim_t[:, j, :], scalar1=P["fi"][:, j:j + 1], scalar2=None, op0=ALU.mult),
              reads=[rP, r_cim, r_cp], writes=[r_cp])
        kb.op("dve", lambda e, j=j: e.scalar_tensor_tensor(out=crp[:, j, :], in0=cre_t[:, j, :], scalar=P["fr"][:, j:j + 1], in1=ctmp[:],
                                                           op0=ALU.mult, op1=ALU.subtract), reads=[rP, r_cre, r_cp], writes=[r_cp])
        kb.op("dve", lambda e, j=j: e.tensor_scalar(out=ctmp[:], in0=cim_t[:, j, :], scalar1=P["fr"][:, j:j + 1], scalar2=None, op0=ALU.mult),
              reads=[rP, r_cim, r_cp], writes=[r_cp])
        kb.op("dve", lambda e, j=j: e.scalar_tensor_tensor(out=ncip[:, j, :], in0=cre_t[:, j, :], scalar=P["fi"][:, j:j + 1], in1=ctmp[:],
                                                           op0=ALU.mult, op1=ALU.add), reads=[rP, r_cre, r_cp], writes=[r_cp])
        kb.op("dve", lambda e, j=j: e.tensor_scalar(out=ncip[:, j, :], in0=ncip[:, j, :], scalar1=-1.0, scalar2=None, op0=ALU.mult),
              reads=[r_cp], writes=[r_cp])
    ctab = kb.sb([128, NG, TW + 1], F32, "s5ctab")
    stab = kb.sb([128, NG, TW + 1], F32, "s5stab")
    rtab = kb.sb([128, NG, TW], F32, "s5rtab")
    angt = kb.sb([128, TW + 1], F32, "angt")
    r_tab, r_angt = Res(), Res()
    for j in range(NG):
        kb.op("dve", lambda e, j=j: e.tensor_scalar(out=angt[:], in0=tau_t[:], scalar1=P["th"][:, j:j + 1], scalar2=None, op0=ALU.mult),
              reads=[rP, r_tau, r_angt], writes=[r_angt])
        emit_sincos(kb, angt[:], r_angt, stab[:, j, :], ctab[:, j, :], r_tab, [128, TW + 1], "sc%d" % (j + 1))
        kb.op("dve", lambda e, j=j: e.tensor_scalar(out=rtab[:, j, :], in0=tau_t[:, 0:TW], scalar1=0.0, scalar2=P["r"][:, j:j + 1],
                                                    op0=ALU.mult, op1=ALU.add), reads=[rP, r_tau], writes=[r_tab])
    ut = [kb.sb([128, TW], F32, "ut%d" % i) for i in range(2)]
    r_ut = [Res(), Res()]
    pbr = [kb.ps("pbr%d" % i) for i in range(2)]
    pbi = [kb.ps("pbi%d" % i) for i in range(2)]
    r_pbr = [Res(), Res()]
    r_pbi = [Res(), Res()]
    py = [kb.ps("py%d" % i) for i in range(2)]
    r_py = [Res(), Res()]
    names = ["m1", "m2", "m3", "m4", "n1", "n2", "n3", "n4", "wr", "wi", "xr", "xi"]
    T = {n: [kb.sb([128, TW], F32, "s5_%s%d" % (n, i)) for i in range(2)] for n in names}
    R = {n: [Res(), Res()] for n in names}
    pzr = kb.ps("pzr")
    pzi = kb.ps("pzi")
    r_pzr, r_pzi = Res(), Res()
    carry = kb.sb([128, NG, 4], F32, "carry")
    r_carry = [Res() for _ in range(NG)]
    yo = [kb.sb([128, TW], F32, "yo%d" % i) for i in range(2)]
    r_yo = [Res(), Res()]
    its = []
    for b in range(NB):
        for ck in range(L // TW):
            for gp in range(NG):
                its.append((b, ck, gp))

    def stage_a(i):
        b, ck, gp = its[i]
        us, s = ck % 2, i % 2
        g = lambda n: T[n][s][:]
        rg = lambda n: R[n][s]
        if gp == 0:
            kb.dma(ut[us][:], hT[b, :, ck * TW:(ck + 1) * TW], writes=[r_ut[us]])
        kb.op("pe", lambda e: e.matmul(pbr[s][:], lhsT=bre_t[:, gp, :], rhs=ut[us][:], start=True, stop=True),
              reads=[r_bre, r_ut[us]], writes=[r_pbr[s]])
        kb.op("pe", lambda e: e.matmul(pbi[s][:], lhsT=bim_t[:, gp, :], rhs=ut[us][:], start=True, stop=True),
              reads=[r_bim, r_ut[us]], writes=[r_pbi[s]])
        cc = ctab[:, gp, 0:TW]
        ss = stab[:, gp, 0:TW]
        kb.op("dve", lambda e: e.tensor_tensor(out=g("m1"), in0=pbr[s][:], in1=cc, op=ALU.mult), reads=[r_pbr[s], r_tab], writes=[rg("m1")])
        kb.op("dve", lambda e: e.tensor_tensor(out=g("m2"), in0=pbi[s][:], in1=ss, op=ALU.mult), reads=[r_pbi[s], r_tab], writes=[rg("m2")])
        kb.op("dve", lambda e: e.tensor_tensor(out=g("m3"), in0=pbi[s][:], in1=cc, op=ALU.mult), reads=[r_pbi[s], r_tab], writes=[rg("m3")])
        kb.op("dve", lambda e: e.tensor_tensor(out=g("m4"), in0=pbr[s][:], in1=ss, op=ALU.mult), reads=[r_pbr[s], r_tab], writes=[rg("m4")])
        kb.op("pool", lambda e: e.tensor_tensor(out=g("wr"), in0=g("m1"), in1=g("m2"), op=ALU.add), reads=[rg("m1"), rg("m2")], writes=[rg("wr")])
        kb.op("pool", lambda e: e.tensor_tensor(out=g("wi"), in0=g("m3"), in1=g("m4"), op=ALU.subtract), reads=[rg("m3"), rg("m4")], writes=[rg("wi")])

    def stage_b(i):
        b, ck, gp = its[i]
        us, s = ck % 2, i % 2
        g = lambda n: T[n][s][:]
        rg = lambda n: R[n][s]
        cc = ctab[:, gp, 0:TW]
        ss = stab[:, gp, 0:TW]
        if ck == 0:
            ini_r, ini_i = 0.0, 0.0
        else:
            ini_r, ini_i = carry[:, gp, 2:3], carry[:, gp, 3:4]
        kb.op("dve", lambda e: e.tensor_tensor_scan(out=pzr[:], data0=rtab[:, gp, :], data1=g("wr"), initial=ini_r, op0=ALU.mult, op1=ALU.add),
              reads=[rg("wr"), r_tab, r_carry[gp]], writes=[r_pzr])
        kb.op("dve", lambda e: e.tensor_tensor_scan(out=pzi[:], data0=rtab[:, gp, :], data1=g("wi"), initial=ini_i, op0=ALU.mult, op1=ALU.add),
              reads=[rg("wi"), r_tab, r_carry[gp]], writes=[r_pzi])
        cT = ctab[:, gp, TW:TW + 1]
        sT = stab[:, gp, TW:TW + 1]
        kb.op("dve", lambda e: e.tensor_tensor(out=carry[:, gp, 0:1], in0=pzi[:, TW - 1:TW], in1=sT, op=ALU.mult),
              reads=[r_pzi, r_tab, r_carry[gp]], writes=[r_carry[gp]])
        kb.op("dve", lambda e: e.scalar_tensor_tensor(out=carry[:, gp, 2:3], in0=pzr[:, TW - 1:TW], scalar=cT, in1=carry[:, gp, 0:1],
                                                      op0=ALU.mult, op1=ALU.subtract), reads=[r_pzr, r_tab, r_carry[gp]], writes=[r_carry[gp]])
        kb.op("dve", lambda e: e.tensor_tensor(out=carry[:, gp, 1:2], in0=pzi[:, TW - 1:TW], in1=cT, op=ALU.mult),
              reads=[r_pzi, r_tab, r_carry[gp]], writes=[r_carry[gp]])
        kb.op("dve", lambda e: e.scalar_tensor_tensor(out=carry[:, gp, 3:4], in0=pzr[:, TW - 1:TW], scalar=sT, in1=carry[:, gp, 1:2],
                                                      op0=ALU.mult, op1=ALU.add), reads=[r_pzr, r_tab, r_carry[gp]], writes=[r_carry[gp]])
        kb.op("dve", lambda e: e.tensor_tensor(out=g("n1"), in0=pzr[:], in1=cc, op=ALU.mult), reads=[r_pzr, r_tab], writes=[rg("n1")])
        kb.op("dve", lambda e: e.tensor_tensor(out=g("n2"), in0=pzi[:], in1=ss, op=ALU.mult), reads=[r_pzi, r_tab], writes=[rg("n2")])
        kb.op("dve", lambda e: e.tensor_tensor(out=g("n3"), in0=pzr[:], in1=ss, op=ALU.mult), reads=[r_pzr, r_tab], writes=[rg("n3")])
        kb.op("dve", lambda e: e.tensor_tensor(out=g("n4"), in0=pzi[:], in1=cc, op=ALU.mult), reads=[r_pzi, r_tab], writes=[rg("n4")])
        kb.op("pool", lambda e: e.tensor_tensor(out=g("xr"), in0=g("n1"), in1=g("n2"), op=ALU.subtract), reads=[rg("n1"), rg("n2")], writes=[rg("xr")])
        kb.op("pool", lambda e: e.tensor_tensor(out=g("xi"), in0=g("n3"), in1=g("n4"), op=ALU.add), reads=[rg("n3"), rg("n4")], writes=[rg("xi")])
        kb.op("pe", lambda e: e.matmul(py[us][:], lhsT=crp[:, gp, :], rhs=g("xr"), start=(gp == 0), stop=False),
              reads=[r_cp, rg("xr")], writes=[r_py[us]])
        kb.op("pe", lambda e: e.matmul(py[us][:], lhsT=ncip[:, gp, :], rhs=g("xi"), start=False, stop=(gp == NG - 1)),
              reads=[r_cp, rg("xi")], writes=[r_py[us]])
        if gp == NG - 1:
            kb.op("dve", lambda e: e.scalar_tensor_tensor(out=yo[us][:], in0=ut[us][:], scalar=dsk_t[:, 0:1], in1=py[us][:],
                                                          op0=ALU.mult, op1=ALU.add), reads=[r_ut[us], r_dsk, r_py[us]], writes=[r_yo[us]])
            kb.dma(y[b, :, ck * TW:(ck + 1) * TW], yo[us][:], reads=[r_yo[us]], q="sp", is_output=True)

    stage_a(0)
    for i in range(len(its)):
        if i + 1 < len(its):
            stage_a(i + 1)
        stage_b(i)
    return kb.finish()


NFFT = 16384
MAGIC = 12582912.0
PI = 3.141592653589793


def build_hy2(NCB=16):
    kb = KB()
    u3 = kb.inp("u3", [3, 2, 64, 128, 2, 128])
    zext = kb.inp("zext", [33, NFFT])
    w1 = kb.inp("w1", [33, 64])
    w2 = kb.inp("w2", [64, 64])
    bf1 = kb.inp("bf1", [64, 3])
    w3 = kb.inp("w3", [64, 4, 128])
    decf = kb.inp("decf", [128, 128, 128])
    decb = kb.inp("decb", [128, 128, 128])
    skp = kb.inp("skp", [64, 2, 128])
    ftab = kb.inp("ftab", [128, 4, 256])
    fri = kb.inp("fri", [128, 2, 128])
    tw = kb.inp("tw", [128, 2, 128])
    zout = kb.outp("zout", [2, 64, 128, 2, 128])

    w1_t, r_w1 = small_load(kb, w1, [33, 64], "w1_t")
    w2_t, r_w2 = small_load(kb, w2, [64, 64], "w2_t")
    bf_t, r_bf = small_load(kb, bf1, [64, 3], "bf_t")
    sk_t, r_sk = small_load(kb, skp, [64, 2, 128], "sk_t")
    ft_t, r_ft = small_load(kb, ftab, [128, 4, 256], "ft_t")
    fr_t, r_fr = small_load(kb, fri, [128, 2, 128], "fr_t")
    tw_t, r_tw = small_load(kb, tw, [128, 2, 128], "tw_t")
    w3_t = kb.sb([64, 4, 128], BF16, "w3_t")
    r_w3 = Res()
    kb.dma(w3_t[:], w3, writes=[r_w3], q="pool")
    onesf = kb.sb([128, 128], F32, "onesf")
    r_ones = Res()
    kb.op("dve", lambda e: e.memset(onesf[:], 1.0), writes=[r_ones])

    ps1 = kb.ps("ps1", (128, 2048))
    ps2 = kb.ps("ps2", (128, 2048))
    r_ps1, r_ps2 = Res(), Res()

    a2T = kb.sb([64, NFFT], BF16, "a2T")
    r_a2 = Res()
    zc = [kb.sb([33, 512], F32, "zc%d" % i) for i in range(2)]
    r_zc = [Res(), Res()]
    arg = kb.sb([64, 512], F32, "arg")
    kf = kb.sb([64, 512], F32, "kfm")
    a1c = kb.sb([64, 512], F32, "a1c")
    r_arg, r_kf, r_a1c = Res(), Res(), Res()

    def sin_rr(src_ps, r_src, bcol, out_ap, r_out):
        kb.op("dve", lambda e: e.tensor_scalar(out=arg[:], in0=src_ps, scalar1=bf_t[:, bcol:bcol + 1], scalar2=bf_t[:, 2:3], op0=ALU.add, op1=ALU.mult),
              reads=[r_src, r_bf], writes=[r_arg])
        kb.op("dve", lambda e: e.tensor_scalar(out=kf[:], in0=arg[:], scalar1=1.0 / TWO_PI, scalar2=MAGIC, op0=ALU.mult, op1=ALU.add), reads=[r_arg], writes=[r_kf])
        kb.op("dve", lambda e: e.tensor_scalar(out=kf[:], in0=kf[:], scalar1=-MAGIC, scalar2=None, op0=ALU.add), reads=[r_kf], writes=[r_kf])
        kb.op("dve", lambda e: e.scalar_tensor_tensor(out=arg[:], in0=kf[:], scalar=-TWO_PI, in1=arg[:], op0=ALU.mult, op1=ALU.add),
              reads=[r_kf, r_arg], writes=[r_arg])
        kb.op("dve", lambda e: e.tensor_scalar(out=arg[:], in0=arg[:], scalar1=-PI, scalar2=PI, op0=ALU.max, op1=ALU.min), reads=[r_arg], writes=[r_arg])
        kb.op("act", lambda e: e.activation(out=out_ap, in_=arg[:], func=AF.Sin), reads=[r_arg], writes=[r_out])

    for ck in range(NFFT // 512):
        s = ck % 2
        kb.dma(zc[s][:], zext[:, ck * 512:(ck + 1) * 512], writes=[r_zc[s]])
        kb.op("pe", lambda e: e.matmul(ps1[0:64, 0:512], lhsT=w1_t[:], rhs=zc[s][:], start=True, stop=True), reads=[r_w1, r_zc[s]], writes=[r_ps1])
        sin_rr(ps1[0:64, 0:512], r_ps1, 0, a1c[:], r_a1c)
        kb.op("pe", lambda e: e.matmul(ps2[0:64, 0:512], lhsT=w2_t[:], rhs=a1c[:], start=True, stop=True), reads=[r_w2, r_a1c], writes=[r_ps2])
        sin_rr(ps2[0:64, 0:512], r_ps2, 1, a2T[:, ck * 512:(ck + 1) * 512], r_a2)

    dft = kb.sb([128, 8, 128], F32, "dft")
    dbt = kb.sb([128, 8, 128], F32, "dbt")
    r_dft, r_dbt = Res(), Res()
    kft = kb.sb([128, 2, 8, 128], F32, "kft")
    r_kft = Res()
    ktmp = kb.sb([128, 8, 64], F32, "ktmp")
    r_ktmp = Res()
    part = kb.sb([128, 16], F32, "part")
    rS = kb.sb([128, 16], F32, "rS")
    r_part, r_rS = Res(), Res()
    KR = kb.sb([128, 2, 8, 128], F32, "KR")
    KI = kb.sb([128, 2, 8, 128], F32, "KI")
    r_K = Res()
    A1 = kb.sb([128, 8, 2, 128], F32, "A1")
    A2 = kb.sb([128, 8, 2, 128], F32, "A2")
    r_A = Res()
    P1 = kb.sb([128, 8, 2, 128], F32, "P1")
    P2 = kb.sb([128, 8, 2, 128], F32, "P2")
    r_P = Res()
    M = [kb.sb([128, 8, 128], F32, "mm%d" % i) for i in range(4)]
    r_M = Res()
    X = kb.sb([64, 8, 2, 128], F32, "X")
    G = kb.sb([64, 8, 2, 128], F32, "G")
    Z1 = kb.sb([64, 8, 2, 128], F32, "Z1")
    Z2 = kb.sb([64, 8, 2, 128], F32, "Z2")
    T_ = kb.sb([64, 8, 2, 128], F32, "Tt")
    r_X, r_G, r_Z1, r_Z2, r_T = Res(), Res(), Res(), Res(), Res()
    twr_b = tw_t[:, 0, :].unsqueeze(1).to_broadcast([128, 8, 128])
    twi_b = tw_t[:, 1, :].unsqueeze(1).to_broadcast([128, 8, 128])

    def cplx_evac(ps, r_ps, tr, ti, r_t, conj, O1, O2, r_O, arr):
        v = ps.rearrange("k (a c n) -> k a c n", a=8, c=2)
        pre, pim = v[:, :, 0, :], v[:, :, 1, :]
        rd = [r_ps, r_t]
        kb.op("dve", lambda e: e.tensor_tensor(out=M[0][:], in0=pre, in1=tr, op=ALU.mult), reads=rd, writes=[r_M])
        kb.op("dve", lambda e: e.tensor_tensor(out=M[1][:], in0=pim, in1=ti, op=ALU.mult), reads=rd, writes=[r_M])
        kb.op("dve", lambda e: e.tensor_tensor(out=M[2][:], in0=pre, in1=ti, op=ALU.mult), reads=rd, writes=[r_M])
        kb.op("dve", lambda e: e.tensor_tensor(out=M[3][:], in0=pim, in1=tr, op=ALU.mult), reads=rd, writes=[r_M])
        if not conj:
            kb.op("pool", lambda e: e.tensor_tensor(out=O1[:, :, 0, :], in0=M[0][:], in1=M[1][:], op=ALU.subtract), reads=[r_M], writes=[r_O])
            kb.op("pool", lambda e: e.tensor_tensor(out=O1[:, :, 1, :], in0=M[2][:], in1=M[3][:], op=ALU.add), reads=[r_M], writes=[r_O])
        else:
            kb.op("pool", lambda e: e.tensor_tensor(out=O1[:, :, 0, :], in0=M[0][:], in1=M[1][:], op=ALU.add), reads=[r_M], writes=[r_O])
            kb.op("pool", lambda e: e.tensor_tensor(out=O1[:, :, 1, :], in0=M[3][:], in1=M[2][:], op=ALU.subtract), reads=[r_M], writes=[r_O])
        if arr == "fwd":
            kb.op("act", lambda e: e.activation(out=O2[:, :, 0, :], in_=O1[:, :, 1, :], func=AF.Copy, scale=-1.0), reads=[r_O], writes=[r_O])
            kb.op("act", lambda e: e.activation(out=O2[:, :, 1, :], in_=O1[:, :, 0, :], func=AF.Copy), reads=[r_O], writes=[r_O])
        else:
            kb.op("act", lambda e: e.activation(out=O2[:, :, 0, :], in_=O1[:, :, 1, :], func=AF.Copy), reads=[r_O], writes=[r_O])
            kb.op("act", lambda e: e.activation(out=O2[:, :, 1, :], in_=O1[:, :, 0, :], func=AF.Copy, scale=-1.0), reads=[r_O], writes=[r_O])

    def stage2(I1, I2, r_I, M_rows):
        f1 = I1.rearrange("k a c n -> k (a c n)")
        f2 = I2.rearrange("k a c n -> k (a c n)")
        for q in range(4):
            kb.op("pe", lambda e, q=q: e.matmul(ps2[0:M_rows, q * 512:(q + 1) * 512], lhsT=fr_t[:, 0, 0:M_rows], rhs=f1[:, q * 512:(q + 1) * 512],
                                                start=True, stop=False), reads=[r_fr, r_I], writes=[r_ps2])
            kb.op("pe", lambda e, q=q: e.matmul(ps2[0:M_rows, q * 512:(q + 1) * 512], lhsT=fr_t[:, 1, 0:M_rows], rhs=f2[:, q * 512:(q + 1) * 512],
                                                start=False, stop=True), reads=[r_fr, r_I], writes=[r_ps2])

    def conv_pass(src, r_src, o):
        for c in range(8):
            kb.op("pe", lambda e, c=c: e.matmul(ps1[:, c * 256:(c + 1) * 256], lhsT=src[:, c, 0, :], rhs=ft_t[0:64, 0, :], start=True, stop=False),
                  reads=[r_src, r_ft], writes=[r_ps1])
            kb.op("pe", lambda e, c=c: e.matmul(ps1[:, c * 256:(c + 1) * 256], lhsT=src[:, c, 1, :], rhs=ft_t[0:64, 1, :], start=False, stop=True),
                  reads=[r_src, r_ft], writes=[r_ps1])
        cplx_evac(ps1, r_ps1, twr_b, twi_b, r_tw, False, A1, A2, r_A, "fwd")
        stage2(A1, A2, r_A, 128)
        cplx_evac(ps2, r_ps2, KR[:, o], KI[:, o], r_K, False, P1, P2, r_P, "inv")
        for c in range(8):
            kb.op("pe", lambda e, c=c: e.matmul(ps1[:, c * 256:(c + 1) * 256], lhsT=P1[:, c, 0, :], rhs=ft_t[:, 2, :], start=True, stop=False),
                  reads=[r_P, r_ft], writes=[r_ps1])
            kb.op("pe", lambda e, c=c: e.matmul(ps1[:, c * 256:(c + 1) * 256], lhsT=P1[:, c, 1, :], rhs=ft_t[:, 3, :], start=False, stop=True),
                  reads=[r_P, r_ft], writes=[r_ps1])
        cplx_evac(ps1, r_ps1, twr_b, twi_b, r_tw, True, A1, A2, r_A, "inv")
        stage2(A1, A2, r_A, 64)

    for cb in range(NCB):
        ch0 = 8 * cb
        kb.dma(dft[:], decf[:, ch0:ch0 + 8, :], writes=[r_dft])
        kb.dma(dbt[:], decb[:, ch0:ch0 + 8, :], writes=[r_dbt])
        for p in range(128):
            psx, r_psx = (ps1, r_ps1) if p < 64 else (ps2, r_ps2)
            pp = p % 64
            kb.op("pe", lambda e, p=p, pp=pp, psx=psx: e.matmul(psx[:, pp * 32:(pp + 1) * 32], lhsT=a2T[:, p:NFFT:128], rhs=w3_t[:, :, ch0:ch0 + 8],
                                                              start=True, stop=True), reads=[r_a2, r_w3], writes=[r_psx])
        for o in range(2):
            for half in range(2):
                psx, r_psx = (ps1, r_ps1) if half == 0 else (ps2, r_ps2)
                vw = psx.rearrange("i (p q c) -> i q c p", q=4, c=8)
                hs = slice(half * 64, half * 64 + 64)
                kb.op("dve", lambda e, vw=vw, hs=hs, o=o: e.tensor_tensor(out=kft[:, o, :, hs], in0=vw[:, 2 * o], in1=dft[:, :, hs], op=ALU.mult),
                      reads=[r_psx, r_dft], writes=[r_kft])
                kb.op("dve", lambda e, vw=vw, hs=hs, o=o: e.tensor_tensor(out=ktmp[:], in0=vw[:, 2 * o + 1], in1=dbt[:, :, hs], op=ALU.mult),
                      reads=[r_psx, r_dbt], writes=[r_ktmp])
                kb.op("pool", lambda e, hs=hs, o=o: e.tensor_tensor(out=kft[:, o, :, hs], in0=kft[:, o, :, hs], in1=ktmp[:], op=ALU.add),
                      reads=[r_ktmp, r_kft], writes=[r_kft])
        kb.op("dve", lambda e: e.tensor_reduce(out=part[:], in_=kft[:].rearrange("i o c p -> i (o c) p"), axis=AX.X, op=ALU.add, apply_absolute_value=True),
              reads=[r_kft], writes=[r_part])
        kb.op("pe", lambda e: e.matmul(ps1[:, 0:16], lhsT=onesf[:], rhs=part[:], start=True, stop=True), reads=[r_ones, r_part], writes=[r_ps1])
        kb.op("dve", lambda e: e.tensor_scalar(out=rS[:], in0=ps1[:, 0:16], scalar1=float(NFFT), scalar2=None, op0=ALU.mult), reads=[r_ps1], writes=[r_rS])
        kb.op("dve", lambda e: e.reciprocal(out=rS[:], in_=rS[:]), reads=[r_rS], writes=[r_rS])
        for o in range(2):
            for c in range(8):
                kb.op("pe", lambda e, c=c, o=o: e.matmul(ps1[:, c * 256:(c + 1) * 256], lhsT=kft[:, o, c, :], rhs=ft_t[:, 0, :], start=True, stop=True),
                      reads=[r_kft, r_ft], writes=[r_ps1])
            cplx_evac(ps1, r_ps1, twr_b, twi_b, r_tw, False, A1, A2, r_A, "fwd")
            stage2(A1, A2, r_A, 128)
            v2 = ps2.rearrange("k (a c n) -> k a c n", a=8, c=2)
            rsb = rS[:, o * 8:(o + 1) * 8].unsqueeze(2).to_broadcast([128, 8, 128])
            kb.op("dve", lambda e, o=o, v2=v2, rsb=rsb: e.tensor_tensor(out=KR[:, o], in0=v2[:, :, 0, :], in1=rsb, op=ALU.mult),
                  reads=[r_ps2, r_rS], writes=[r_K])
            kb.op("dve", lambda e, o=o, v2=v2, rsb=rsb: e.tensor_tensor(out=KI[:, o], in0=v2[:, :, 1, :], in1=rsb, op=ALU.mult),
                  reads=[r_ps2, r_rS], writes=[r_K])
        for pr in range(2):
            kb.dma(X[:], u3[0, pr, :, ch0:ch0 + 8, :, :], writes=[r_X])
            kb.dma(G[:], u3[1, pr, :, ch0:ch0 + 8, :, :], writes=[r_G])
            conv_pass(X, r_X, 0)
            sk0 = sk_t[:, 0, ch0:ch0 + 8].unsqueeze(2).to_broadcast([64, 8, 256])
            sk1 = sk_t[:, 1, ch0:ch0 + 8].unsqueeze(2).to_broadcast([64, 8, 256])
            fl = lambda t: t[:].rearrange("i a c n -> i a (c n)")
            yv = ps2[0:64, :].rearrange("i (a n) -> i a n", a=8)
            kb.op("dve", lambda e: e.tensor_tensor(out=fl(T_), in0=fl(X), in1=sk0, op=ALU.mult), reads=[r_X, r_sk], writes=[r_T])
            kb.op("dve", lambda e: e.tensor_tensor(out=fl(T_), in0=fl(T_), in1=yv, op=ALU.add), reads=[r_T, r_ps2], writes=[r_T])
            kb.op("pool", lambda e: e.tensor_tensor(out=fl(Z1), in0=fl(T_), in1=fl(G), op=ALU.mult), reads=[r_T, r_G], writes=[r_Z1])
            kb.dma(G[:], u3[2, pr, :, ch0:ch0 + 8, :, :], writes=[r_G])
            conv_pass(Z1, r_Z1, 1)
            kb.op("dve", lambda e: e.tensor_tensor(out=fl(T_), in0=fl(Z1), in1=sk1, op=ALU.mult), reads=[r_Z1, r_sk], writes=[r_T])
            kb.op("dve", lambda e: e.tensor_tensor(out=fl(T_), in0=fl(T_), in1=yv, op=ALU.add), reads=[r_T, r_ps2], writes=[r_T])
            kb.op("pool", lambda e: e.tensor_tensor(out=fl(Z2), in0=fl(T_), in1=fl(G), op=ALU.mult), reads=[r_T, r_G], writes=[r_Z2])
            kb.dma(zout[pr, :, ch0:ch0 + 8, :, :], Z2[:], reads=[r_Z2], q="sp", is_output=True)
    return kb.finish()


def build_tok(kind, NT):
    kb = KB()
    halo = 2 if kind == "hy1" else 0
    TW = 256 if kind == "hy1" else 512
    W = TW + halo
    xT = kb.inp("xT", [128, 8, NT + halo])
    c_in = kb.inp("c2", [128, 8, 2])
    adaw = kb.inp("adaw", [1024, 3 * D])
    adab = kb.inp("adab", [128, 24])
    pss = kb.ps("pmod")
    r_pss = Res()
    mod, r_mod = emit_mod(kb, c_in, adaw, adab, 24, "mod", ps=pss, r_ps_in=r_pss)
    xt = [kb.sb([128, 8, W], F32, "xt%d" % i) for i in range(2)]
    r_xt = [Res(), Res()]
    tmp = kb.sb([128, W], F32, "tmp")
    r_tmp = Res()
    pa = [kb.ps("pa%d" % i) for i in range(2)]
    pb = [kb.ps("pb%d" % i) for i in range(2)]
    r_pa = [Res(), Res()]
    r_pb = [Res(), Res()]
    if kind in ("hy1", "s5a"):
        ng = kb.inp("ng", [128, 8])
        ng_t, r_ng = small_load(kb, ng, [128, 8], "ng_t")
        gs, r_gs = emit_gs(kb, mod, r_mod, 8, ng_t, r_ng, "gs")
        nrm = Norm(kb, W, "nrm").init_eps()
    if kind == "hy1":
        win = kb.inp("win", [D, 3 * D])
        cw = kb.inp("cw", [128, 72])
        cb = kb.inp("cb", [128, 24])
        msk = kb.inp("msk", [128, 2])
        out = kb.outp("out", [128, 24, NT])
        r_w = Res()
        w_t = load_cast_weight(kb, win, 8, 3 * D, "w_t", r_w)
        cw_t, r_cw = small_load(kb, cw, [128, 72], "cw_t")
        cb_t, r_cb = small_load(kb, cb, [128, 24], "cb_t")
        mk_t, r_mk = small_load(kb, msk, [128, 2], "mk_t")
        h = kb.sb([128, 8, W], BF16, "h")
        r_h = Res()
        uo = kb.sb([128, 24, TW], F32, "uo")
        r_uo = Res()
    elif kind == "s5a":
        out = kb.outp("out", [128, 8, NT])
        ho = kb.sb([128, 8, W], F32, "ho")
        r_ho = Res()
    elif kind == "hypost":
        zin = kb.inp("z", [128, 8, NT])
        wout = kb.inp("wout", [D, D])
        out = kb.outp("out", [128, 8, NT])
        r_w = Res()
        w_t = load_cast_weight(kb, wout, 8, D, "w_t", r_w)
        zt = kb.sb([128, 8, W], F32, "zt")
        zb = kb.sb([128, 8, W], BF16, "zb")
        r_zt, r_zb = Res(), Res()
        xo = kb.sb([128, 8, W], F32, "xo")
        r_xo = Res()
    elif kind == "s5post":
        yf = kb.inp("yf", [128, 8, NT])
        yb = kb.inp("yb", [128, 8, NT])
        wglu = kb.inp("wglu", [D, 2 * D])
        out = kb.outp("out", [128, 8, NT])
        r_w = Res()
        w_t = load_cast_weight(kb, wglu, 8, 2 * D, "w_t", r_w)
        y1 = kb.sb([128, 8, W], F32, "y1")
        y2 = kb.sb([128, 8, W], F32, "y2")
        r_y1, r_y2 = Res(), Res()
        t3 = kb.sb([128, 8, W], F32, "t3")
        r_t3 = Res()
        gl = kb.sb([128, 8, W], BF16, "gl")
        r_gl = Res()
        sg = kb.sb([128, W], F32, "sg")
        r_sg = Res()
        xo = kb.sb([128, 8, W], F32, "xo")
        r_xo = Res()
    ntiles = (NT + TW - 1) // TW
    for ti in range(ntiles):
        o0 = ti * TW
        o1 = min(NT, o0 + TW)
        w = o1 - o0 + halo
        wo_ = o1 - o0
        s = ti % 2
        kb.dma(xt[s][:, :, :w], xT[:, :, o0:o0 + w], writes=[r_xt[s]])
        if kind in ("hy1", "s5a"):
            rstd, r_rstd = nrm.emit(xt[s][:, :, :w], r_xt[s], w)
            for k in range(8):
                kb.op("dve", lambda e, k=k: e.scalar_tensor_tensor(out=tmp[:, :w], in0=xt[s][:, k, :w], scalar=gs[:, k:k + 1],
                                                                    in1=rstd, op0=ALU.mult, op1=ALU.mult),
                      reads=[r_xt[s], r_gs, r_rstd], writes=[r_tmp])
                if kind == "hy1":
                    kb.op("act", lambda e, k=k: e.activation(out=h[:, k, :w], in_=tmp[:, :w], func=AF.Identity, bias=mod[:, k:k + 1], scale=1.0),
                          reads=[r_tmp, r_mod], writes=[r_h])
                else:
                    kb.op("act", lambda e, k=k: e.activation(out=ho[:, k, :w], in_=tmp[:, :w], func=AF.Identity, bias=mod[:, k:k + 1], scale=1.0),
                          reads=[r_tmp, r_mod], writes=[r_ho])
        if kind == "s5a":
            kb.dma(out[:, :, o0:o1], ho[:, :, :w], reads=[r_ho], q="sp", is_output=True)
        elif kind == "hy1":
            if ti == 0:
                kb.op("dve", lambda e: e.tensor_scalar(out=h[:, :, 0:1], in0=h[:, :, 0:1], scalar1=mk_t[:, 0:1], scalar2=None, op0=ALU.mult),
                      reads=[r_mk, r_h], writes=[r_h])
            if ti == ntiles - 1:
                kb.op("dve", lambda e: e.tensor_scalar(out=h[:, :, w - 1:w], in0=h[:, :, w - 1:w], scalar1=mk_t[:, 1:2], scalar2=None, op0=ALU.mult),
                      reads=[r_mk, r_h], writes=[r_h])
            for j in range(24):
                q = j % 2
                for k in range(8):
                    kb.op("pe", lambda e, k=k, j=j, q=q: e.matmul(pa[q][:, :w], lhsT=w_t[:, k, j * 128:(j + 1) * 128], rhs=h[:, k, :w],
                                                                   start=(k == 0), stop=(k == 7)), reads=[r_w, r_h], writes=[r_pa[q]])
                kb.op("act", lambda e, j=j, q=q: e.activation(out=uo[:, j, :wo_], in_=pa[q][:, 1:w - 1], func=AF.Identity,
                                                              scale=cw_t[:, 3 * j + 1:3 * j + 2], bias=cb_t[:, j:j + 1]),
                      reads=[r_pa[q], r_cw, r_cb], writes=[r_uo])
                kb.op("dve", lambda e, j=j, q=q: e.scalar_tensor_tensor(out=uo[:, j, :wo_], in0=pa[q][:, 0:w - 2], scalar=cw_t[:, 3 * j:3 * j + 1],
                                                                        in1=uo[:, j, :wo_], op0=ALU.mult, op1=ALU.add),
                      reads=[r_pa[q], r_cw, r_uo], writes=[r_uo])
                kb.op("dve", lambda e, j=j, q=q: e.scalar_tensor_tensor(out=uo[:, j, :wo_], in0=pa[q][:, 2:w], scalar=cw_t[:, 3 * j + 2:3 * j + 3],
                                                                        in1=uo[:, j, :wo_], op0=ALU.mult, op1=ALU.add),
                      reads=[r_pa[q], r_cw, r_uo], writes=[r_uo])
            kb.dma(out[:, :, o0:o1], uo[:, :, :wo_], reads=[r_uo], q="sp", is_output=True)
        elif kind == "hypost":
            kb.dma(zt[:, :, :w], zin[:, :, o0:o1], writes=[r_zt])
            kb.op("act", lambda e: e.activation(out=zb[:, :, :w], in_=zt[:, :, :w], func=AF.Copy), reads=[r_zt], writes=[r_zb])
            for m in range(8):
                q = m % 2
                for k in range(8):
                    kb.op("pe", lambda e, k=k, m=m, q=q: e.matmul(pa[q][:, :w], lhsT=w_t[:, k, m * 128:(m + 1) * 128], rhs=zb[:, k, :w],
                                                                   start=(k == 0), stop=(k == 7)), reads=[r_w, r_zb], writes=[r_pa[q]])
                kb.op("dve", lambda e, m=m, q=q: e.scalar_tensor_tensor(out=xo[:, m, :w], in0=pa[q][:, :w], scalar=mod[:, 16 + m:17 + m],
                                                                        in1=xt[s][:, m, :w], op0=ALU.mult, op1=ALU.add),
                      reads=[r_pa[q], r_mod, r_xt[s]], writes=[r_xo])
            kb.dma(out[:, :, o0:o1], xo[:, :, :w], reads=[r_xo], q="sp", is_output=True)
        elif kind == "s5post":
            kb.dma(y1[:, :, :w], yf[:, :, o0:o1], writes=[r_y1])
            kb.dma(y2[:, :, :w], yb[:, :, o0:o1], writes=[r_y2])
            a3 = lambda t: t[:, :, :w]
            kb.op("dve", lambda e: e.tensor_tensor(out=a3(y1), in0=a3(y1), in1=a3(y2), op=ALU.add), reads=[r_y1, r_y2], writes=[r_y1])
            kb.op("pool", lambda e: e.tensor_tensor(out=a3(t3), in0=a3(y1), in1=a3(y1), op=ALU.mult), reads=[r_y1], writes=[r_t3])
            kb.op("dve", lambda e: e.tensor_scalar(out=a3(t3), in0=a3(t3), scalar1=0.044715, scalar2=1.0, op0=ALU.mult, op1=ALU.add), reads=[r_t3], writes=[r_t3])
            kb.op("pool", lambda e: e.tensor_tensor(out=a3(t3), in0=a3(t3), in1=a3(y1), op=ALU.mult), reads=[r_y1, r_t3], writes=[r_t3])
            kb.op("act", lambda e: e.activation(out=a3(t3), in_=a3(t3), func=AF.Sigmoid, scale=1.5957691216057308), reads=[r_t3], writes=[r_t3])
            kb.op("dve", lambda e: e.tensor_tensor(out=a3(gl), in0=a3(t3), in1=a3(y1), op=ALU.mult), reads=[r_y1, r_t3], writes=[r_gl])
            for m in range(8):
                q = m % 2
                for k in range(8):
                    kb.op("pe", lambda e, k=k, m=m, q=q: e.matmul(pa[q][:, :w], lhsT=w_t[:, k, m * 128:(m + 1) * 128], rhs=gl[:, k, :w],
                                                                   start=(k == 0), stop=(k == 7)), reads=[r_w, r_gl], writes=[r_pa[q]])
                for k in range(8):
                    kb.op("pe", lambda e, k=k, m=m, q=q: e.matmul(pb[q][:, :w], lhsT=w_t[:, k, D + m * 128:D + (m + 1) * 128], rhs=gl[:, k, :w],
                                                                   start=(k == 0), stop=(k == 7)), reads=[r_w, r_gl], writes=[r_pb[q]])
                kb.op("act", lambda e, q=q: e.activation(out=sg[:, :w], in_=pb[q][:, :w], func=AF.Sigmoid), reads=[r_pb[q]], writes=[r_sg])
                kb.op("dve", lambda e, q=q: e.tensor_tensor(out=sg[:, :w], in0=sg[:, :w], in1=pa[q][:, :w], op=ALU.mult), reads=[r_sg, r_pa[q]], writes=[r_sg])
                kb.op("dve", lambda e, m=m: e.scalar_tensor_tensor(out=xo[:, m, :w], in0=sg[:, :w], scalar=mod[:, 16 + m:17 + m],
                                                                   in1=xt[s][:, m, :w], op0=ALU.mult, op1=ALU.add),
                      reads=[r_sg, r_mod, r_xt[s]], writes=[r_xo])
            kb.dma(out[:, :, o0:o1], xo[:, :, :w], reads=[r_xo], q="sp", is_output=True)
    return kb.finish()


import math
import numpy as np
L = 8192; NFFT = 16384; Dm = 1024

def hy_consts():
    t = np.linspace(0.0, 1.0, L, dtype=np.float32)[:, None]
    bands = 16
    w = (2.0 * math.pi * np.arange(L, dtype=np.float32) / L).astype(np.float32)
    f = np.linspace(1e-4, bands - 1, bands, dtype=np.float32)
    ang = w[:, None] * f[None, :]
    z = np.concatenate([t, np.cos(ang), -np.sin(ang)], -1).astype(np.float32)
    deltas = np.abs(np.linspace(math.log(1e-2) / 1.5, math.log(1e-2) / 0.3, Dm, dtype=np.float32))
    decay = np.exp(-t * deltas[None, :]).astype(np.float32)
    idx = np.arange(NFFT)
    src = np.where(idx < L, idx, np.where(idx == L, 0, 2 * L - idx))
    zext = np.ascontiguousarray(z[src].T)
    dec_f = np.where((idx < L)[:, None], decay[src], 0.0).astype(np.float32)
    dec_b = np.where((idx > L)[:, None], decay[src], 0.0).astype(np.float32)
    k = np.arange(128)
    F = np.exp(-2j * np.pi * np.outer(k, k) / 128)
    Fr = F.real.astype(np.float32); Fi = F.imag.astype(np.float32)
    ftab = np.stack([np.concatenate([Fr, Fi], 1), np.concatenate([-Fi, Fr], 1),
                     np.concatenate([Fr, -Fi], 1), np.concatenate([Fi, Fr], 1)], 1)
    fri = np.stack([Fr, Fi], 1)
    T = np.exp(-2j * np.pi * np.outer(k, k) / NFFT)
    tw = np.stack([T.real.astype(np.float32), T.imag.astype(np.float32)], 1)
    return dict(z=z, decay=decay, zext=zext, dec_f=dec_f, dec_b=dec_b, ftab=np.ascontiguousarray(ftab.astype(np.float32)),
                fri=np.ascontiguousarray(fri), tw=np.ascontiguousarray(tw))

def dec_core(dec, core):
    return np.ascontiguousarray(dec[:, 128 * core:128 * core + 128].reshape(128, 128, 128).transpose(0, 2, 1))

def to_u3(u, core):
    a = u.reshape(2, 2, 64, 128, 3, Dm)[..., 128 * core:128 * core + 128]
    return np.ascontiguousarray(a.transpose(4, 0, 2, 5, 1, 3))

def from_zout(zs):
    out = np.zeros((4, L, Dm), np.float32)
    for core, zc in enumerate(zs):
        a = zc.transpose(0, 3, 1, 4, 2)
        out[:, :, 128 * core:128 * core + 128] = a.reshape(4, L, 128)
    return out


_PROGS = {}


def _prog(key, fn):
    if key not in _PROGS:
        _PROGS[key] = fn()
    return _PROGS[key]


def _fm(x):
    T, C = x.shape
    return np.ascontiguousarray(x.T.reshape(C // 128, 128, T).transpose(1, 0, 2))


def _unfm(a):
    return np.ascontiguousarray(a.transpose(2, 1, 0).reshape(a.shape[2], -1))


def _vfm(v):
    return np.ascontiguousarray(np.asarray(v, np.float32).reshape(-1, 128).T)


def _run(nc, in_maps):
    res = run_bass_kernel_spmd(nc, in_maps, core_ids=list(range(8)))
    return res.results


NTC = 4096
SEQ = 8192
NBATCH = 4
SWAP = np.arange(64) ^ 1


def _rope_fm(Ls):
    rows = Ls // 64
    nf = 16
    inv = (1.0 / (np.float32(10000.0) ** (np.arange(nf, dtype=np.float32) / np.float32(nf)))).astype(np.float32)
    r = np.arange(rows, dtype=np.float32)
    col = np.arange(64, dtype=np.float32)
    ang_r = np.broadcast_to(r[:, None, None] * inv, (rows, 64, nf))
    ang_c = np.broadcast_to(col[None, :, None] * inv, (rows, 64, nf))
    ang = np.concatenate([ang_r, ang_c], -1).reshape(Ls, 2 * nf).astype(np.float32)
    cos, sin = np.cos(ang), np.sin(ang)
    C = np.repeat(cos, 2, axis=1).T
    S = np.repeat(sin, 2, axis=1).T.copy()
    S[0::2] *= -1
    return np.ascontiguousarray(np.stack([np.concatenate([C, C], 0), np.concatenate([S, S], 0)], 0).astype(np.float32))


def _common(c, adaw, adab, b):
    return {"c2": np.ascontiguousarray(np.repeat(_vfm(c[b])[:, :, None], 2, axis=2)), "adaw": np.ascontiguousarray(adaw), "adab": _vfm(adab)}


def _halo_x(xs, b, half):
    xp = np.pad(xs[b], ((1, 1), (0, 0)))
    return _fm(xp[half * NTC: half * NTC + NTC + 2])


def _msk(half):
    return np.tile(np.array([[0.0 if half == 0 else 1.0, 1.0 if half == 0 else 0.0]], np.float32), (128, 1))


def _gather_tok(results, key="out"):
    xs = np.zeros((NBATCH, SEQ, D), np.float32)
    for core in range(8):
        b, half = core // 2, core % 2
        xs[b, half * NTC:(half + 1) * NTC] = _unfm(results[core][key])
    return xs


def run_attn(xs, c, adaw, adab, ng, w_qkv, w_o, qg, kg):
    nc = _prog("attn", lambda: build_attn(L=SEQ, NQ=NTC))
    wq_ = w_qkv[:, :1024].reshape(D, 16, 64)
    wk_ = w_qkv[:, 1024:1280].reshape(D, 4, 64)
    wv_ = w_qkv[:, 1280:]
    wq_p = wq_[:, QPERM, :]
    wq_all = np.ascontiguousarray(np.concatenate([wq_p.reshape(D, 1024), wq_p[:, :, SWAP].reshape(D, 1024)], 1))
    wkv = np.ascontiguousarray(np.concatenate([wk_.reshape(D, 256), wk_[:, :, SWAP].reshape(D, 256), wv_], 1))
    wo = np.ascontiguousarray(w_o)
    gains = np.ascontiguousarray(np.stack([np.tile(qg, 2), np.tile(qg[SWAP], 2), np.tile(kg, 2), np.tile(kg[SWAP], 2)], 1).astype(np.float32))
    rp = _rope_fm(SEQ)
    maps = []
    for core in range(8):
        b, half = core // 2, core % 2
        xf = _fm(xs[b])
        m = _common(c, adaw, adab, b)
        m.update({"xall": xf, "xq": np.ascontiguousarray(xf[:, :, half * NTC:(half + 1) * NTC]), "ng": _vfm(ng), "wkv": wkv, "wq": wq_all,
                  "wo": wo, "gains": gains, "ropek": rp, "ropeq": np.ascontiguousarray(rp[:, :, half * NTC:(half + 1) * NTC])})
        maps.append(m)
    return _gather_tok(_run(nc, maps))


def run_ffn(xs, c, adaw, adab, ng, wup, cw, cb, wdn, final_g=None):
    final = final_g is not None
    nc = _prog("ffn%d" % final, lambda: build_ffn(NTC, final=final))
    cwl = np.ascontiguousarray(cw.T.reshape(22, 128, 3).transpose(1, 0, 2).reshape(128, 66))
    maps = []
    for core in range(8):
        b, half = core // 2, core % 2
        m = _common(c, adaw, adab, b)
        m.update({"xT": _halo_x(xs, b, half), "ng": _vfm(ng), "wup": np.ascontiguousarray(wup), "wdn": np.ascontiguousarray(wdn),
                  "cw": cwl, "cb": _vfm(cb), "msk": _msk(half)})
        if final:
            m["fg"] = _vfm(final_g)
        maps.append(m)
    return _gather_tok(_run(nc, maps))


def run_hyena(xs, c, adaw, adab, ng, w_in, conv_w, conv_b, f_w1, f_b1, f_w2, f_b2, f_w3, f_freq, skip, w_out):
    nc1 = _prog("hy1", lambda: build_tok("hy1", NTC))
    cwl = np.ascontiguousarray(conv_w.T.reshape(24, 128, 3).transpose(1, 0, 2).reshape(128, 72))
    maps = []
    for core in range(8):
        b, half = core // 2, core % 2
        m = _common(c, adaw, adab, b)
        m.update({"xT": _halo_x(xs, b, half), "ng": _vfm(ng), "win": np.ascontiguousarray(w_in), "cw": cwl, "cb": _vfm(conv_b), "msk": _msk(half)})
        maps.append(m)
    u = _gather_tok_c(_run(nc1, maps), 3 * D)
    nc2 = _prog("hy2", lambda: build_hy2(NCB=16))
    C = hy_consts()
    maps = []
    for core in range(8):
        sl = slice(128 * core, 128 * core + 128)
        maps.append(dict(u3=to_u3(u, core), zext=C["zext"], w1=np.ascontiguousarray(f_w1), w2=np.ascontiguousarray(f_w2),
                         bf1=np.ascontiguousarray(np.stack([f_b1, f_b2, f_freq], 1)),
                         w3=np.ascontiguousarray(f_w3.reshape(64, 4, D)[:, :, sl]), decf=dec_core(C["dec_f"], core), decb=dec_core(C["dec_b"], core),
                         skp=np.ascontiguousarray(np.tile(skip[None, :, sl], (64, 1, 1))), ftab=C["ftab"], fri=C["fri"], tw=C["tw"]))
    r = _run(nc2, maps)
    z = from_zout([r[cc]["zout"] for cc in range(8)])
    nc3 = _prog("hypost", lambda: build_tok("hypost", NTC))
    maps = []
    for core in range(8):
        b, half = core // 2, core % 2
        sl = slice(half * NTC, (half + 1) * NTC)
        m = _common(c, adaw, adab, b)
        m.update({"xT": _fm(xs[b, sl]), "z": _fm(z[b, sl]), "wout": np.ascontiguousarray(w_out)})
        maps.append(m)
    return _gather_tok(_run(nc3, maps))


def _gather_tok_c(results, C_):
    o = np.zeros((NBATCH, SEQ, C_), np.float32)
    for core in range(8):
        b, half = core // 2, core % 2
        o[b, half * NTC:(half + 1) * NTC] = _unfm(results[core]["out"])
    return o


def _s5_params(A_re, A_im, log_dt, B_re, B_im, C_re, C_im, core, TW=512):
    gs = [8 * core + k for k in range(8)]
    are = np.zeros((128, 8), np.float32); aim = np.zeros((128, 8), np.float32); ldt = np.zeros((128, 8), np.float32)
    bre = np.zeros((128, 4, 128), np.float32); bim = np.zeros((128, 4, 128), np.float32)
    cre = np.zeros((128, 4, 128), np.float32); cim = np.zeros((128, 4, 128), np.float32)
    for gp in range(4):
        for g2 in range(2):
            g = gs[2 * gp + g2]
            sl = slice(64 * g2, 64 * g2 + 64)
            are[sl, gp] = A_re[g]; aim[sl, gp] = A_im[g]; ldt[sl, gp] = log_dt[g]
            rows = slice(16 * (2 * gp + g2), 16 * (2 * gp + g2) + 16)
            bre[rows, gp, sl] = B_re[g].T
            bim[rows, gp, sl] = B_im[g].T
            cre[sl, gp, rows] = C_re[g].T
            cim[sl, gp, rows] = C_im[g].T
    are[:, 4:] = are[:, :4]; aim[:, 4:] = aim[:, :4]; ldt[:, 4:] = ldt[:, :4]
    tau = np.tile(np.arange(TW + 1, dtype=np.float32)[None], (128, 1))
    return dict(are=are, aim=aim, ldt=ldt, bre=bre, bim=bim, cre=cre, cim=cim, tau=tau)


def run_s5(xs, c, adaw, adab, ng, A_re, A_im, log_dt, B_re, B_im, C_re, C_im, d_skip, w_glu):
    nc1 = _prog("s5a", lambda: build_tok("s5a", NTC))
    maps = []
    for core in range(8):
        b, half = core // 2, core % 2
        m = _common(c, adaw, adab, b)
        m.update({"xT": _fm(xs[b, half * NTC:(half + 1) * NTC]), "ng": _vfm(ng)})
        maps.append(m)
    h = _gather_tok(_run(nc1, maps))
    nc2 = _prog("s5", lambda: build_s5(L=SEQ, NB=NBATCH))
    ys = []
    for d in range(2):
        hd = h if d == 0 else h[:, ::-1]
        maps = []
        for core in range(8):
            m = _s5_params(A_re[d], A_im[d], log_dt[d], B_re[d], B_im[d], C_re[d], C_im[d], core)
            m["hT"] = np.ascontiguousarray(hd[:, :, 128 * core:128 * core + 128].transpose(0, 2, 1))
            dk = d_skip[128 * core:128 * core + 128, None] if d == 0 else np.zeros((128, 1), np.float32)
            m["dsk"] = np.ascontiguousarray(dk.astype(np.float32))
            maps.append(m)
        r = _run(nc2, maps)
        yd = np.concatenate([r[cc]["y"] for cc in range(8)], axis=1).transpose(0, 2, 1)
        ys.append(yd if d == 0 else yd[:, ::-1])
    nc3 = _prog("s5post", lambda: build_tok("s5post", NTC))
    maps = []
    for core in range(8):
        b, half = core // 2, core % 2
        sl = slice(half * NTC, (half + 1) * NTC)
        m = _common(c, adaw, adab, b)
        m.update({"xT": _fm(xs[b, sl]), "yf": _fm(ys[0][b, sl]), "yb": _fm(ys[1][b, sl]), "wglu": np.ascontiguousarray(w_glu)})
        maps.append(m)
    return _gather_tok(_run(nc3, maps))


def kernel(x, c, ada_w, ada_b, norm1_g, norm2_g, final_g,
           attn_w_qkv, attn_w_o, attn_q_gain, attn_k_gain,
           hy_w_in, hy_conv_w, hy_conv_b, hy_f_w1, hy_f_b1, hy_f_w2, hy_f_b2, hy_f_w3, hy_f_freq, hy_skip, hy_w_out,
           s5_A_re, s5_A_im, s5_log_dt, s5_B_re, s5_B_im, s5_C_re, s5_C_im, s5_D, s5_w_glu,
           ffn_w_up, ffn_conv_w, ffn_conv_b, ffn_w_down):
    A = lambda v: np.asarray(v, dtype=np.float32)
    xs = A(x)
    c = A(c)
    ada_w, ada_b = A(ada_w), A(ada_b)
    for i in range(4):
        m, j = i % 3, i // 3
        aw1, ab1 = ada_w[i][:, :3 * D], ada_b[i][:3 * D]
        aw2, ab2 = ada_w[i][:, 3 * D:], ada_b[i][3 * D:]
        if m == 0:
            xs = run_attn(xs, c, aw1, ab1, A(norm1_g)[i], A(attn_w_qkv)[j], A(attn_w_o)[j], A(attn_q_gain)[j], A(attn_k_gain)[j])
        elif m == 1:
            xs = run_hyena(xs, c, aw1, ab1, A(norm1_g)[i], A(hy_w_in)[j], A(hy_conv_w)[j], A(hy_conv_b)[j], A(hy_f_w1)[j], A(hy_f_b1)[j],
                           A(hy_f_w2)[j], A(hy_f_b2)[j], A(hy_f_w3)[j], A(hy_f_freq)[j], A(hy_skip)[j], A(hy_w_out)[j])
        else:
            xs = run_s5(xs, c, aw1, ab1, A(norm1_g)[i], A(s5_A_re)[j], A(s5_A_im)[j], A(s5_log_dt)[j], A(s5_B_re)[j], A(s5_B_im)[j],
                        A(s5_C_re)[j], A(s5_C_im)[j], A(s5_D)[j], A(s5_w_glu)[j])
        xs = run_ffn(xs, c, aw2, ab2, A(norm2_g)[i], A(ffn_w_up)[i], A(ffn_conv_w)[i], A(ffn_conv_b)[i], A(ffn_w_down)[i],
                     final_g=A(final_g) if i == 3 else None)
    return xs.astype(np.float32)
```

```python
import numpy as np
import ml_dtypes
import concourse.bass as bass
import concourse.mybir as mybir
from concourse.bass_utils import run_bass_kernel_spmd

F32 = mybir.dt.float32
BF16 = mybir.dt.bfloat16
I32 = mybir.dt.int32
AF = mybir.ActivationFunctionType
ALU = mybir.AluOpType
AX = mybir.AxisListType

D = 1024
DFF = 2816
EPS = 1e-6


class Res:
    __slots__ = ("w", "r")

    def __init__(self):
        self.w = None
        self.r = []


class KB:
    ENG = ("pe", "act", "dve", "pool", "sp")

    def __init__(self, n_dma_sems=24):
        nc = bass.Bass("TRN2", target_bir_lowering=False)
        self.nc = nc
        self.e = {"pe": nc.tensor, "act": nc.scalar, "dve": nc.vector, "pool": nc.gpsimd, "sp": nc.sync}
        self.sem = {}
        self.cnt = {}
        for k in self.ENG:
            self.sem[k] = nc.semaphore("s_" + k).__enter__()
            self.cnt[k] = 0
        self.dsem = []
        for i in range(n_dma_sems):
            key = "d%d" % i
            self.sem[key] = nc.semaphore("s_" + key).__enter__()
            self.cnt[key] = 0
            self.dsem.append(key)
        self.dnext = 0
        self.waited = {k: {} for k in self.ENG}
        self.out_events = []
        self.n_inst = 0
        self._names = 0
        self.sfx = ""

    def sb(self, shape, dt, name=None):
        self._names += 1
        return self.nc.sbuf_tensor((name or ("t%d" % self._names)) + self.sfx, list(shape), dt).__enter__()

    def ps(self, name=None, shape=(128, 512), dt=F32):
        self._names += 1
        return self.nc.psum_tensor((name or ("p%d" % self._names)) + self.sfx, list(shape), dt).__enter__()

    def dram(self, name, shape, dt, kind="Internal"):
        return self.nc.dram_tensor(name, list(shape), dt, kind=kind).ap()

    def inp(self, name, shape, dt=F32):
        return self.nc.dram_tensor(name + self.sfx, list(shape), dt, kind="ExternalInput").ap()

    def outp(self, name, shape, dt=F32):
        return self.nc.dram_tensor(name + self.sfx, list(shape), dt, kind="ExternalOutput").ap()

    def _wait(self, eng, ev):
        if ev is None:
            return
        key, val = ev
        if key == eng and eng == "pe":
            return
        if self.waited[eng].get(key, 0) >= val:
            return
        self.e[eng].wait_ge(self.sem[key], val)
        self.waited[eng][key] = val

    def _deps(self, eng, reads, writes):
        for r in reads:
            self._wait(eng, r.w)
        for w in writes:
            self._wait(eng, w.w)
            for ev in w.r:
                if ev[0] == eng:
                    continue
                self._wait(eng, ev)

    def _commit(self, ev, reads, writes):
        for r in reads:
            r.r.append(ev)
            if len(r.r) > 64:
                best = {}
                for k, v in r.r:
                    if best.get(k, 0) < v:
                        best[k] = v
                r.r = list(best.items())
        for w in writes:
            w.w = ev
            w.r = []

    def op(self, eng, fn, reads=(), writes=()):
        self._deps(eng, reads, writes)
        inst = fn(self.e[eng])
        self.cnt[eng] += 1
        inst.then_inc(self.sem[eng], 1)
        ev = (eng, self.cnt[eng])
        self._commit(ev, reads, writes)
        self.n_inst += 1
        return ev

    def dma(self, out, in_, reads=(), writes=(), q="sp", is_output=False, **kw):
        key = self.dsem[self.dnext]
        self.dnext = (self.dnext + 1) % len(self.dsem)
        if self.cnt[key] > 0:
            self._wait(q, (key, self.cnt[key]))
        self._deps(q, reads, writes)
        inst = self.e[q].dma_start(out=out, in_=in_, **kw)
        self.cnt[key] += 16
        inst.then_inc(self.sem[key], 16)
        ev = (key, self.cnt[key])
        self._commit(ev, reads, writes)
        if is_output:
            self.out_events.append(ev)
        self.n_inst += 1
        return ev

    def finish(self):
        for ev in self.out_events:
            self._wait("sp", ev)
        return self.nc


def bf(x):
    return np.asarray(x, dtype=np.float32).astype(ml_dtypes.bfloat16).astype(np.float32)


def load_cast_weight(kb, w_ap, kchunks, ncols, name, res, colblk=1024):
    wt = kb.sb([128, kchunks, ncols], BF16, name)
    src = w_ap.rearrange("(k p) n -> p k n", p=128)
    for k in range(kchunks):
        for c0 in range(0, ncols, colblk):
            c1 = min(ncols, c0 + colblk)
            kb.dma(wt[:, k, c0:c1], src[:, k, c0:c1], writes=[res], q="pool")
    return wt


def emit_mod(kb, c_ap, adaw_ap, adab_ap, nch, name, ps=None, r_ps_in=None):
    r_c, r_w, r_ps, r_mod, r_b = Res(), Res(), Res(), Res(), Res()
    ct = kb.sb([128, 8, 2], F32, name + "_c")
    sg = kb.sb([128, 8, 2], F32, name + "_sg")
    ca = kb.sb([128, 8, 2], F32, name + "_ca")
    bt = kb.sb([128, nch], F32, name + "_b")
    mod = kb.sb([128, nch], F32, name)
    kb.dma(ct[:], c_ap, writes=[r_c])
    kb.dma(bt[:], adab_ap, writes=[r_b])
    kb.op("act", lambda e: e.activation(out=sg[:], in_=ct[:], func=AF.Sigmoid), reads=[r_c], writes=[r_mod])
    kb.op("dve", lambda e: e.tensor_tensor(out=ca[:], in0=ct[:], in1=sg[:], op=ALU.mult), reads=[r_c, r_mod], writes=[r_ps])
    r_ca = r_ps
    src = adaw_ap.rearrange("(k p) n -> p k n", p=128)
    wts = [kb.sb([128, 8, 128], F32, name + "_w%d" % i) for i in range(2)]
    r_wts = [Res(), Res()]
    ps1 = ps if ps is not None else kb.ps(name + "_ps")
    r_ps1 = r_ps_in if r_ps_in is not None else Res()
    for j in range(nch):
        s = j % 2
        kb.dma(wts[s][:], src[:, :, j * 128:(j + 1) * 128], writes=[r_wts[s]])
        for k in range(8):
            kb.op("pe", lambda e, k=k, s=s, j=j: e.matmul(ps1[:, 2 * j:2 * j + 2], lhsT=wts[s][:, k, :], rhs=ca[:, k, :],
                                                      start=(k == 0), stop=(k == 7)),
                  reads=[r_wts[s], r_ca], writes=[r_ps1])
    kb.op("dve", lambda e: e.tensor_tensor(out=mod[:], in0=ps1[:, 0:2 * nch:2], in1=bt[:], op=ALU.add),
          reads=[r_ps1, r_b], writes=[r_mod])
    return mod, r_mod


def small_load(kb, ap, shape, name, dt=F32):
    t = kb.sb(shape, dt, name)
    r = Res()
    kb.dma(t[:], ap, writes=[r])
    return t, r


class Norm:
    def __init__(self, kb, W, name):
        self.kb = kb
        self.ones = kb.sb([128, 128], BF16, name + "_ones")
        self.r_ones = Res()
        kb.op("dve", lambda e: e.memset(self.ones[:], 1.0), writes=[self.r_ones])
        self.sq = kb.sb([128, 8, W], BF16, name + "_sq")
        self.r_sq = Res()
        self.ps = kb.ps(name + "_ps")
        self.r_ps = Res()
        self.sd = kb.sb([128, W], F32, name + "_sd")
        self.rstd = kb.sb([128, W], F32, name + "_rstd")
        self.r_sd = Res()
        self.r_rstd = Res()

    def emit(self, x3, r_x, w):
        kb = self
        kb = self.kb
        kb.op("act", lambda e: e.activation(out=self.sq[:, :, :w], in_=x3, func=AF.Square), reads=[r_x], writes=[self.r_sq])
        for k in range(8):
            kb.op("pe", lambda e, k=k: e.matmul(self.ps[:, :w], lhsT=self.ones[:], rhs=self.sq[:, k, :w], start=(k == 0), stop=(k == 7)),
                  reads=[self.r_sq, self.r_ones], writes=[self.r_ps])
        kb.op("act", lambda e: e.activation(out=self.sd[:, :w], in_=self.ps[:, :w], func=AF.Sqrt, scale=1.0 / D, bias=self.epsb[:]),
              reads=[self.r_ps, self.r_eps], writes=[self.r_sd])
        kb.op("dve", lambda e: e.reciprocal(out=self.rstd[:, :w], in_=self.sd[:, :w]), reads=[self.r_sd], writes=[self.r_rstd])
        return self.rstd[:, :w], self.r_rstd

    def init_eps(self):
        kb = self.kb
        self.epsb = kb.sb([128, 1], F32, "epsb%d" % id(self))
        self.r_eps = Res()
        kb.op("dve", lambda e: e.memset(self.epsb[:], EPS), writes=[self.r_eps])
        return self


def emit_gs(kb, mod, r_mod, sc_off, g_t, r_g, name):
    gs = kb.sb([128, 8], F32, name)
    r = Res()
    kb.op("dve", lambda e: e.scalar_tensor_tensor(out=gs[:], in0=mod[:, sc_off:sc_off + 8], scalar=1.0, in1=g_t[:],
                                                  op0=ALU.add, op1=ALU.mult), reads=[r_mod, r_g], writes=[r])
    return gs, r


def build_ffn(NT, final=False, TW=256):
    kb = KB()
    xT = kb.inp("xT", [128, 8, NT + 2])
    c_in = kb.inp("c2", [128, 8, 2])
    adaw = kb.inp("adaw", [1024, 3 * D])
    adab = kb.inp("adab", [128, 24])
    ng = kb.inp("ng", [128, 8])
    wup = kb.inp("wup", [D, 2 * DFF])
    wdn = kb.inp("wdn", [DFF, D])
    cw = kb.inp("cw", [128, 22 * 3])
    cb = kb.inp("cb", [128, 22])
    msk = kb.inp("msk", [128, 2])
    if final:
        fg = kb.inp("fg", [128, 8])
    out = kb.outp("out", [128, 8, NT])

    W = TW + 2
    r_wup, r_wdn = Res(), Res()
    wup_t = load_cast_weight(kb, wup, 8, 2 * DFF, "wup_t", r_wup, colblk=1408)
    wdn_t = load_cast_weight(kb, wdn, 22, D, "wdn_t", r_wdn)
    mod, r_mod = emit_mod(kb, c_in, adaw, adab, 24, "mod")
    ng_t, r_ng = small_load(kb, ng, [128, 8], "ng_t")
    cw_t, r_cw = small_load(kb, cw, [128, 66], "cw_t")
    cb_t, r_cb = small_load(kb, cb, [128, 22], "cb_t")
    mk_t, r_mk = small_load(kb, msk, [128, 2], "mk_t")
    if final:
        fg_t, r_fg = small_load(kb, fg, [128, 8], "fg_t")
    gs, r_gs = emit_gs(kb, mod, r_mod, 8, ng_t, r_ng, "gs")
    nrm = Norm(kb, W, "nrm").init_eps()

    xt = [kb.sb([128, 8, W], F32, "xt%d" % i) for i in range(2)]
    r_xt = [Res(), Res()]
    tmp = kb.sb([128, W], F32, "tmp")
    r_tmp = Res()
    h = kb.sb([128, 8, W], BF16, "h")
    r_h = Res()
    a = kb.sb([128, 22, W], BF16, "a")
    r_a = Res()
    cbuf = [kb.sb([128, W], F32, "cbuf%d" % i) for i in range(2)]
    r_cbuf = [Res(), Res()]
    sbuf_ = [kb.sb([128, W], F32, "sbuf%d" % i) for i in range(2)]
    r_sbuf = [Res(), Res()]
    xo = kb.sb([128, 8, W], F32, "xo")
    r_xo = Res()
    pg = [kb.ps("pg%d" % i) for i in range(2)]
    pv = [kb.ps("pv%d" % i) for i in range(2)]
    r_pg = [Res(), Res()]
    r_pv = [Res(), Res()]
    po = [kb.ps("po%d" % i) for i in range(2)]
    r_po = [Res(), Res()]

    ntiles = (NT + TW - 1) // TW
    for ti in range(ntiles):
        o0 = ti * TW
        o1 = min(NT, o0 + TW)
        w = o1 - o0 + 2
        s = ti % 2
        x3 = xt[s][:, :, :w]
        kb.dma(x3, xT[:, :, o0:o0 + w], writes=[r_xt[s]])
        rstd, r_rstd = nrm.emit(x3, r_xt[s], w)
        for k in range(8):
            kb.op("dve", lambda e, k=k: e.scalar_tensor_tensor(out=tmp[:, :w], in0=xt[s][:, k, :w], scalar=gs[:, k:k + 1],
                                                                in1=rstd, op0=ALU.mult, op1=ALU.mult),
                  reads=[r_xt[s], r_gs, r_rstd], writes=[r_tmp])
            kb.op("act", lambda e, k=k: e.activation(out=h[:, k, :w], in_=tmp[:, :w], func=AF.Identity,
                                                     bias=mod[:, k:k + 1], scale=1.0),
                  reads=[r_tmp, r_mod], writes=[r_h])
        if ti == 0:
            kb.op("dve", lambda e: e.tensor_scalar(out=h[:, :, 0:1], in0=h[:, :, 0:1], scalar1=mk_t[:, 0:1], scalar2=None, op0=ALU.mult),
                  reads=[r_mk, r_h], writes=[r_h])
        if ti == ntiles - 1:
            kb.op("dve", lambda e: e.tensor_scalar(out=h[:, :, w - 1:w], in0=h[:, :, w - 1:w], scalar1=mk_t[:, 1:2], scalar2=None, op0=ALU.mult),
                  reads=[r_mk, r_h], writes=[r_h])
        for j in range(22):
            q = j % 2
            for k in range(8):
                kb.op("pe", lambda e, k=k, j=j, q=q: e.matmul(pg[q][:, :w], lhsT=wup_t[:, k, j * 128:(j + 1) * 128], rhs=h[:, k, :w],
                                                               start=(k == 0), stop=(k == 7)),
                      reads=[r_wup, r_h], writes=[r_pg[q]])
            for k in range(8):
                kb.op("pe", lambda e, k=k, j=j, q=q: e.matmul(pv[q][:, :w], lhsT=wup_t[:, k, DFF + j * 128:DFF + (j + 1) * 128], rhs=h[:, k, :w],
                                                               start=(k == 0), stop=(k == 7)),
                      reads=[r_wup, r_h], writes=[r_pv[q]])
            cbq = cbuf[q]
            kb.op("act", lambda e, j=j, q=q, cbq=cbq: e.activation(out=cbq[:, 1:w - 1], in_=pg[q][:, 1:w - 1], func=AF.Identity,
                                                                 scale=cw_t[:, 3 * j + 1:3 * j + 2], bias=cb_t[:, j:j + 1]),
                  reads=[r_pg[q], r_cw, r_cb], writes=[r_cbuf[q]])
            kb.op("dve", lambda e, j=j, q=q, cbq=cbq: e.scalar_tensor_tensor(out=cbq[:, 1:w - 1], in0=pg[q][:, 0:w - 2], scalar=cw_t[:, 3 * j:3 * j + 1],
                                                                           in1=cbq[:, 1:w - 1], op0=ALU.mult, op1=ALU.add),
                  reads=[r_pg[q], r_cw, r_cbuf[q]], writes=[r_cbuf[q]])
            kb.op("dve", lambda e, j=j, q=q, cbq=cbq: e.scalar_tensor_tensor(out=cbq[:, 1:w - 1], in0=pg[q][:, 2:w], scalar=cw_t[:, 3 * j + 2:3 * j + 3],
                                                                           in1=cbq[:, 1:w - 1], op0=ALU.mult, op1=ALU.add),
                  reads=[r_pg[q], r_cw, r_cbuf[q]], writes=[r_cbuf[q]])
            sbq = sbuf_[q]
            kb.op("act", lambda e, q=q, cbq=cbq, sbq=sbq: e.activation(out=sbq[:, 1:w - 1], in_=cbq[:, 1:w - 1], func=AF.Silu),
                  reads=[r_cbuf[q]], writes=[r_sbuf[q]])
            kb.op("dve", lambda e, j=j, q=q, sbq=sbq: e.tensor_tensor(out=a[:, j, 1:w - 1], in0=sbq[:, 1:w - 1], in1=pv[q][:, 1:w - 1], op=ALU.mult),
                  reads=[r_sbuf[q], r_pv[q]], writes=[r_a])
        for m in range(8):
            q = m % 2
            for j in range(22):
                kb.op("pe", lambda e, m=m, j=j, q=q: e.matmul(po[q][:, :w - 2], lhsT=wdn_t[:, j, m * 128:(m + 1) * 128], rhs=a[:, j, 1:w - 1],
                                                               start=(j == 0), stop=(j == 21)),
                      reads=[r_wdn, r_a], writes=[r_po[q]])
            kb.op("dve", lambda e, m=m, q=q: e.scalar_tensor_tensor(out=xo[:, m, :w - 2], in0=po[q][:, :w - 2], scalar=mod[:, 16 + m:17 + m],
                                                                    in1=xt[s][:, m, 1:w - 1], op0=ALU.mult, op1=ALU.add),
                  reads=[r_po[q], r_mod, r_xt[s]], writes=[r_xo])
        if final:
            rstd2, r_rstd2 = nrm.emit(xo[:, :, :w - 2], r_xo, w - 2)
            for m in range(8):
                kb.op("dve", lambda e, m=m: e.scalar_tensor_tensor(out=xo[:, m, :w - 2], in0=xo[:, m, :w - 2], scalar=fg_t[:, m:m + 1],
                                                                   in1=rstd2, op0=ALU.mult, op1=ALU.mult),
                      reads=[r_xo, r_fg, r_rstd2], writes=[r_xo])
        kb.dma(out[:, :, o0:o1], xo[:, :, :w - 2], reads=[r_xo], q="sp", is_output=True)
    return kb.finish()


HD = 64
QPERM = [0, 4, 1, 5, 2, 6, 3, 7, 8, 12, 9, 13, 10, 14, 11, 15]


def build_attn(L=8192, NQ=4096, TW=512):
    kb = KB()
    nc = kb.nc
    xall = kb.inp("xall", [128, 8, L])
    xq = kb.inp("xq", [128, 8, NQ])
    c_in = kb.inp("c2", [128, 8, 2])
    adaw = kb.inp("adaw", [1024, 3 * D])
    adab = kb.inp("adab", [128, 24])
    ng = kb.inp("ng", [128, 8])
    wkv = kb.inp("wkv", [D, 768])
    wq = kb.inp("wq", [D, 2048])
    wo = kb.inp("wo", [D, D])
    gains = kb.inp("gains", [128, 4])
    ropek = kb.inp("ropek", [2, 128, L])
    ropeq = kb.inp("ropeq", [2, 128, NQ])
    out = kb.outp("out", [128, 8, NQ])

    kT = kb.sb([128, 2, L], BF16, "kT")
    r_kT = Res()
    NKT = L // 128
    vaug = kb.sb([128, NKT, 4, 65], BF16, "vaug")
    r_v = Res()
    kb.op("pool", lambda e: e.memset(vaug[:, :, :, 64:65], 1.0), writes=[r_v])
    pss = kb.ps("pss")
    r_pss = Res()
    mod, r_mod = emit_mod(kb, c_in, adaw, adab, 24, "mod", ps=pss, r_ps_in=r_pss)
    ng_t, r_ng = small_load(kb, ng, [128, 8], "ng_t")
    gn_t, r_gn = small_load(kb, gains, [128, 4], "gn_t")
    gs, r_gs = emit_gs(kb, mod, r_mod, 8, ng_t, r_ng, "gs")
    nrm = Norm(kb, TW, "nrm").init_eps()
    bones = kb.sb([128, 128], BF16, "bones")
    r_bones = Res()
    kb.op("dve", lambda e: e.memset(bones[:], 0.0), writes=[r_bones])
    kb.op("dve", lambda e: e.memset(bones[0:64, 0:64], 1.0), writes=[r_bones])
    kb.op("dve", lambda e: e.memset(bones[64:128, 64:128], 1.0), writes=[r_bones])
    sel = kb.sb([65, 64], F32, "sel")
    r_sel = Res()
    kb.op("dve", lambda e: e.memset(sel[:], 0.0), writes=[r_sel])
    kb.op("dve", lambda e: e.memset(sel[64:65, :], 1.0), writes=[r_sel])

    xt = kb.sb([128, 8, TW], F32, "xt")
    r_xt = Res()
    h = kb.sb([128, 8, TW], BF16, "h")
    r_h = Res()
    tmp = kb.sb([128, TW], F32, "tmp")
    r_tmp = Res()
    ctab = kb.sb([128, 2, TW], F32, "ctab")
    r_ctab = Res()
    sqh = kb.sb([128, TW], BF16, "sqh")
    r_sqh = Res()
    t1 = kb.sb([128, TW], F32, "t1")
    t2 = kb.sb([128, TW], F32, "t2")
    r_t1, r_t2 = Res(), Res()
    rs = kb.sb([128, TW], F32, "rs")
    r_rs = Res()
    pa = [kb.ps("pa%d" % i) for i in range(2)]
    r_pa = [Res(), Res()]

    def modnorm(src_ap, w):
        kb.dma(xt[:, :, :w], src_ap, writes=[r_xt])
        rstd, r_rstd = nrm.emit(xt[:, :, :w], r_xt, w)
        for k in range(8):
            kb.op("dve", lambda e, k=k: e.scalar_tensor_tensor(out=tmp[:, :w], in0=xt[:, k, :w], scalar=gs[:, k:k + 1],
                                                                in1=rstd, op0=ALU.mult, op1=ALU.mult),
                  reads=[r_xt, r_gs, r_rstd], writes=[r_tmp])
            kb.op("act", lambda e, k=k: e.activation(out=h[:, k, :w], in_=tmp[:, :w], func=AF.Identity,
                                                     bias=mod[:, k:k + 1], scale=1.0),
                  reads=[r_tmp, r_mod], writes=[r_h])

    def proj_rope(wt, r_wt, col, col_sw, gcol, dst_ap, r_dst, w, scale):
        for k in range(8):
            kb.op("pe", lambda e, k=k: e.matmul(pa[0][:, :w], lhsT=wt[:, k, col:col + 128], rhs=h[:, k, :w], start=(k == 0), stop=(k == 7)),
                  reads=[r_wt, r_h], writes=[r_pa[0]])
        for k in range(8):
            kb.op("pe", lambda e, k=k: e.matmul(pa[1][:, :w], lhsT=wt[:, k, col_sw:col_sw + 128], rhs=h[:, k, :w], start=(k == 0), stop=(k == 7)),
                  reads=[r_wt, r_h], writes=[r_pa[1]])
        kb.op("act", lambda e: e.activation(out=sqh[:, :w], in_=pa[0][:, :w], func=AF.Square), reads=[r_pa[0]], writes=[r_sqh])
        kb.op("pe", lambda e: e.matmul(pss[:, :w], lhsT=bones[:], rhs=sqh[:, :w], start=True, stop=True),
              reads=[r_sqh, r_bones], writes=[r_pss])
        kb.op("act", lambda e: e.activation(out=rs[:, :w], in_=pss[:, :w], func=AF.Sqrt, scale=1.0 / HD, bias=nrm.epsb[:]),
              reads=[r_pss, nrm.r_eps], writes=[r_rs])
        kb.op("dve", lambda e: e.reciprocal(out=rs[:, :w], in_=rs[:, :w]), reads=[r_rs], writes=[r_rs])
        kb.op("dve", lambda e: e.scalar_tensor_tensor(out=t1[:, :w], in0=pa[0][:, :w], scalar=gn_t[:, gcol:gcol + 1], in1=ctab[:, 0, :w],
                                                      op0=ALU.mult, op1=ALU.mult), reads=[r_pa[0], r_gn, r_ctab], writes=[r_t1])
        kb.op("dve", lambda e: e.scalar_tensor_tensor(out=t2[:, :w], in0=pa[1][:, :w], scalar=gn_t[:, gcol + 1:gcol + 2], in1=ctab[:, 1, :w],
                                                      op0=ALU.mult, op1=ALU.mult), reads=[r_pa[1], r_gn, r_ctab], writes=[r_t2])
        kb.op("pool", lambda e: e.tensor_tensor(out=t1[:, :w], in0=t1[:, :w], in1=t2[:, :w], op=ALU.add), reads=[r_t1, r_t2], writes=[r_t1])
        if isinstance(dst_ap, tuple):
            for (dap, lo) in dst_ap:
                kb.op("dve", lambda e, dap=dap, lo=lo: e.scalar_tensor_tensor(out=dap, in0=t1[lo:lo + 64, :w], scalar=float(scale), in1=rs[lo:lo + 64, :w],
                                                                              op0=ALU.mult, op1=ALU.mult), reads=[r_t1, r_rs], writes=[r_dst])
        else:
            kb.op("dve", lambda e: e.scalar_tensor_tensor(out=dst_ap, in0=t1[:, :w], scalar=float(scale), in1=rs[:, :w],
                                                          op0=ALU.mult, op1=ALU.mult), reads=[r_t1, r_rs], writes=[r_dst])

    r_wkv = Res()
    g_wkv = nc.sbuf_tensor("wkv_t", [128, 8, 768], BF16)
    wkv_t = g_wkv.__enter__()
    srckv = wkv.rearrange("(k p) n -> p k n", p=128)
    for k in range(8):
        kb.dma(wkv_t[:, k, :], srckv[:, k, :], writes=[r_wkv], q="pool")
    pvp = kb.ps("pvp")
    r_pvp = Res()
    for ti in range(L // TW):
        t0 = ti * TW
        w = TW
        modnorm(xall[:, :, t0:t0 + w], w)
        kb.dma(ctab[:, :, :w], ropek[:, :, t0:t0 + w].rearrange("c p t -> p c t"), writes=[r_ctab])
        for kc in range(2):
            proj_rope(wkv_t, r_wkv, kc * 128, 256 + kc * 128, 2, kT[:, kc, t0:t0 + w], r_kT, w, 1.0)
        for ts in range(w // 128):
            kt = (t0 // 128) + ts
            for k in range(8):
                kb.op("pe", lambda e, k=k, ts=ts: e.matmul(pvp[:, 0:256], lhsT=h[:, k, ts * 128:(ts + 1) * 128], rhs=wkv_t[:, k, 512:768],
                                                           start=(k == 0), stop=(k == 7)), reads=[r_h, r_wkv], writes=[r_pvp])
            kb.op("act", lambda e, kt=kt: e.activation(out=vaug[:, kt, :, 0:64], in_=pvp[:, 0:256].rearrange("p (a b) -> p a b", a=4),
                                                       func=AF.Copy), reads=[r_pvp], writes=[r_v])
    r_free = r_wkv
    g_wkv.__exit__(None, None, None)

    r_wq, r_wo = Res(), Res()
    r_wq.r = list(r_free.r)
    r_wq.w = r_free.w
    r_wo.r = list(r_free.r)
    r_wo.w = r_free.w
    wq_t = kb.sb([128, 8, 2048], BF16, "wq_t")
    srcq = wq.rearrange("(k p) n -> p k n", p=128)
    for k in range(8):
        for c0 in range(0, 2048, 1024):
            kb.dma(wq_t[:, k, c0:c0 + 1024], srcq[:, k, c0:c0 + 1024], writes=[r_wq], q="pool")
    wo_t = kb.sb([128, 8, D], BF16, "wo_t")
    srco = wo.rearrange("(k p) n -> p k n", p=128)
    for k in range(8):
        kb.dma(wo_t[:, k, :], srco[:, k, :], writes=[r_wo], q="pool")
    qT = kb.sb([128, 16, TW], BF16, "qT")
    r_qT = Res()
    kb.op("pool", lambda e: e.memset(qT[:], 0.0), writes=[r_qT])
    oT = kb.sb([128, 8, TW], BF16, "oT")
    r_oT = Res()
    otmp = [kb.sb([64, TW], BF16, "otmp%d" % i) for i in range(2)]
    r_otmp = [Res(), Res()]
    pT = [kb.sb([128, TW], BF16, "pT%d" % i) for i in range(3)]
    r_pT = [Res() for _ in range(3)]
    oacc = kb.sb([65, TW], F32, "oacc")
    r_oacc = Res()
    rec = kb.sb([64, TW], F32, "rec")
    r_rec = Res()
    psc = [kb.ps("psc%d" % i) for i in range(2)] + [pa[1]]
    r_psc = [Res(), Res(), r_pa[1]]
    pso = kb.ps("pso")
    r_pso = Res()
    pmisc = pvp
    r_pmisc = r_pvp

    for qi in range(NQ // TW):
        t0 = qi * TW
        w = TW
        modnorm(xq[:, :, t0:t0 + w], w)
        kb.dma(ctab[:, :, :w], ropeq[:, :, t0:t0 + w].rearrange("c p t -> p c t"), writes=[r_ctab])
        for j in range(8):
            proj_rope(wq_t, r_wq, j * 128, 1024 + j * 128, 0, ((qT[0:64, 2 * j, :w], 0), (qT[64:128, 2 * j + 1, :w], 64)), r_qT, w, 0.125)
        for j in range(8):
            for hb in range(2):
                hq = QPERM[2 * j + hb]
                kvh = hq // 4
                assert kvh % 2 == hb
                kc = kvh // 2
                b0 = 64 * hb
                hidx = 2 * j + hb
                pso_c, r_pso_c = (pso, r_pso) if hidx % 2 == 0 else (pa[0], r_pa[0])

                def qk(kt):
                    sl = kt % 3
                    kb.op("pe", lambda e, kt=kt, sl=sl: e.matmul(psc[sl][:, :w], lhsT=kT[:, kc, kt * 128:(kt + 1) * 128],
                                                                 rhs=qT[:, 2 * j + hb, :w], start=True, stop=True),
                          reads=[r_kT, r_qT], writes=[r_psc[sl]])

                def ex(kt):
                    sl = kt % 3
                    kb.op("act", lambda e, sl=sl: e.activation(out=pT[sl][:, :w], in_=psc[sl][:, :w], func=AF.Exp),
                          reads=[r_psc[sl]], writes=[r_pT[sl]])

                def pv(kt):
                    sl = kt % 3
                    kb.op("pe", lambda e, kt=kt, sl=sl: e.matmul(pso_c[0:65, :w], lhsT=vaug[:, kt, kvh, :], rhs=pT[sl][:, :w],
                                                                 start=(kt == 0), stop=(kt == NKT - 1)),
                          reads=[r_v, r_pT[sl]], writes=[r_pso_c])
                qk(0)
                if NKT > 1:
                    qk(1)
                for kt in range(NKT):
                    ex(kt)
                    if kt + 2 < NKT:
                        qk(kt + 2)
                    pv(kt)
                kb.op("dve", lambda e: e.tensor_copy(out=oacc[:, :w], in_=pso_c[0:65, :w]), reads=[r_pso_c], writes=[r_oacc])
                kb.op("pe", lambda e: e.matmul(pmisc[0:64, :w], lhsT=sel[:], rhs=oacc[:, :w], start=True, stop=True),
                      reads=[r_sel, r_oacc], writes=[r_pmisc])
                kb.op("dve", lambda e: e.reciprocal(out=rec[:, :w], in_=pmisc[0:64, :w]), reads=[r_pmisc], writes=[r_rec])
                if hq % 2 == 0:
                    kb.op("dve", lambda e, hq=hq: e.tensor_tensor(out=oT[0:64, hq // 2, :w], in0=oacc[0:64, :w], in1=rec[:, :w], op=ALU.mult),
                          reads=[r_oacc, r_rec], writes=[r_oT])
                else:
                    osl = (hq // 2) % 2
                    kb.op("dve", lambda e, osl=osl: e.tensor_tensor(out=otmp[osl][:, :w], in0=oacc[0:64, :w], in1=rec[:, :w], op=ALU.mult),
                          reads=[r_oacc, r_rec], writes=[r_otmp[osl]])
                    kb.dma(oT[64:128, hq // 2, :w], otmp[osl][:, :w], reads=[r_otmp[osl]], writes=[r_oT], q="sp")
        for m in range(8):
            for k in range(8):
                kb.op("pe", lambda e, m=m, k=k: e.matmul(pmisc[:, :w], lhsT=wo_t[:, k, m * 128:(m + 1) * 128], rhs=oT[:, k, :w],
                                                         start=(k == 0), stop=(k == 7)), reads=[r_wo, r_oT], writes=[r_pmisc])
            kb.op("dve", lambda e, m=m: e.scalar_tensor_tensor(out=xt[:, m, :w], in0=pmisc[:, :w], scalar=mod[:, 16 + m:17 + m],
                                                               in1=xt[:, m, :w], op0=ALU.mult, op1=ALU.add),
                  reads=[r_pmisc, r_mod, r_xt], writes=[r_xt])
        kb.dma(out[:, :, t0:t0 + w], xt[:, :, :w], reads=[r_xt], q="sp", is_output=True)
    return kb.finish()


TWO_PI = 6.283185307179586


def emit_sincos(kb, ang, r_ang, sin_out, cos_out, r_out, shape, name):
    PI = 3.141592653589793
    MAGIC = 12582912.0
    kf = kb.sb(shape, F32, name + "_kf")
    mk = kb.sb(shape, F32, name + "_mk")
    r2, r3 = Res(), Res()
    kb.op("dve", lambda e: e.tensor_scalar(out=kf[:], in0=ang, scalar1=1.0 / TWO_PI, scalar2=MAGIC, op0=ALU.mult, op1=ALU.add), reads=[r_ang], writes=[r2])
    kb.op("dve", lambda e: e.tensor_scalar(out=kf[:], in0=kf[:], scalar1=-MAGIC, scalar2=None, op0=ALU.add), reads=[r2], writes=[r2])
    kb.op("dve", lambda e: e.scalar_tensor_tensor(out=ang, in0=kf[:], scalar=-TWO_PI, in1=ang, op0=ALU.mult, op1=ALU.add),
          reads=[r2, r_ang], writes=[r_ang])
    kb.op("dve", lambda e: e.tensor_scalar(out=kf[:], in0=ang, scalar1=-PI, scalar2=PI, op0=ALU.max, op1=ALU.min), reads=[r_ang], writes=[r2])
    kb.op("act", lambda e: e.activation(out=sin_out, in_=kf[:], func=AF.Sin), reads=[r2], writes=[r_out])
    kb.op("dve", lambda e: e.tensor_scalar(out=ang, in0=ang, scalar1=PI / 2, scalar2=None, op0=ALU.add), reads=[r_ang, r2], writes=[r_ang])
    kb.op("dve", lambda e: e.tensor_scalar(out=mk[:], in0=ang, scalar1=PI, scalar2=None, op0=ALU.is_gt), reads=[r_ang], writes=[r3])
    kb.op("dve", lambda e: e.scalar_tensor_tensor(out=ang, in0=mk[:], scalar=-TWO_PI, in1=ang, op0=ALU.mult, op1=ALU.add),
          reads=[r3, r_ang], writes=[r_ang])
    kb.op("dve", lambda e: e.tensor_scalar(out=kf[:], in0=ang, scalar1=-PI, scalar2=PI, op0=ALU.max, op1=ALU.min), reads=[r_ang, r_out], writes=[r2])
    kb.op("act", lambda e: e.activation(out=cos_out, in_=kf[:], func=AF.Sin), reads=[r2], writes=[r_out])


def build_s5(L=8192, NB=4, TW=512, ndir=1):
    kb = KB()
    sh = {}
    for d in range(ndir):
        kb.sfx = "" if ndir == 1 else str(d)
        _s5_dir(kb, sh, L, NB, TW)
    kb.sfx = ""
    return kb.finish()


def _s5_dir(kb, sh, L, NB, TW):
    hT = kb.inp("hT", [NB, 128, L])
    are = kb.inp("are", [128, 8])
    aim = kb.inp("aim", [128, 8])
    ldt = kb.inp("ldt", [128, 8])
    bre = kb.inp("bre", [128, 4, 128])
    bim = kb.inp("bim", [128, 4, 128])
    cre = kb.inp("cre", [128, 4, 128])
    cim = kb.inp("cim", [128, 4, 128])
    dsk = kb.inp("dsk", [128, 1])
    tau = kb.inp("tau", [128, TW + 1])
    y = kb.outp("y", [NB, 128, L])
    NG = 4
    are_t, r_are = small_load(kb, are, [128, 8], "are_t")
    aim_t, r_aim = small_load(kb, aim, [128, 8], "aim_t")
    ldt_t, r_ldt = small_load(kb, ldt, [128, 8], "ldt_t")
    bre_t, r_bre = small_load(kb, bre, [128, 4, 128], "bre_t")
    bim_t, r_bim = small_load(kb, bim, [128, 4, 128], "bim_t")
    cre_t, r_cre = small_load(kb, cre, [128, 4, 128], "cre_t")
    cim_t, r_cim = small_load(kb, cim, [128, 4, 128], "cim_t")
    dsk_t, r_dsk = small_load(kb, dsk, [128, 1], "dsk_t")
    tau_t, r_tau = small_load(kb, tau, [128, TW + 1], "tau_t")
    P = {}
    rP = Res()

    def sm(name):
        P[name] = kb.sb([128, 8], F32, "p_" + name)
        return P[name]
    for n in ("lre", "dt", "a", "th", "r", "s1", "c1", "lbr1", "lbi", "den", "fr", "fi", "t"):
        sm(n)
    V = lambda n: P[n][:]
    rr_all = [r_are, r_aim, r_ldt, rP]
    kb.op("dve", lambda e: e.tensor_scalar(out=V("lre"), in0=are_t[:], scalar1=-1e-4, scalar2=None, op0=ALU.min), reads=rr_all, writes=[rP])
    kb.op("act", lambda e: e.activation(out=V("dt"), in_=ldt_t[:], func=AF.Exp), reads=rr_all, writes=[rP])
    kb.op("dve", lambda e: e.tensor_tensor(out=V("a"), in0=V("lre"), in1=V("dt"), op=ALU.mult), reads=rr_all, writes=[rP])
    kb.op("dve", lambda e: e.tensor_tensor(out=V("th"), in0=aim_t[:], in1=V("dt"), op=ALU.mult), reads=rr_all, writes=[rP])
    kb.op("act", lambda e: e.activation(out=V("r"), in_=V("a"), func=AF.Exp), reads=rr_all, writes=[rP])
    kb.op("dve", lambda e: e.tensor_copy(out=V("t"), in_=V("th")), reads=rr_all, writes=[rP])
    emit_sincos(kb, V("t"), rP, V("s1"), V("c1"), rP, [128, 8], "sc0")
    kb.op("dve", lambda e: e.tensor_tensor(out=V("lbr1"), in0=V("r"), in1=V("c1"), op=ALU.mult), reads=[rP], writes=[rP])
    kb.op("dve", lambda e: e.tensor_scalar(out=V("lbr1"), in0=V("lbr1"), scalar1=-1.0, scalar2=None, op0=ALU.add), reads=[rP], writes=[rP])
    kb.op("dve", lambda e: e.tensor_tensor(out=V("lbi"), in0=V("r"), in1=V("s1"), op=ALU.mult), reads=[rP], writes=[rP])
    kb.op("dve", lambda e: e.tensor_tensor(out=V("den"), in0=V("lre"), in1=V("lre"), op=ALU.mult), reads=[rP], writes=[rP])
    kb.op("dve", lambda e: e.tensor_tensor(out=V("t"), in0=aim_t[:], in1=aim_t[:], op=ALU.mult), reads=[rP, r_aim], writes=[rP])
    kb.op("dve", lambda e: e.tensor_tensor(out=V("den"), in0=V("den"), in1=V("t"), op=ALU.add), reads=[rP], writes=[rP])
    kb.op("dve", lambda e: e.reciprocal(out=V("den"), in_=V("den")), reads=[rP], writes=[rP])
    kb.op("dve", lambda e: e.tensor_tensor(out=V("fr"), in0=V("lbr1"), in1=V("lre"), op=ALU.mult), reads=[rP], writes=[rP])
    kb.op("dve", lambda e: e.tensor_tensor(out=V("t"), in0=V("lbi"), in1=aim_t[:], op=ALU.mult), reads=[rP], writes=[rP])
    kb.op("dve", lambda e: e.tensor_tensor(out=V("fr"), in0=V("fr"), in1=V("t"), op=ALU.add), reads=[rP], writes=[rP])
    kb.op("dve", lambda e: e.tensor_tensor(out=V("fr"), in0=V("fr"), in1=V("den"), op=ALU.mult), reads=[rP], writes=[rP])
    kb.op("dve", lambda e: e.tensor_tensor(out=V("fi"), in0=V("lbi"), in1=V("lre"), op=ALU.mult), reads=[rP], writes=[rP])
    kb.op("dve", lambda e: e.tensor_tensor(out=V("t"), in0=V("lbr1"), in1=aim_t[:], op=ALU.mult), reads=[rP], writes=[rP])
    kb.op("dve", lambda e: e.tensor_tensor(out=V("fi"), in0=V("fi"), in1=V("t"), op=ALU.subtract), reads=[rP], writes=[rP])
    kb.op("dve", lambda e: e.tensor_tensor(out=V("fi"), in0=V("fi"), in1=V("den"), op=ALU.mult), reads=[rP], writes=[rP])
    crp = kb.sb([128, 4, 128], F32, "crp")
    ncip = kb.sb([128, 4, 128], F32, "ncip")
    ctmp = kb.sb([128, 128], F32, "ctmp")
    r_cp = Res()
    for j in range(NG):
        kb.op("dve", lambda e, j=j: e.tensor_scalar(out=ctmp[:], in0=cim_t[:, j, :], scalar1=P["fi"][:, j:j + 1], scalar2=None, op0=ALU.mult),
              reads=[rP, r_cim, r_cp], writes=[r_cp])
        kb.op("dve", lambda e, j=j: e.scalar_tensor_tensor(out=crp[:, j, :], in0=cre_t[:, j, :], scalar=P["fr"][:, j:j + 1], in1=ctmp[:],
                                                           op0=ALU.mult, op1=ALU.subtract), reads=[rP, r_cre, r_cp], writes=[r_cp])
        kb.op("dve", lambda e, j=j: e.tensor_scalar(out=ctmp[:], in0=cim_t[:, j, :], scalar1=P["fr"][:, j:j + 1], scalar2=None, op0=ALU.mult),
              reads=[rP, r_cim, r_cp], writes=[r_cp])
        kb.op("dve", lambda e, j=j: e.scalar_tensor_tensor(out=ncip[:, j, :], in0=cre_t[:, j, :], scalar=P["fi"][:, j:j + 1], in1=ctmp[:],
                                                           op0=ALU.mult, op1=ALU.add), reads=[rP, r_cre, r_cp], writes=[r_cp])
        kb.op("dve", lambda e, j=j: e.tensor_scalar(out=ncip[:, j, :], in0=ncip[:, j, :], scalar1=-1.0, scalar2=None, op0=ALU.mult),
              reads=[r_cp], writes=[r_cp])
    ctab = kb.sb([128, NG, TW + 1], F32, "s5ctab")
    stab = kb.sb([128, NG, TW + 1], F32, "s5stab")
    rtab = kb.sb([128, NG, TW], F32, "s5rtab")
    angt = kb.sb([128, TW + 1], F32, "angt")
    r_tab, r_angt = Res(), Res()
    for j in range(NG):
        kb.op("dve", lambda e, j=j: e.tensor_scalar(out=angt[:], in0=tau_t[:], scalar1=P["th"][:, j:j + 1], scalar2=None, op0=ALU.mult),
              reads=[rP, r_tau, r_angt], writes=[r_angt])
        emit_sincos(kb, angt[:], r_angt, stab[:, j, :], ctab[:, j, :], r_tab, [128, TW + 1], "sc%d" % (j + 1))
        kb.op("dve", lambda e, j=j: e.tensor_scalar(out=rtab[:, j, :], in0=tau_t[:, 0:TW], scalar1=0.0, scalar2=P["r"][:, j:j + 1],
                                                    op0=ALU.mult, op1=ALU.add), reads=[rP, r_tau], writes=[r_tab])
    if "w" not in sh:
        sfx_keep = kb.sfx
        kb.sfx = ""
        w_ = {}
        w_["ut"] = [kb.sb([128, TW], F32, "ut%d" % i) for i in range(2)]
        w_["r_ut"] = [Res(), Res()]
        w_["pbr"] = [kb.ps("pbr%d" % i) for i in range(2)]
        w_["pbi"] = [kb.ps("pbi%d" % i) for i in range(2)]
        w_["r_pbr"] = [Res(), Res()]
        w_["r_pbi"] = [Res(), Res()]
        w_["py"] = [kb.ps("py%d" % i) for i in range(2)]
        w_["r_py"] = [Res(), Res()]
        names_ = ["m1", "m2", "m3", "m4", "n1", "n2", "n3", "n4", "wr", "wi", "xr", "xi"]
        w_["T"] = {n: [kb.sb([128, TW], F32, "s5_%s%d" % (n, i)) for i in range(2)] for n in names_}
        w_["R"] = {n: [Res(), Res()] for n in names_}
        w_["pzr"] = kb.ps("pzr")
        w_["pzi"] = kb.ps("pzi")
        w_["r_pzr"], w_["r_pzi"] = Res(), Res()
        w_["carry"] = kb.sb([128, NG, 4], F32, "carry")
        w_["r_carry"] = [Res() for _ in range(NG)]
        w_["yo"] = [kb.sb([128, TW], F32, "yo%d" % i) for i in range(2)]
        w_["r_yo"] = [Res(), Res()]
        kb.sfx = sfx_keep
        sh["w"] = w_
    w_ = sh["w"]
    ut, r_ut, pbr, pbi, r_pbr, r_pbi, py, r_py = (w_[k] for k in ("ut", "r_ut", "pbr", "pbi", "r_pbr", "r_pbi", "py", "r_py"))
    T, R, pzr, pzi, r_pzr, r_pzi = (w_[k] for k in ("T", "R", "pzr", "pzi", "r_pzr", "r_pzi"))
    carry, r_carry, yo, r_yo = (w_[k] for k in ("carry", "r_carry", "yo", "r_yo"))
    its = []
    for b in range(NB):
        for ck in range(L // TW):
            for gp in range(NG):
                its.append((b, ck, gp))

    def stage_a(i):
        b, ck, gp = its[i]
        us, s = ck % 2, i % 2
        g = lambda n: T[n][s][:]
        rg = lambda n: R[n][s]
        if gp == 0:
            kb.dma(ut[us][:], hT[b, :, ck * TW:(ck + 1) * TW], writes=[r_ut[us]])
        kb.op("pe", lambda e: e.matmul(pbr[s][:], lhsT=bre_t[:, gp, :], rhs=ut[us][:], start=True, stop=True),
              reads=[r_bre, r_ut[us]], writes=[r_pbr[s]])
        kb.op("pe", lambda e: e.matmul(pbi[s][:], lhsT=bim_t[:, gp, :], rhs=ut[us][:], start=True, stop=True),
              reads=[r_bim, r_ut[us]], writes=[r_pbi[s]])
        cc = ctab[:, gp, 0:TW]
        ss = stab[:, gp, 0:TW]
        kb.op("dve", lambda e: e.tensor_tensor(out=g("m1"), in0=pbr[s][:], in1=cc, op=ALU.mult), reads=[r_pbr[s], r_tab], writes=[rg("m1")])
        kb.op("dve", lambda e: e.tensor_tensor(out=g("m2"), in0=pbi[s][:], in1=ss, op=ALU.mult), reads=[r_pbi[s], r_tab], writes=[rg("m2")])
        kb.op("dve", lambda e: e.tensor_tensor(out=g("m3"), in0=pbi[s][:], in1=cc, op=ALU.mult), reads=[r_pbi[s], r_tab], writes=[rg("m3")])
        kb.op("dve", lambda e: e.tensor_tensor(out=g("m4"), in0=pbr[s][:], in1=ss, op=ALU.mult), reads=[r_pbr[s], r_tab], writes=[rg("m4")])
        kb.op("pool", lambda e: e.tensor_tensor(out=g("wr"), in0=g("m1"), in1=g("m2"), op=ALU.add), reads=[rg("m1"), rg("m2")], writes=[rg("wr")])
        kb.op("pool", lambda e: e.tensor_tensor(out=g("wi"), in0=g("m3"), in1=g("m4"), op=ALU.subtract), reads=[rg("m3"), rg("m4")], writes=[rg("wi")])

    def stage_b(i):
        b, ck, gp = its[i]
        us, s = ck % 2, i % 2
        g = lambda n: T[n][s][:]
        rg = lambda n: R[n][s]
        cc = ctab[:, gp, 0:TW]
        ss = stab[:, gp, 0:TW]
        if ck == 0:
            ini_r, ini_i = 0.0, 0.0
        else:
            ini_r, ini_i = carry[:, gp, 2:3], carry[:, gp, 3:4]
        kb.op("dve", lambda e: e.tensor_tensor_scan(out=pzr[:], data0=rtab[:, gp, :], data1=g("wr"), initial=ini_r, op0=ALU.mult, op1=ALU.add),
              reads=[rg("wr"), r_tab, r_carry[gp]], writes=[r_pzr])
        kb.op("dve", lambda e: e.tensor_tensor_scan(out=pzi[:], data0=rtab[:, gp, :], data1=g("wi"), initial=ini_i, op0=ALU.mult, op1=ALU.add),
              reads=[rg("wi"), r_tab, r_carry[gp]], writes=[r_pzi])
        cT = ctab[:, gp, TW:TW + 1]
        sT = stab[:, gp, TW:TW + 1]
        kb.op("dve", lambda e: e.tensor_tensor(out=carry[:, gp, 0:1], in0=pzi[:, TW - 1:TW], in1=sT, op=ALU.mult),
              reads=[r_pzi, r_tab, r_carry[gp]], writes=[r_carry[gp]])
        kb.op("dve", lambda e: e.scalar_tensor_tensor(out=carry[:, gp, 2:3], in0=pzr[:, TW - 1:TW], scalar=cT, in1=carry[:, gp, 0:1],
                                                      op0=ALU.mult, op1=ALU.subtract), reads=[r_pzr, r_tab, r_carry[gp]], writes=[r_carry[gp]])
        kb.op("dve", lambda e: e.tensor_tensor(out=carry[:, gp, 1:2], in0=pzi[:, TW - 1:TW], in1=cT, op=ALU.mult),
              reads=[r_pzi, r_tab, r_carry[gp]], writes=[r_carry[gp]])
        kb.op("dve", lambda e: e.scalar_tensor_tensor(out=carry[:, gp, 3:4], in0=pzr[:, TW - 1:TW], scalar=sT, in1=carry[:, gp, 1:2],
                                                      op0=ALU.mult, op1=ALU.add), reads=[r_pzr, r_tab, r_carry[gp]], writes=[r_carry[gp]])
        kb.op("dve", lambda e: e.tensor_tensor(out=g("n1"), in0=pzr[:], in1=cc, op=ALU.mult), reads=[r_pzr, r_tab], writes=[rg("n1")])
        kb.op("dve", lambda e: e.tensor_tensor(out=g("n2"), in0=pzi[:], in1=ss, op=ALU.mult), reads=[r_pzi, r_tab], writes=[rg("n2")])
        kb.op("dve", lambda e: e.tensor_tensor(out=g("n3"), in0=pzr[:], in1=ss, op=ALU.mult), reads=[r_pzr, r_tab], writes=[rg("n3")])
        kb.op("dve", lambda e: e.tensor_tensor(out=g("n4"), in0=pzi[:], in1=cc, op=ALU.mult), reads=[r_pzi, r_tab], writes=[rg("n4")])
        kb.op("pool", lambda e: e.tensor_tensor(out=g("xr"), in0=g("n1"), in1=g("n2"), op=ALU.subtract), reads=[rg("n1"), rg("n2")], writes=[rg("xr")])
        kb.op("pool", lambda e: e.tensor_tensor(out=g("xi"), in0=g("n3"), in1=g("n4"), op=ALU.add), reads=[rg("n3"), rg("n4")], writes=[rg("xi")])
        kb.op("pe", lambda e: e.matmul(py[us][:], lhsT=crp[:, gp, :], rhs=g("xr"), start=(gp == 0), stop=False),
              reads=[r_cp, rg("xr")], writes=[r_py[us]])
        kb.op("pe", lambda e: e.matmul(py[us][:], lhsT=ncip[:, gp, :], rhs=g("xi"), start=False, stop=(gp == NG - 1)),
              reads=[r_cp, rg("xi")], writes=[r_py[us]])
        if gp == NG - 1:
            kb.op("dve", lambda e: e.scalar_tensor_tensor(out=yo[us][:], in0=ut[us][:], scalar=dsk_t[:, 0:1], in1=py[us][:],
                                                          op0=ALU.mult, op1=ALU.add), reads=[r_ut[us], r_dsk, r_py[us]], writes=[r_yo[us]])
            kb.dma(y[b, :, ck * TW:(ck + 1) * TW], yo[us][:], reads=[r_yo[us]], q="sp", is_output=True)

    stage_a(0)
    for i in range(len(its)):
        if i + 1 < len(its):
            stage_a(i + 1)
        stage_b(i)


NFFT = 16384
MAGIC = 12582912.0
PI = 3.141592653589793


def build_hy2(NCB=16):
    kb = KB()
    u3 = kb.inp("u3", [3, 2, 64, 128, 2, 128])
    zext = kb.inp("zext", [33, NFFT])
    w1 = kb.inp("w1", [33, 64])
    w2 = kb.inp("w2", [64, 64])
    bf1 = kb.inp("bf1", [64, 3])
    w3 = kb.inp("w3", [64, 4, 128])
    decf = kb.inp("decf", [128, 128, 128])
    decb = kb.inp("decb", [128, 128, 128])
    skp = kb.inp("skp", [64, 2, 128])
    ftab = kb.inp("ftab", [128, 4, 256])
    fri = kb.inp("fri", [128, 2, 128])
    tw = kb.inp("tw", [128, 2, 128])
    zout = kb.outp("zout", [2, 64, 128, 2, 128])

    w1_t, r_w1 = small_load(kb, w1, [33, 64], "w1_t")
    w2_t, r_w2 = small_load(kb, w2, [64, 64], "w2_t")
    bf_t, r_bf = small_load(kb, bf1, [64, 3], "bf_t")
    sk_t, r_sk = small_load(kb, skp, [64, 2, 128], "sk_t")
    ft_t, r_ft = small_load(kb, ftab, [128, 4, 256], "ft_t")
    fr_t, r_fr = small_load(kb, fri, [128, 2, 128], "fr_t")
    tw_t, r_tw = small_load(kb, tw, [128, 2, 128], "tw_t")
    w3_t = kb.sb([64, 4, 128], BF16, "w3_t")
    r_w3 = Res()
    kb.dma(w3_t[:], w3, writes=[r_w3], q="pool")
    onesf = kb.sb([128, 128], F32, "onesf")
    r_ones = Res()
    kb.op("dve", lambda e: e.memset(onesf[:], 1.0), writes=[r_ones])

    ps1 = kb.ps("ps1", (128, 2048))
    ps2 = kb.ps("ps2", (128, 2048))
    r_ps1, r_ps2 = Res(), Res()

    a2T = kb.sb([64, NFFT], BF16, "a2T")
    r_a2 = Res()
    zc = [kb.sb([33, 512], F32, "zc%d" % i) for i in range(2)]
    r_zc = [Res(), Res()]
    arg = kb.sb([64, 512], F32, "arg")
    kf = kb.sb([64, 512], F32, "kfm")
    a1c = kb.sb([64, 512], F32, "a1c")
    r_arg, r_kf, r_a1c = Res(), Res(), Res()

    def sin_rr(src_ps, r_src, bcol, out_ap, r_out):
        kb.op("dve", lambda e: e.tensor_scalar(out=arg[:], in0=src_ps, scalar1=bf_t[:, bcol:bcol + 1], scalar2=bf_t[:, 2:3], op0=ALU.add, op1=ALU.mult),
              reads=[r_src, r_bf], writes=[r_arg])
        kb.op("dve", lambda e: e.tensor_scalar(out=kf[:], in0=arg[:], scalar1=1.0 / TWO_PI, scalar2=MAGIC, op0=ALU.mult, op1=ALU.add), reads=[r_arg], writes=[r_kf])
        kb.op("dve", lambda e: e.tensor_scalar(out=kf[:], in0=kf[:], scalar1=-MAGIC, scalar2=None, op0=ALU.add), reads=[r_kf], writes=[r_kf])
        kb.op("dve", lambda e: e.scalar_tensor_tensor(out=arg[:], in0=kf[:], scalar=-TWO_PI, in1=arg[:], op0=ALU.mult, op1=ALU.add),
              reads=[r_kf, r_arg], writes=[r_arg])
        kb.op("dve", lambda e: e.tensor_scalar(out=arg[:], in0=arg[:], scalar1=-PI, scalar2=PI, op0=ALU.max, op1=ALU.min), reads=[r_arg], writes=[r_arg])
        kb.op("act", lambda e: e.activation(out=out_ap, in_=arg[:], func=AF.Sin), reads=[r_arg], writes=[r_out])

    for ck in range(NFFT // 512):
        s = ck % 2
        kb.dma(zc[s][:], zext[:, ck * 512:(ck + 1) * 512], writes=[r_zc[s]])
        kb.op("pe", lambda e: e.matmul(ps1[0:64, 0:512], lhsT=w1_t[:], rhs=zc[s][:], start=True, stop=True), reads=[r_w1, r_zc[s]], writes=[r_ps1])
        sin_rr(ps1[0:64, 0:512], r_ps1, 0, a1c[:], r_a1c)
        kb.op("pe", lambda e: e.matmul(ps2[0:64, 0:512], lhsT=w2_t[:], rhs=a1c[:], start=True, stop=True), reads=[r_w2, r_a1c], writes=[r_ps2])
        sin_rr(ps2[0:64, 0:512], r_ps2, 1, a2T[:, ck * 512:(ck + 1) * 512], r_a2)

    dft = kb.sb([128, 8, 128], F32, "dft")
    dbt = kb.sb([128, 8, 128], F32, "dbt")
    r_dft, r_dbt = Res(), Res()
    kft = kb.sb([128, 2, 8, 128], F32, "kft")
    r_kft = Res()
    ktmp = kb.sb([128, 8, 64], F32, "ktmp")
    r_ktmp = Res()
    part = kb.sb([128, 16], F32, "part")
    rS = kb.sb([128, 16], F32, "rS")
    r_part, r_rS = Res(), Res()
    KR = kb.sb([128, 2, 8, 128], F32, "KR")
    KI = kb.sb([128, 2, 8, 128], F32, "KI")
    r_K = Res()
    A1 = kb.sb([128, 8, 2, 128], F32, "A1")
    A2 = kb.sb([128, 8, 2, 128], F32, "A2")
    r_A = Res()
    P1 = kb.sb([128, 8, 2, 128], F32, "P1")
    P2 = kb.sb([128, 8, 2, 128], F32, "P2")
    r_P = Res()
    M = [kb.sb([128, 8, 128], F32, "mm%d" % i) for i in range(4)]
    r_M = Res()
    X = kb.sb([64, 8, 2, 128], F32, "X")
    G = kb.sb([64, 8, 2, 128], F32, "G")
    Z1 = kb.sb([64, 8, 2, 128], F32, "Z1")
    Z2 = kb.sb([64, 8, 2, 128], F32, "Z2")
    T_ = kb.sb([64, 8, 2, 128], F32, "Tt")
    r_X, r_G, r_Z1, r_Z2, r_T = Res(), Res(), Res(), Res(), Res()
    twr_b = tw_t[:, 0, :].unsqueeze(1).to_broadcast([128, 8, 128])
    twi_b = tw_t[:, 1, :].unsqueeze(1).to_broadcast([128, 8, 128])

    def cplx_evac(ps, r_ps, tr, ti, r_t, conj, O1, O2, r_O, arr):
        v = ps.rearrange("k (a c n) -> k a c n", a=8, c=2)
        pre, pim = v[:, :, 0, :], v[:, :, 1, :]
        rd = [r_ps, r_t]
        kb.op("dve", lambda e: e.tensor_tensor(out=M[0][:], in0=pre, in1=tr, op=ALU.mult), reads=rd, writes=[r_M])
        kb.op("dve", lambda e: e.tensor_tensor(out=M[1][:], in0=pim, in1=ti, op=ALU.mult), reads=rd, writes=[r_M])
        kb.op("dve", lambda e: e.tensor_tensor(out=M[2][:], in0=pre, in1=ti, op=ALU.mult), reads=rd, writes=[r_M])
        kb.op("dve", lambda e: e.tensor_tensor(out=M[3][:], in0=pim, in1=tr, op=ALU.mult), reads=rd, writes=[r_M])
        if not conj:
            kb.op("pool", lambda e: e.tensor_tensor(out=O1[:, :, 0, :], in0=M[0][:], in1=M[1][:], op=ALU.subtract), reads=[r_M], writes=[r_O])
            kb.op("pool", lambda e: e.tensor_tensor(out=O1[:, :, 1, :], in0=M[2][:], in1=M[3][:], op=ALU.add), reads=[r_M], writes=[r_O])
        else:
            kb.op("pool", lambda e: e.tensor_tensor(out=O1[:, :, 0, :], in0=M[0][:], in1=M[1][:], op=ALU.add), reads=[r_M], writes=[r_O])
            kb.op("pool", lambda e: e.tensor_tensor(out=O1[:, :, 1, :], in0=M[3][:], in1=M[2][:], op=ALU.subtract), reads=[r_M], writes=[r_O])
        if arr == "fwd":
            kb.op("act", lambda e: e.activation(out=O2[:, :, 0, :], in_=O1[:, :, 1, :], func=AF.Copy, scale=-1.0), reads=[r_O], writes=[r_O])
            kb.op("act", lambda e: e.activation(out=O2[:, :, 1, :], in_=O1[:, :, 0, :], func=AF.Copy), reads=[r_O], writes=[r_O])
        else:
            kb.op("act", lambda e: e.activation(out=O2[:, :, 0, :], in_=O1[:, :, 1, :], func=AF.Copy), reads=[r_O], writes=[r_O])
            kb.op("act", lambda e: e.activation(out=O2[:, :, 1, :], in_=O1[:, :, 0, :], func=AF.Copy, scale=-1.0), reads=[r_O], writes=[r_O])

    def stage2(I1, I2, r_I, M_rows):
        f1 = I1.rearrange("k a c n -> k (a c n)")
        f2 = I2.rearrange("k a c n -> k (a c n)")
        for q in range(4):
            kb.op("pe", lambda e, q=q: e.matmul(ps2[0:M_rows, q * 512:(q + 1) * 512], lhsT=fr_t[:, 0, 0:M_rows], rhs=f1[:, q * 512:(q + 1) * 512],
                                                start=True, stop=False), reads=[r_fr, r_I], writes=[r_ps2])
            kb.op("pe", lambda e, q=q: e.matmul(ps2[0:M_rows, q * 512:(q + 1) * 512], lhsT=fr_t[:, 1, 0:M_rows], rhs=f2[:, q * 512:(q + 1) * 512],
                                                start=False, stop=True), reads=[r_fr, r_I], writes=[r_ps2])

    def conv_pass(src, r_src, o):
        for c in range(8):
            kb.op("pe", lambda e, c=c: e.matmul(ps1[:, c * 256:(c + 1) * 256], lhsT=src[:, c, 0, :], rhs=ft_t[0:64, 0, :], start=True, stop=False),
                  reads=[r_src, r_ft], writes=[r_ps1])
            kb.op("pe", lambda e, c=c: e.matmul(ps1[:, c * 256:(c + 1) * 256], lhsT=src[:, c, 1, :], rhs=ft_t[0:64, 1, :], start=False, stop=True),
                  reads=[r_src, r_ft], writes=[r_ps1])
        cplx_evac(ps1, r_ps1, twr_b, twi_b, r_tw, False, A1, A2, r_A, "fwd")
        stage2(A1, A2, r_A, 128)
        cplx_evac(ps2, r_ps2, KR[:, o], KI[:, o], r_K, False, P1, P2, r_P, "inv")
        for c in range(8):
            kb.op("pe", lambda e, c=c: e.matmul(ps1[:, c * 256:(c + 1) * 256], lhsT=P1[:, c, 0, :], rhs=ft_t[:, 2, :], start=True, stop=False),
                  reads=[r_P, r_ft], writes=[r_ps1])
            kb.op("pe", lambda e, c=c: e.matmul(ps1[:, c * 256:(c + 1) * 256], lhsT=P1[:, c, 1, :], rhs=ft_t[:, 3, :], start=False, stop=True),
                  reads=[r_P, r_ft], writes=[r_ps1])
        cplx_evac(ps1, r_ps1, twr_b, twi_b, r_tw, True, A1, A2, r_A, "inv")
        stage2(A1, A2, r_A, 64)

    for cb in range(NCB):
        ch0 = 8 * cb
        kb.dma(dft[:], decf[:, ch0:ch0 + 8, :], writes=[r_dft])
        kb.dma(dbt[:], decb[:, ch0:ch0 + 8, :], writes=[r_dbt])
        for p in range(128):
            psx, r_psx = (ps1, r_ps1) if p < 64 else (ps2, r_ps2)
            pp = p % 64
            kb.op("pe", lambda e, p=p, pp=pp, psx=psx: e.matmul(psx[:, pp * 32:(pp + 1) * 32], lhsT=a2T[:, p:NFFT:128], rhs=w3_t[:, :, ch0:ch0 + 8],
                                                              start=True, stop=True), reads=[r_a2, r_w3], writes=[r_psx])
        for o in range(2):
            for half in range(2):
                psx, r_psx = (ps1, r_ps1) if half == 0 else (ps2, r_ps2)
                vw = psx.rearrange("i (p q c) -> i q c p", q=4, c=8)
                hs = slice(half * 64, half * 64 + 64)
                kb.op("dve", lambda e, vw=vw, hs=hs, o=o: e.tensor_tensor(out=kft[:, o, :, hs], in0=vw[:, 2 * o], in1=dft[:, :, hs], op=ALU.mult),
                      reads=[r_psx, r_dft], writes=[r_kft])
                kb.op("dve", lambda e, vw=vw, hs=hs, o=o: e.tensor_tensor(out=ktmp[:], in0=vw[:, 2 * o + 1], in1=dbt[:, :, hs], op=ALU.mult),
                      reads=[r_psx, r_dbt], writes=[r_ktmp])
                kb.op("pool", lambda e, hs=hs, o=o: e.tensor_tensor(out=kft[:, o, :, hs], in0=kft[:, o, :, hs], in1=ktmp[:], op=ALU.add),
                      reads=[r_ktmp, r_kft], writes=[r_kft])
        kb.op("dve", lambda e: e.tensor_reduce(out=part[:], in_=kft[:].rearrange("i o c p -> i (o c) p"), axis=AX.X, op=ALU.add, apply_absolute_value=True),
              reads=[r_kft], writes=[r_part])
        kb.op("pe", lambda e: e.matmul(ps1[:, 0:16], lhsT=onesf[:], rhs=part[:], start=True, stop=True), reads=[r_ones, r_part], writes=[r_ps1])
        kb.op("dve", lambda e: e.tensor_scalar(out=rS[:], in0=ps1[:, 0:16], scalar1=float(NFFT), scalar2=None, op0=ALU.mult), reads=[r_ps1], writes=[r_rS])
        kb.op("dve", lambda e: e.reciprocal(out=rS[:], in_=rS[:]), reads=[r_rS], writes=[r_rS])
        for o in range(2):
            for c in range(8):
                kb.op("pe", lambda e, c=c, o=o: e.matmul(ps1[:, c * 256:(c + 1) * 256], lhsT=kft[:, o, c, :], rhs=ft_t[:, 0, :], start=True, stop=True),
                      reads=[r_kft, r_ft], writes=[r_ps1])
            cplx_evac(ps1, r_ps1, twr_b, twi_b, r_tw, False, A1, A2, r_A, "fwd")
            stage2(A1, A2, r_A, 128)
            v2 = ps2.rearrange("k (a c n) -> k a c n", a=8, c=2)
            rsb = rS[:, o * 8:(o + 1) * 8].unsqueeze(2).to_broadcast([128, 8, 128])
            kb.op("dve", lambda e, o=o, v2=v2, rsb=rsb: e.tensor_tensor(out=KR[:, o], in0=v2[:, :, 0, :], in1=rsb, op=ALU.mult),
                  reads=[r_ps2, r_rS], writes=[r_K])
            kb.op("dve", lambda e, o=o, v2=v2, rsb=rsb: e.tensor_tensor(out=KI[:, o], in0=v2[:, :, 1, :], in1=rsb, op=ALU.mult),
                  reads=[r_ps2, r_rS], writes=[r_K])
        for pr in range(2):
            kb.dma(X[:], u3[0, pr, :, ch0:ch0 + 8, :, :], writes=[r_X])
            kb.dma(G[:], u3[1, pr, :, ch0:ch0 + 8, :, :], writes=[r_G])
            conv_pass(X, r_X, 0)
            sk0 = sk_t[:, 0, ch0:ch0 + 8].unsqueeze(2).to_broadcast([64, 8, 256])
            sk1 = sk_t[:, 1, ch0:ch0 + 8].unsqueeze(2).to_broadcast([64, 8, 256])
            fl = lambda t: t[:].rearrange("i a c n -> i a (c n)")
            yv = ps2[0:64, :].rearrange("i (a n) -> i a n", a=8)
            kb.op("dve", lambda e: e.tensor_tensor(out=fl(T_), in0=fl(X), in1=sk0, op=ALU.mult), reads=[r_X, r_sk], writes=[r_T])
            kb.op("dve", lambda e: e.tensor_tensor(out=fl(T_), in0=fl(T_), in1=yv, op=ALU.add), reads=[r_T, r_ps2], writes=[r_T])
            kb.op("pool", lambda e: e.tensor_tensor(out=fl(Z1), in0=fl(T_), in1=fl(G), op=ALU.mult), reads=[r_T, r_G], writes=[r_Z1])
            kb.dma(G[:], u3[2, pr, :, ch0:ch0 + 8, :, :], writes=[r_G])
            conv_pass(Z1, r_Z1, 1)
            kb.op("dve", lambda e: e.tensor_tensor(out=fl(T_), in0=fl(Z1), in1=sk1, op=ALU.mult), reads=[r_Z1, r_sk], writes=[r_T])
            kb.op("dve", lambda e: e.tensor_tensor(out=fl(T_), in0=fl(T_), in1=yv, op=ALU.add), reads=[r_T, r_ps2], writes=[r_T])
            kb.op("pool", lambda e: e.tensor_tensor(out=fl(Z2), in0=fl(T_), in1=fl(G), op=ALU.mult), reads=[r_T, r_G], writes=[r_Z2])
            kb.dma(zout[pr, :, ch0:ch0 + 8, :, :], Z2[:], reads=[r_Z2], q="sp", is_output=True)
    return kb.finish()


def build_tok(kind, NT):
    kb = KB()
    halo = 2 if kind == "hy1" else 0
    TW = 256 if kind == "hy1" else 512
    W = TW + halo
    xT = kb.inp("xT", [128, 8, NT + halo])
    c_in = kb.inp("c2", [128, 8, 2])
    adaw = kb.inp("adaw", [1024, 3 * D])
    adab = kb.inp("adab", [128, 24])
    pss = kb.ps("pmod")
    r_pss = Res()
    mod, r_mod = emit_mod(kb, c_in, adaw, adab, 24, "mod", ps=pss, r_ps_in=r_pss)
    xt = [kb.sb([128, 8, W], F32, "xt%d" % i) for i in range(2)]
    r_xt = [Res(), Res()]
    tmp = kb.sb([128, W], F32, "tmp")
    r_tmp = Res()
    pa = [kb.ps("pa%d" % i) for i in range(2)]
    pb = [kb.ps("pb%d" % i) for i in range(2)]
    r_pa = [Res(), Res()]
    r_pb = [Res(), Res()]
    if kind in ("hy1", "s5a"):
        ng = kb.inp("ng", [128, 8])
        ng_t, r_ng = small_load(kb, ng, [128, 8], "ng_t")
        gs, r_gs = emit_gs(kb, mod, r_mod, 8, ng_t, r_ng, "gs")
        nrm = Norm(kb, W, "nrm").init_eps()
    if kind == "hy1":
        win = kb.inp("win", [D, 3 * D])
        cw = kb.inp("cw", [128, 72])
        cb = kb.inp("cb", [128, 24])
        msk = kb.inp("msk", [128, 2])
        out = kb.outp("out", [128, 24, NT])
        r_w = Res()
        w_t = load_cast_weight(kb, win, 8, 3 * D, "w_t", r_w)
        cw_t, r_cw = small_load(kb, cw, [128, 72], "cw_t")
        cb_t, r_cb = small_load(kb, cb, [128, 24], "cb_t")
        mk_t, r_mk = small_load(kb, msk, [128, 2], "mk_t")
        h = kb.sb([128, 8, W], BF16, "h")
        r_h = Res()
        uo = kb.sb([128, 24, TW], F32, "uo")
        r_uo = Res()
    elif kind == "s5a":
        out = kb.outp("out", [128, 8, NT])
        ho = kb.sb([128, 8, W], F32, "ho")
        r_ho = Res()
    elif kind == "hypost":
        zin = kb.inp("z", [128, 8, NT])
        wout = kb.inp("wout", [D, D])
        out = kb.outp("out", [128, 8, NT])
        r_w = Res()
        w_t = load_cast_weight(kb, wout, 8, D, "w_t", r_w)
        zt = kb.sb([128, 8, W], F32, "zt")
        zb = kb.sb([128, 8, W], BF16, "zb")
        r_zt, r_zb = Res(), Res()
        xo = kb.sb([128, 8, W], F32, "xo")
        r_xo = Res()
    elif kind == "s5post":
        yf = kb.inp("yf", [128, 8, NT])
        yb = kb.inp("yb", [128, 8, NT])
        wglu = kb.inp("wglu", [D, 2 * D])
        out = kb.outp("out", [128, 8, NT])
        r_w = Res()
        w_t = load_cast_weight(kb, wglu, 8, 2 * D, "w_t", r_w)
        y1 = kb.sb([128, 8, W], F32, "y1")
        y2 = kb.sb([128, 8, W], F32, "y2")
        r_y1, r_y2 = Res(), Res()
        t3 = kb.sb([128, 8, W], F32, "t3")
        r_t3 = Res()
        gl = kb.sb([128, 8, W], BF16, "gl")
        r_gl = Res()
        sg = kb.sb([128, W], F32, "sg")
        r_sg = Res()
        xo = kb.sb([128, 8, W], F32, "xo")
        r_xo = Res()
    ntiles = (NT + TW - 1) // TW
    for ti in range(ntiles):
        o0 = ti * TW
        o1 = min(NT, o0 + TW)
        w = o1 - o0 + halo
        wo_ = o1 - o0
        s = ti % 2
        kb.dma(xt[s][:, :, :w], xT[:, :, o0:o0 + w], writes=[r_xt[s]])
        if kind in ("hy1", "s5a"):
            rstd, r_rstd = nrm.emit(xt[s][:, :, :w], r_xt[s], w)
            for k in range(8):
                kb.op("dve", lambda e, k=k: e.scalar_tensor_tensor(out=tmp[:, :w], in0=xt[s][:, k, :w], scalar=gs[:, k:k + 1],
                                                                    in1=rstd, op0=ALU.mult, op1=ALU.mult),
                      reads=[r_xt[s], r_gs, r_rstd], writes=[r_tmp])
                if kind == "hy1":
                    kb.op("act", lambda e, k=k: e.activation(out=h[:, k, :w], in_=tmp[:, :w], func=AF.Identity, bias=mod[:, k:k + 1], scale=1.0),
                          reads=[r_tmp, r_mod], writes=[r_h])
                else:
                    kb.op("act", lambda e, k=k: e.activation(out=ho[:, k, :w], in_=tmp[:, :w], func=AF.Identity, bias=mod[:, k:k + 1], scale=1.0),
                          reads=[r_tmp, r_mod], writes=[r_ho])
        if kind == "s5a":
            kb.dma(out[:, :, o0:o1], ho[:, :, :w], reads=[r_ho], q="sp", is_output=True)
        elif kind == "hy1":
            if ti == 0:
                kb.op("dve", lambda e: e.tensor_scalar(out=h[:, :, 0:1], in0=h[:, :, 0:1], scalar1=mk_t[:, 0:1], scalar2=None, op0=ALU.mult),
                      reads=[r_mk, r_h], writes=[r_h])
            if ti == ntiles - 1:
                kb.op("dve", lambda e: e.tensor_scalar(out=h[:, :, w - 1:w], in0=h[:, :, w - 1:w], scalar1=mk_t[:, 1:2], scalar2=None, op0=ALU.mult),
                      reads=[r_mk, r_h], writes=[r_h])
            for j in range(24):
                q = j % 2
                for k in range(8):
                    kb.op("pe", lambda e, k=k, j=j, q=q: e.matmul(pa[q][:, :w], lhsT=w_t[:, k, j * 128:(j + 1) * 128], rhs=h[:, k, :w],
                                                                   start=(k == 0), stop=(k == 7)), reads=[r_w, r_h], writes=[r_pa[q]])
                kb.op("act", lambda e, j=j, q=q: e.activation(out=uo[:, j, :wo_], in_=pa[q][:, 1:w - 1], func=AF.Identity,
                                                              scale=cw_t[:, 3 * j + 1:3 * j + 2], bias=cb_t[:, j:j + 1]),
                      reads=[r_pa[q], r_cw, r_cb], writes=[r_uo])
                kb.op("dve", lambda e, j=j, q=q: e.scalar_tensor_tensor(out=uo[:, j, :wo_], in0=pa[q][:, 0:w - 2], scalar=cw_t[:, 3 * j:3 * j + 1],
                                                                        in1=uo[:, j, :wo_], op0=ALU.mult, op1=ALU.add),
                      reads=[r_pa[q], r_cw, r_uo], writes=[r_uo])
                kb.op("dve", lambda e, j=j, q=q: e.scalar_tensor_tensor(out=uo[:, j, :wo_], in0=pa[q][:, 2:w], scalar=cw_t[:, 3 * j + 2:3 * j + 3],
                                                                        in1=uo[:, j, :wo_], op0=ALU.mult, op1=ALU.add),
                      reads=[r_pa[q], r_cw, r_uo], writes=[r_uo])
            kb.dma(out[:, :, o0:o1], uo[:, :, :wo_], reads=[r_uo], q="sp", is_output=True)
        elif kind == "hypost":
            kb.dma(zt[:, :, :w], zin[:, :, o0:o1], writes=[r_zt])
            kb.op("act", lambda e: e.activation(out=zb[:, :, :w], in_=zt[:, :, :w], func=AF.Copy), reads=[r_zt], writes=[r_zb])
            for m in range(8):
                q = m % 2
                for k in range(8):
                    kb.op("pe", lambda e, k=k, m=m, q=q: e.matmul(pa[q][:, :w], lhsT=w_t[:, k, m * 128:(m + 1) * 128], rhs=zb[:, k, :w],
                                                                   start=(k == 0), stop=(k == 7)), reads=[r_w, r_zb], writes=[r_pa[q]])
                kb.op("dve", lambda e, m=m, q=q: e.scalar_tensor_tensor(out=xo[:, m, :w], in0=pa[q][:, :w], scalar=mod[:, 16 + m:17 + m],
                                                                        in1=xt[s][:, m, :w], op0=ALU.mult, op1=ALU.add),
                      reads=[r_pa[q], r_mod, r_xt[s]], writes=[r_xo])
            kb.dma(out[:, :, o0:o1], xo[:, :, :w], reads=[r_xo], q="sp", is_output=True)
        elif kind == "s5post":
            kb.dma(y1[:, :, :w], yf[:, :, o0:o1], writes=[r_y1])
            kb.dma(y2[:, :, :w], yb[:, :, o0:o1], writes=[r_y2])
            a3 = lambda t: t[:, :, :w]
            kb.op("dve", lambda e: e.tensor_tensor(out=a3(y1), in0=a3(y1), in1=a3(y2), op=ALU.add), reads=[r_y1, r_y2], writes=[r_y1])
            kb.op("pool", lambda e: e.tensor_tensor(out=a3(t3), in0=a3(y1), in1=a3(y1), op=ALU.mult), reads=[r_y1], writes=[r_t3])
            kb.op("dve", lambda e: e.tensor_scalar(out=a3(t3), in0=a3(t3), scalar1=0.044715, scalar2=1.0, op0=ALU.mult, op1=ALU.add), reads=[r_t3], writes=[r_t3])
            kb.op("pool", lambda e: e.tensor_tensor(out=a3(t3), in0=a3(t3), in1=a3(y1), op=ALU.mult), reads=[r_y1, r_t3], writes=[r_t3])
            kb.op("act", lambda e: e.activation(out=a3(t3), in_=a3(t3), func=AF.Sigmoid, scale=1.5957691216057308), reads=[r_t3], writes=[r_t3])
            kb.op("dve", lambda e: e.tensor_tensor(out=a3(gl), in0=a3(t3), in1=a3(y1), op=ALU.mult), reads=[r_y1, r_t3], writes=[r_gl])
            for m in range(8):
                q = m % 2
                for k in range(8):
                    kb.op("pe", lambda e, k=k, m=m, q=q: e.matmul(pa[q][:, :w], lhsT=w_t[:, k, m * 128:(m + 1) * 128], rhs=gl[:, k, :w],
                                                                   start=(k == 0), stop=(k == 7)), reads=[r_w, r_gl], writes=[r_pa[q]])
                for k in range(8):
                    kb.op("pe", lambda e, k=k, m=m, q=q: e.matmul(pb[q][:, :w], lhsT=w_t[:, k, D + m * 128:D + (m + 1) * 128], rhs=gl[:, k, :w],
                                                                   start=(k == 0), stop=(k == 7)), reads=[r_w, r_gl], writes=[r_pb[q]])
                kb.op("act", lambda e, q=q: e.activation(out=sg[:, :w], in_=pb[q][:, :w], func=AF.Sigmoid), reads=[r_pb[q]], writes=[r_sg])
                kb.op("dve", lambda e, q=q: e.tensor_tensor(out=sg[:, :w], in0=sg[:, :w], in1=pa[q][:, :w], op=ALU.mult), reads=[r_sg, r_pa[q]], writes=[r_sg])
                kb.op("dve", lambda e, m=m: e.scalar_tensor_tensor(out=xo[:, m, :w], in0=sg[:, :w], scalar=mod[:, 16 + m:17 + m],
                                                                   in1=xt[s][:, m, :w], op0=ALU.mult, op1=ALU.add),
                      reads=[r_sg, r_mod, r_xt[s]], writes=[r_xo])
            kb.dma(out[:, :, o0:o1], xo[:, :, :w], reads=[r_xo], q="sp", is_output=True)
    return kb.finish()


import math
import numpy as np
L = 8192; NFFT = 16384; Dm = 1024

def hy_consts():
    t = np.linspace(0.0, 1.0, L, dtype=np.float32)[:, None]
    bands = 16
    w = (2.0 * math.pi * np.arange(L, dtype=np.float32) / L).astype(np.float32)
    f = np.linspace(1e-4, bands - 1, bands, dtype=np.float32)
    ang = w[:, None] * f[None, :]
    z = np.concatenate([t, np.cos(ang), -np.sin(ang)], -1).astype(np.float32)
    deltas = np.abs(np.linspace(math.log(1e-2) / 1.5, math.log(1e-2) / 0.3, Dm, dtype=np.float32))
    decay = np.exp(-t * deltas[None, :]).astype(np.float32)
    idx = np.arange(NFFT)
    src = np.where(idx < L, idx, np.where(idx == L, 0, 2 * L - idx))
    zext = np.ascontiguousarray(z[src].T)
    dec_f = np.where((idx < L)[:, None], decay[src], 0.0).astype(np.float32)
    dec_b = np.where((idx > L)[:, None], decay[src], 0.0).astype(np.float32)
    k = np.arange(128)
    F = np.exp(-2j * np.pi * np.outer(k, k) / 128)
    Fr = F.real.astype(np.float32); Fi = F.imag.astype(np.float32)
    ftab = np.stack([np.concatenate([Fr, Fi], 1), np.concatenate([-Fi, Fr], 1),
                     np.concatenate([Fr, -Fi], 1), np.concatenate([Fi, Fr], 1)], 1)
    fri = np.stack([Fr, Fi], 1)
    T = np.exp(-2j * np.pi * np.outer(k, k) / NFFT)
    tw = np.stack([T.real.astype(np.float32), T.imag.astype(np.float32)], 1)
    return dict(z=z, decay=decay, zext=zext, dec_f=dec_f, dec_b=dec_b, ftab=np.ascontiguousarray(ftab.astype(np.float32)),
                fri=np.ascontiguousarray(fri), tw=np.ascontiguousarray(tw))

def dec_core(dec, core):
    return np.ascontiguousarray(dec[:, 128 * core:128 * core + 128].reshape(128, 128, 128).transpose(0, 2, 1))

def to_u3(u, core):
    a = u.reshape(2, 2, 64, 128, 3, Dm)[..., 128 * core:128 * core + 128]
    return np.ascontiguousarray(a.transpose(4, 0, 2, 5, 1, 3))

def from_zout(zs):
    out = np.zeros((4, L, Dm), np.float32)
    for core, zc in enumerate(zs):
        a = zc.transpose(0, 3, 1, 4, 2)
        out[:, :, 128 * core:128 * core + 128] = a.reshape(4, L, 128)
    return out


_PROGS = {}


def _prog(key, fn):
    if key not in _PROGS:
        _PROGS[key] = fn()
    return _PROGS[key]


def _fm(x):
    T, C = x.shape
    return np.ascontiguousarray(x.T.reshape(C // 128, 128, T).transpose(1, 0, 2))


def _unfm(a):
    return np.ascontiguousarray(a.transpose(2, 1, 0).reshape(a.shape[2], -1))


def _vfm(v):
    return np.ascontiguousarray(np.asarray(v, np.float32).reshape(-1, 128).T)


def _run(nc, in_maps):
    res = run_bass_kernel_spmd(nc, in_maps, core_ids=list(range(8)))
    return res.results


NTC = 4096
SEQ = 8192
NBATCH = 4
SWAP = np.arange(64) ^ 1


def _rope_fm(Ls):
    rows = Ls // 64
    nf = 16
    inv = (1.0 / (np.float32(10000.0) ** (np.arange(nf, dtype=np.float32) / np.float32(nf)))).astype(np.float32)
    r = np.arange(rows, dtype=np.float32)
    col = np.arange(64, dtype=np.float32)
    ang_r = np.broadcast_to(r[:, None, None] * inv, (rows, 64, nf))
    ang_c = np.broadcast_to(col[None, :, None] * inv, (rows, 64, nf))
    ang = np.concatenate([ang_r, ang_c], -1).reshape(Ls, 2 * nf).astype(np.float32)
    cos, sin = np.cos(ang), np.sin(ang)
    C = np.repeat(cos, 2, axis=1).T
    S = np.repeat(sin, 2, axis=1).T.copy()
    S[0::2] *= -1
    return np.ascontiguousarray(np.stack([np.concatenate([C, C], 0), np.concatenate([S, S], 0)], 0).astype(np.float32))


def _common(c, adaw, adab, b):
    return {"c2": np.ascontiguousarray(np.repeat(_vfm(c[b])[:, :, None], 2, axis=2)), "adaw": np.ascontiguousarray(adaw), "adab": _vfm(adab)}


def _halo_x(xs, b, half):
    xp = np.pad(xs[b], ((1, 1), (0, 0)))
    return _fm(xp[half * NTC: half * NTC + NTC + 2])


def _msk(half):
    return np.tile(np.array([[0.0 if half == 0 else 1.0, 1.0 if half == 0 else 0.0]], np.float32), (128, 1))


def _gather_tok(results, key="out"):
    xs = np.zeros((NBATCH, SEQ, D), np.float32)
    for core in range(8):
        b, half = core // 2, core % 2
        xs[b, half * NTC:(half + 1) * NTC] = _unfm(results[core][key])
    return xs


def run_attn(xs, c, adaw, adab, ng, w_qkv, w_o, qg, kg):
    nc = _prog("attn", lambda: build_attn(L=SEQ, NQ=NTC))
    wq_ = w_qkv[:, :1024].reshape(D, 16, 64)
    wk_ = w_qkv[:, 1024:1280].reshape(D, 4, 64)
    wv_ = w_qkv[:, 1280:]
    wq_p = wq_[:, QPERM, :]
    wq_all = np.ascontiguousarray(np.concatenate([wq_p.reshape(D, 1024), wq_p[:, :, SWAP].reshape(D, 1024)], 1))
    wkv = np.ascontiguousarray(np.concatenate([wk_.reshape(D, 256), wk_[:, :, SWAP].reshape(D, 256), wv_], 1))
    wo = np.ascontiguousarray(w_o)
    gains = np.ascontiguousarray(np.stack([np.tile(qg, 2), np.tile(qg[SWAP], 2), np.tile(kg, 2), np.tile(kg[SWAP], 2)], 1).astype(np.float32))
    rp = _rope_fm(SEQ)
    maps = []
    for core in range(8):
        b, half = core // 2, core % 2
        xf = _fm(xs[b])
        m = _common(c, adaw, adab, b)
        m.update({"xall": xf, "xq": np.ascontiguousarray(xf[:, :, half * NTC:(half + 1) * NTC]), "ng": _vfm(ng), "wkv": wkv, "wq": wq_all,
                  "wo": wo, "gains": gains, "ropek": rp, "ropeq": np.ascontiguousarray(rp[:, :, half * NTC:(half + 1) * NTC])})
        maps.append(m)
    return _gather_tok(_run(nc, maps))


def run_ffn(xs, c, adaw, adab, ng, wup, cw, cb, wdn, final_g=None):
    final = final_g is not None
    nc = _prog("ffn%d" % final, lambda: build_ffn(NTC, final=final))
    cwl = np.ascontiguousarray(cw.T.reshape(22, 128, 3).transpose(1, 0, 2).reshape(128, 66))
    maps = []
    for core in range(8):
        b, half = core // 2, core % 2
        m = _common(c, adaw, adab, b)
        m.update({"xT": _halo_x(xs, b, half), "ng": _vfm(ng), "wup": np.ascontiguousarray(wup), "wdn": np.ascontiguousarray(wdn),
                  "cw": cwl, "cb": _vfm(cb), "msk": _msk(half)})
        if final:
            m["fg"] = _vfm(final_g)
        maps.append(m)
    return _gather_tok(_run(nc, maps))


def run_hyena(xs, c, adaw, adab, ng, w_in, conv_w, conv_b, f_w1, f_b1, f_w2, f_b2, f_w3, f_freq, skip, w_out):
    nc1 = _prog("hy1", lambda: build_tok("hy1", NTC))
    cwl = np.ascontiguousarray(conv_w.T.reshape(24, 128, 3).transpose(1, 0, 2).reshape(128, 72))
    maps = []
    for core in range(8):
        b, half = core // 2, core % 2
        m = _common(c, adaw, adab, b)
        m.update({"xT": _halo_x(xs, b, half), "ng": _vfm(ng), "win": np.ascontiguousarray(w_in), "cw": cwl, "cb": _vfm(conv_b), "msk": _msk(half)})
        maps.append(m)
    u = _gather_tok_c(_run(nc1, maps), 3 * D)
    nc2 = _prog("hy2", lambda: build_hy2(NCB=16))
    C = hy_consts()
    maps = []
    for core in range(8):
        sl = slice(128 * core, 128 * core + 128)
        maps.append(dict(u3=to_u3(u, core), zext=C["zext"], w1=np.ascontiguousarray(f_w1), w2=np.ascontiguousarray(f_w2),
                         bf1=np.ascontiguousarray(np.stack([f_b1, f_b2, f_freq], 1)),
                         w3=np.ascontiguousarray(f_w3.reshape(64, 4, D)[:, :, sl]), decf=dec_core(C["dec_f"], core), decb=dec_core(C["dec_b"], core),
                         skp=np.ascontiguousarray(np.tile(skip[None, :, sl], (64, 1, 1))), ftab=C["ftab"], fri=C["fri"], tw=C["tw"]))
    r = _run(nc2, maps)
    z = from_zout([r[cc]["zout"] for cc in range(8)])
    nc3 = _prog("hypost", lambda: build_tok("hypost", NTC))
    maps = []
    for core in range(8):
        b, half = core // 2, core % 2
        sl = slice(half * NTC, (half + 1) * NTC)
        m = _common(c, adaw, adab, b)
        m.update({"xT": _fm(xs[b, sl]), "z": _fm(z[b, sl]), "wout": np.ascontiguousarray(w_out)})
        maps.append(m)
    return _gather_tok(_run(nc3, maps))


def _gather_tok_c(results, C_):
    o = np.zeros((NBATCH, SEQ, C_), np.float32)
    for core in range(8):
        b, half = core // 2, core % 2
        o[b, half * NTC:(half + 1) * NTC] = _unfm(results[core]["out"])
    return o


def _s5_params(A_re, A_im, log_dt, B_re, B_im, C_re, C_im, core, TW=512):
    gs = [8 * core + k for k in range(8)]
    are = np.zeros((128, 8), np.float32); aim = np.zeros((128, 8), np.float32); ldt = np.zeros((128, 8), np.float32)
    bre = np.zeros((128, 4, 128), np.float32); bim = np.zeros((128, 4, 128), np.float32)
    cre = np.zeros((128, 4, 128), np.float32); cim = np.zeros((128, 4, 128), np.float32)
    for gp in range(4):
        for g2 in range(2):
            g = gs[2 * gp + g2]
            sl = slice(64 * g2, 64 * g2 + 64)
            are[sl, gp] = A_re[g]; aim[sl, gp] = A_im[g]; ldt[sl, gp] = log_dt[g]
            rows = slice(16 * (2 * gp + g2), 16 * (2 * gp + g2) + 16)
            bre[rows, gp, sl] = B_re[g].T
            bim[rows, gp, sl] = B_im[g].T
            cre[sl, gp, rows] = C_re[g].T
            cim[sl, gp, rows] = C_im[g].T
    are[:, 4:] = are[:, :4]; aim[:, 4:] = aim[:, :4]; ldt[:, 4:] = ldt[:, :4]
    tau = np.tile(np.arange(TW + 1, dtype=np.float32)[None], (128, 1))
    return dict(are=are, aim=aim, ldt=ldt, bre=bre, bim=bim, cre=cre, cim=cim, tau=tau)


def run_s5(xs, c, adaw, adab, ng, A_re, A_im, log_dt, B_re, B_im, C_re, C_im, d_skip, w_glu):
    nc1 = _prog("s5a", lambda: build_tok("s5a", NTC))
    maps = []
    for core in range(8):
        b, half = core // 2, core % 2
        m = _common(c, adaw, adab, b)
        m.update({"xT": _fm(xs[b, half * NTC:(half + 1) * NTC]), "ng": _vfm(ng)})
        maps.append(m)
    h = _gather_tok(_run(nc1, maps))
    nc2 = _prog("s5", lambda: build_s5(L=SEQ, NB=NBATCH, ndir=2))
    hds = [h, h[:, ::-1]]
    maps = []
    for core in range(8):
        m = {}
        for d in range(2):
            pm = _s5_params(A_re[d], A_im[d], log_dt[d], B_re[d], B_im[d], C_re[d], C_im[d], core)
            for kk, vv in pm.items():
                m[kk + str(d)] = vv
            m["hT%d" % d] = np.ascontiguousarray(hds[d][:, :, 128 * core:128 * core + 128].transpose(0, 2, 1))
            dk = d_skip[128 * core:128 * core + 128, None] if d == 0 else np.zeros((128, 1), np.float32)
            m["dsk%d" % d] = np.ascontiguousarray(dk.astype(np.float32))
        maps.append(m)
    r = _run(nc2, maps)
    ys = []
    for d in range(2):
        yd = np.concatenate([r[cc]["y%d" % d] for cc in range(8)], axis=1).transpose(0, 2, 1)
        ys.append(yd if d == 0 else yd[:, ::-1])
    nc3 = _prog("s5post", lambda: build_tok("s5post", NTC))
    maps = []
    for core in range(8):
        b, half = core // 2, core % 2
        sl = slice(half * NTC, (half + 1) * NTC)
        m = _common(c, adaw, adab, b)
        m.update({"xT": _fm(xs[b, sl]), "yf": _fm(ys[0][b, sl]), "yb": _fm(ys[1][b, sl]), "wglu": np.ascontiguousarray(w_glu)})
        maps.append(m)
    return _gather_tok(_run(nc3, maps))


def kernel(x, c, ada_w, ada_b, norm1_g, norm2_g, final_g,
           attn_w_qkv, attn_w_o, attn_q_gain, attn_k_gain,
           hy_w_in, hy_conv_w, hy_conv_b, hy_f_w1, hy_f_b1, hy_f_w2, hy_f_b2, hy_f_w3, hy_f_freq, hy_skip, hy_w_out,
           s5_A_re, s5_A_im, s5_log_dt, s5_B_re, s5_B_im, s5_C_re, s5_C_im, s5_D, s5_w_glu,
           ffn_w_up, ffn_conv_w, ffn_conv_b, ffn_w_down):
    A = lambda v: np.asarray(v, dtype=np.float32)
    xs = A(x)
    c = A(c)
    ada_w, ada_b = A(ada_w), A(ada_b)
    for i in range(4):
        m, j = i % 3, i // 3
        aw1, ab1 = ada_w[i][:, :3 * D], ada_b[i][:3 * D]
        aw2, ab2 = ada_w[i][:, 3 * D:], ada_b[i][3 * D:]
        if m == 0:
            xs = run_attn(xs, c, aw1, ab1, A(norm1_g)[i], A(attn_w_qkv)[j], A(attn_w_o)[j], A(attn_q_gain)[j], A(attn_k_gain)[j])
        elif m == 1:
            xs = run_hyena(xs, c, aw1, ab1, A(norm1_g)[i], A(hy_w_in)[j], A(hy_conv_w)[j], A(hy_conv_b)[j], A(hy_f_w1)[j], A(hy_f_b1)[j],
                           A(hy_f_w2)[j], A(hy_f_b2)[j], A(hy_f_w3)[j], A(hy_f_freq)[j], A(hy_skip)[j], A(hy_w_out)[j])
        else:
            xs = run_s5(xs, c, aw1, ab1, A(norm1_g)[i], A(s5_A_re)[j], A(s5_A_im)[j], A(s5_log_dt)[j], A(s5_B_re)[j], A(s5_B_im)[j],
                        A(s5_C_re)[j], A(s5_C_im)[j], A(s5_D)[j], A(s5_w_glu)[j])
        xs = run_ffn(xs, c, aw2, ab2, A(norm2_g)[i], A(ffn_w_up)[i], A(ffn_conv_w)[i], A(ffn_conv_b)[i], A(ffn_w_down)[i],
                     final_g=A(final_g) if i == 3 else None)
    return xs.astype(np.float32)
```
